# Optimizing a Trainium2 kernel written in Bass

```python
import math
import jax, jax.numpy as jnp
from jax import lax
import numpy as np

D_MODEL = 2048
BATCH = 4
SEQ = 4096
DEPTH = 1

D_MIX = D_MODEL
D_GLA = D_MIX // 2
D_SWA = D_MIX - D_GLA
GLA_HEADS = 4
GLA_DK = D_GLA // 2 // GLA_HEADS
GLA_DV = D_GLA // GLA_HEADS
GLA_KW = GLA_HEADS * GLA_DK
GATE_RANK = 16
GATE_TAU = 16.0
GLA_CHUNK = 64
SWA_HEAD_DIM = 64
SWA_Q_HEADS = D_SWA // SWA_HEAD_DIM
SWA_KV_HEADS = 2
SWA_GROUP = SWA_Q_HEADS // SWA_KV_HEADS
SWA_KVW = SWA_KV_HEADS * SWA_HEAD_DIM
WINDOW = 128
D_FF = 4 * D_MODEL
SPLITS = (GLA_KW, GLA_KW, D_GLA, D_GLA, GATE_RANK, D_SWA, SWA_KVW, SWA_KVW)
D_IN = sum(SPLITS)
ALPHA = (2 * DEPTH) ** 0.25
BETA = (8 * DEPTH) ** -0.25
LN_EPS = 1e-5
RMS_EPS = 1e-5

kernel_name = "hybrid_gla_swa_sink_deepnorm"


def split_points():
    pts, acc = [], 0
    for s in SPLITS[:-1]:
        acc += s
        pts.append(acc)
    return pts


def layer_norm(x, g, b):
    xf = x.astype(jnp.float32)
    mu = jnp.mean(xf, axis=-1, keepdims=True)
    var = jnp.mean(jnp.square(xf - mu), axis=-1, keepdims=True)
    y = (xf - mu) * lax.rsqrt(var + LN_EPS) * g.astype(jnp.float32) + b.astype(jnp.float32)
    return y.astype(x.dtype)


def gla_mixer(q, k, v, gk, g_out, norm_w):
    B, S, H, dk = q.shape
    dv = v.shape[-1]
    C = GLA_CHUNK
    nc = S // C

    def to_chunks(t):
        return t.astype(jnp.float32).reshape(B, nc, C, H, t.shape[-1]).transpose(1, 0, 3, 2, 4)

    qc = to_chunks(q) * (dk ** -0.5)
    kc, vc, gc = to_chunks(k), to_chunks(v), to_chunks(gk)
    bc = jnp.cumsum(gc, axis=3)
    causal = jnp.tril(jnp.ones((C, C), dtype=bool))

    def step(state, inp):
        qb, kb, vb, bb = inp
        o_inter = jnp.einsum('bhcd,bhde->bhce', qb * jnp.exp(bb), state)
        diff = bb[:, :, :, None, :] - bb[:, :, None, :, :]
        decay = jnp.exp(jnp.where(causal[:, :, None], diff, -jnp.inf))
        attn = jnp.einsum('bhijd,bhjd->bhij', qb[:, :, :, None, :] * decay, kb)
        o_intra = jnp.einsum('bhij,bhje->bhie', attn, vb)
        b_last = bb[:, :, -1:, :]
        k_dec = kb * jnp.exp(b_last - bb)
        new_state = state * jnp.exp(b_last[:, :, 0, :, None]) + jnp.einsum('bhcd,bhce->bhde', k_dec, vb)
        return new_state, o_inter + o_intra

    state0 = jnp.zeros((B, H, dk, dv), jnp.float32)
    _, oc = lax.scan(step, state0, (qc, kc, vc, bc))
    o = oc.transpose(1, 0, 3, 2, 4).reshape(B, S, H, dv)
    o = o * lax.rsqrt(jnp.mean(jnp.square(o), axis=-1, keepdims=True) + RMS_EPS)
    o = o * norm_w.astype(jnp.float32) * jax.nn.silu(g_out.astype(jnp.float32))
    return o.reshape(B, S, H * dv).astype(q.dtype)


def swa_mixer(q, k, v, sinks):
    B, S, _, dh = q.shape
    W = WINDOW
    nb = S // W
    qb = q.astype(jnp.float32).reshape(B, nb, W, SWA_KV_HEADS, SWA_GROUP, dh)

    def band(t):
        tb = t.astype(jnp.float32).reshape(B, nb, W, SWA_KV_HEADS, dh)
        prev = jnp.concatenate([jnp.zeros_like(tb[:, :1]), tb[:, :-1]], axis=1)
        return jnp.concatenate([prev, tb], axis=2)

    kb, vb = band(k), band(v)
    s = jnp.einsum('bnqhgd,bnkhd->bnhgqk', qb, kb) * (dh ** -0.5)
    qi = jnp.arange(W)[:, None]
    kj = jnp.arange(2 * W)[None, :]
    in_window = (kj > qi) & (kj <= qi + W)
    blk = jnp.arange(nb)[:, None, None]
    valid = in_window[None] & ((blk > 0) | (kj[None] >= W))
    s = jnp.where(valid[None, :, None, None], s, -jnp.inf)
    sink = sinks.astype(jnp.float32).reshape(SWA_KV_HEADS, SWA_GROUP)[None, None, :, :, None, None]
    m = jnp.maximum(jnp.max(s, axis=-1, keepdims=True), sink)
    p = jnp.exp(s - m)
    denom = jnp.sum(p, axis=-1, keepdims=True) + jnp.exp(sink - m)
    o = jnp.einsum('bnhgqk,bnkhd->bnqhgd', p / denom, vb)
    return o.reshape(B, S, SWA_Q_HEADS * dh).astype(q.dtype)


def setup_inputs(seed: int = 0) -> dict:
    key = jax.random.key(seed)
    ks = jax.random.split(key, 12)
    f32 = jnp.float32
    x = jax.random.normal(ks[0], (BATCH, SEQ, D_MODEL), f32)
    col_scale = np.concatenate([np.full((s,), BETA if i in (2, 6, 7) else 1.0, np.float32)
                                for i, s in enumerate(SPLITS)])
    w_in = jax.random.normal(ks[1], (DEPTH, D_MODEL, D_IN), f32) * (D_MODEL ** -0.5) * jnp.asarray(col_scale)
    w_gk2 = jax.random.normal(ks[2], (DEPTH, GATE_RANK, GLA_KW), f32) * (GATE_RANK ** -0.5)
    b_gk = 0.1 * jax.random.normal(ks[3], (DEPTH, GLA_KW), f32)
    gla_norm_w = 1.0 + 0.02 * jax.random.normal(ks[4], (DEPTH, GLA_DV), f32)
    swa_sinks = 0.5 * jax.random.normal(ks[5], (DEPTH, SWA_Q_HEADS), f32)
    w_out = jax.random.normal(ks[6], (DEPTH, D_MIX, D_MODEL), f32) * (D_MIX ** -0.5) * BETA
    ln1_g = 1.0 + 0.02 * jax.random.normal(ks[7], (DEPTH, D_MODEL), f32)
    ln1_b = 0.02 * jax.random.normal(ks[8], (DEPTH, D_MODEL), f32)
    w_up = jax.random.normal(ks[9], (DEPTH, D_MODEL, D_FF), f32) * (D_MODEL ** -0.5)
    w_down = jax.random.normal(ks[10], (DEPTH, D_FF, D_MODEL), f32) * (D_FF ** -0.5) * BETA
    k2 = jax.random.split(ks[11], 2)
    ln2_g = 1.0 + 0.02 * jax.random.normal(k2[0], (DEPTH, D_MODEL), f32)
    ln2_b = 0.02 * jax.random.normal(k2[1], (DEPTH, D_MODEL), f32)
    return {"x": x, "w_in": w_in, "w_gk2": w_gk2, "b_gk": b_gk, "gla_norm_w": gla_norm_w,
            "swa_sinks": swa_sinks, "w_out": w_out, "ln1_g": ln1_g, "ln1_b": ln1_b,
            "w_up": w_up, "w_down": w_down, "ln2_g": ln2_g, "ln2_b": ln2_b}


def reference(x, w_in, w_gk2, b_gk, gla_norm_w, swa_sinks, w_out, ln1_g, ln1_b,
              w_up, w_down, ln2_g, ln2_b):
    B, S, _ = x.shape
    pts = split_points()
    for l in range(DEPTH):
        proj = jnp.einsum('bsd,de->bse', x, w_in[l])
        q_g, k_g, v_g, g_g, gk_lo, q_s, k_s, v_s = jnp.split(proj, pts, axis=-1)
        gk = jax.nn.log_sigmoid((jnp.einsum('bsr,rk->bsk', gk_lo, w_gk2[l]) + b_gk[l]).astype(jnp.float32)) / GATE_TAU
        gla_out = gla_mixer(q_g.reshape(B, S, GLA_HEADS, GLA_DK),
                            k_g.reshape(B, S, GLA_HEADS, GLA_DK),
                            v_g.reshape(B, S, GLA_HEADS, GLA_DV),
                            gk.reshape(B, S, GLA_HEADS, GLA_DK),
                            g_g.reshape(B, S, GLA_HEADS, GLA_DV),
                            gla_norm_w[l])
        swa_out = swa_mixer(q_s.reshape(B, S, SWA_Q_HEADS, SWA_HEAD_DIM),
                            k_s.reshape(B, S, SWA_KV_HEADS, SWA_HEAD_DIM),
                            v_s.reshape(B, S, SWA_KV_HEADS, SWA_HEAD_DIM),
                            swa_sinks[l])
        mix = jnp.einsum('bse,ed->bsd', jnp.concatenate([gla_out, swa_out], axis=-1), w_out[l])
        x = layer_norm(ALPHA * x + mix, ln1_g[l], ln1_b[l])
        hdn = jnp.square(jax.nn.relu(jnp.einsum('bsd,df->bsf', x, w_up[l])))
        ff = jnp.einsum('bsf,fd->bsd', hdn, w_down[l])
        x = layer_norm(ALPHA * x + ff, ln2_g[l], ln2_b[l])
    return x
```

```python
import numpy as np
import concourse.bass as bass
import concourse.mybir as mybir
from concourse.bass_utils import run_bass_kernel_spmd

F32 = mybir.dt.float32
BF16 = mybir.dt.bfloat16
AF = mybir.ActivationFunctionType
ALU = mybir.AluOpType

D = 2048
NTOK = 2048
NT = 1024
DFF = 8192
ALPHA = 2.0 ** 0.25
LN_EPS = 1e-5
RMS_EPS = 1e-5
NEG = -30000.0
WIN_COLS = 400 + 4 * 768 + 1024

ENGS = ["pe", "act", "dve", "pool", "sp"]


class Tile:
    __slots__ = ("name", "w", "r")

    def __init__(self, name):
        self.name = name
        self.w = None
        self.r = {}


class Op:
    __slots__ = ("eng", "idx", "fn", "waits", "signal", "stream", "sval", "label")


class Prog:
    def __init__(self):
        self.ops = {e: [] for e in ENGS}
        self.waited = {e: {} for e in ENGS}
        self.streams = {}
        self.tiles = {}
        self.label = ""

    def t(self, *key):
        tl = self.tiles.get(key)
        if tl is None:
            tl = Tile(key)
            self.tiles[key] = tl
        return tl

    def add(self, eng, fn, reads=(), writes=(), stream=None):
        op = Op()
        op.eng = eng
        op.fn = fn
        op.idx = len(self.ops[eng])
        op.signal = False
        op.stream = stream
        op.sval = None
        op.label = self.label
        if stream is not None:
            self.streams[stream] = self.streams.get(stream, 0) + 1
            op.sval = 16 * self.streams[stream]
        deps = []
        for tl in reads:
            if tl.w is not None:
                deps.append(tl.w)
        for tl in writes:
            if tl.w is not None:
                deps.append(tl.w)
            deps.extend(tl.r.values())
        waits = []
        wd = self.waited[eng]
        for d in deps:
            if d.stream is not None:
                key = ("s", d.stream)
                val = d.sval
            else:
                if d.eng == eng and eng == "pe":
                    continue
                key = ("e", d.eng)
                val = d.idx
            if val <= wd.get(key, -1):
                continue
            wd[key] = val
            waits.append(d)
            if d.stream is None:
                d.signal = True
        op.waits = waits
        rkey = ("s", stream) if stream is not None else ("e", eng)
        for tl in reads:
            tl.r[rkey] = op
        for tl in writes:
            tl.w = op
            tl.r = {}
        self.ops[eng].append(op)
        return op

    def barrier(self):
        bt = self.t("__barrier__", len(self.tiles))
        lasts = []
        for e in ENGS:
            if self.ops[e]:
                lasts.append(self.ops[e][-1])
        last_stream = {}
        for e in ("pool", "sp"):
            for op in self.ops[e]:
                if op.stream is not None:
                    last_stream[op.stream] = op
        for e in ENGS:
            op = Op()
            op.eng = e
            op.fn = None
            op.idx = len(self.ops[e])
            op.signal = False
            op.stream = None
            op.sval = None
            waits = []
            wd = self.waited[e]
            for d in lasts:
                if d.eng == e or d.stream is not None:
                    continue
                key = ("e", d.eng)
                if d.idx <= wd.get(key, -1):
                    continue
                wd[key] = d.idx
                d.signal = True
                waits.append(d)
            for sname, d in last_stream.items():
                key = ("s", sname)
                if d.sval <= wd.get(key, -1):
                    continue
                wd[key] = d.sval
                waits.append(d)
            op.waits = waits
            self.ops[e].append(op)
        del bt

    def emit(self, nc, block, sems_eng, sems_stream):
        for e in ENGS:
            cnt = 0
            for op in self.ops[e]:
                if op.stream is None and op.signal:
                    cnt += 1
                    op.sval = cnt

        def run(h, e):
            for op in self.ops[e]:
                for d in op.waits:
                    sem = sems_stream[d.stream] if d.stream is not None else sems_eng[d.eng]
                    h.wait_ge(sem, d.sval)
                if op.fn is None:
                    if op.signal:
                        h.nop().then_inc(sems_eng[e], 1)
                    continue
                ins = op.fn(h)
                if op.stream is not None:
                    ins.then_inc(sems_stream[op.stream], 16)
                elif op.signal:
                    ins.then_inc(sems_eng[e], 1)

        block.tensor(lambda h: run(h, "pe"))
        block.scalar(lambda h: run(h, "act"))
        block.vector(lambda h: run(h, "dve"))
        block.gpsimd(lambda h: run(h, "pool"))
        block.sync(lambda h: run(h, "sp"))


def build_program():
    nc = bass.Bass("TRN2", target_bir_lowering=False)

    def din(name, shape):
        return nc.dram_tensor(name, shape, F32, kind="ExternalInput").ap()

    x_own = din("x_own", [NTOK, D])
    x_prev = din("x_prev", [NTOK, D])
    w_in = din("w_in", [D, WIN_COLS])
    w_gk2 = din("w_gk2", [16, 512])
    b_gk = din("b_gk", [128, 4])
    normw = din("normw", [1, 256])
    sinks = din("sinks", [128, 8])
    w_out = din("w_out", [D, D])
    lnp = din("lnp", [128, 64])
    ln2row = din("ln2row", [2, D])
    w_up = din("w_up", [D, DFF])
    w_down = din("w_down", [DFF, D])
    masks = din("masks", [128, 4 * 128])
    scanpat = din("scanpat", [128, 512])
    ident_in = din("ident", [128, 128])
    y = nc.dram_tensor("y", [NTOK, D], F32, kind="ExternalOutput").ap()

    w_in_v = w_in.rearrange("(kc p) n -> p kc n", p=128)
    w_out_v = w_out.rearrange("(kc p) n -> p kc n", p=128)
    w_up_v = w_up.rearrange("(kc p) n -> p kc n", p=128)
    w_down_v = w_down.rearrange("(fc p) n -> p fc n", p=128)

    import contextlib
    es = contextlib.ExitStack()
    with es:
        def sb(name, shape, dtype):
            return es.enter_context(nc.sbuf_tensor(name, shape, dtype))

        Wr = [sb(f"wring{i}", [128, 8192], BF16) for i in range(2)]
        RA = sb("regA", [128, 32768], BF16)
        RB = sb("regB", [128, 16384], BF16)
        RC = sb("regC", [128, 24576], BF16)
        tmpr_t = sb("tmpr_t", [128, 2, 512], F32)
        ident_b = sb("ident_b", [128, 128], BF16)
        ident_f = sb("ident_f", [128, 128], F32)
        ones_b = sb("ones_b", [128, 64], BF16)
        masks_b = sb("masks_b", [128, 4, 128], BF16)
        pat_f = sb("pat_f", [128, 512], F32)
        wgk2_b = sb("wgk2_b", [16, 512], BF16)
        negb = sb("negb", [128, 4], F32)
        normw_bc = sb("normw_bc", [128, 256], F32)
        sinkexp = sb("sinkexp", [128, 8], F32)
        lnp_s = sb("lnp_s", [128, 64], F32)
        lnpa = sb("lnpa", [128, 32], F32)
        ksT = sb("ksT", [128, 2, 1152], BF16)
        vs = sb("vs", [128, 9, 128], BF16)
        gkloT = sb("gkloT", [16, NT], BF16)
        S_f = sb("S_f", [128, 4, 256], F32)
        S_b = sb("S_b", [128, 4, 256], BF16)
        negcC = sb("negcC", [128, 4, 8], F32)
        eC = sb("eC", [128, 4, 8], F32)
        junk = sb("junk", [128, 256], BF16)
        small = sb("small", [128, 64], F32)
        bnst = sb("bnst", [128, 6, 4, 6], F32)
        ps = [es.enter_context(nc.psum_tensor(f"ps{i}", [128, 512], F32)) for i in range(8)]
        sems_eng = {e: es.enter_context(nc.semaphore(f"sem_{e}")) for e in ENGS}
        stream_names = ["w0", "w1", "w2", "w3", "xtok0", "xtok1", "xtok2", "xtok3", "xtok4", "xtok5", "xtok6", "xtok7", "z0", "z1", "z2", "z3", "z4", "z5", "ost0", "ost1", "constp", "consts", "bc2"]
        sems_stream = {s: es.enter_context(nc.semaphore(f"sem_{s}")) for s in stream_names}
        block = es.enter_context(nc.Block())

        def f32v(reg, off, n):
            return reg[:, off:off + 2 * n].bitcast(F32)

        c_f = f32v(RA, 0, 4096).rearrange("p (h t) -> p h t", h=4)
        tmpf = [f32v(RA, 8192 + i * 1024, 512) for i in range(4)]
        ktT = [RA[:, 12288 + i * 1024: 12288 + (i + 1) * 1024] for i in range(2)]
        qtT = [RA[:, 14336 + i * 1024: 14336 + (i + 1) * 1024] for i in range(2)]
        kdT = [RA[:, 16384 + i * 512: 16384 + (i + 1) * 512] for i in range(2)]
        kd_tok = [RA[:, 17408 + i * 1024: 17408 + (i + 1) * 1024].rearrange("p (t d) -> p t d", d=128) for i in range(2)]
        v_tok = [RA[:, 19456 + i * 2048: 19456 + (i + 1) * 2048].rearrange("p (t e) -> p t e", e=256) for i in range(2)]
        gw = [f32v(RA, 23552 + i * 4096, 2048).rearrange("p (t e) -> p t e", e=256) for i in range(2)]
        At = [RA[:, 31744 + i * 128: 31744 + (i + 1) * 128] for i in range(8)]
        At4 = [RA[:, 31744 + i * 512: 31744 + (i + 1) * 512] for i in range(2)]
        qsT = [RA[:, 8192 + i * 4096: 8192 + (i + 1) * 4096].rearrange("p (c t) -> p c t", c=4) for i in range(2)]
        PT = [RA[:, 16384 + i * 512: 16384 + (i + 1) * 512] for i in range(8)]
        swtmp = [f32v(RA, 20480 + i * 1024, 512) for i in range(2)]
        xT = RC[:, 0:16384].rearrange("p (k t) -> p k t", k=16)
        xtok = [RB[:, i * 2048:(i + 1) * 2048] for i in range(8)]
        gla_h = [RC[:, 20480 + i * 2048: 20480 + (i + 1) * 2048].rearrange("p (t e) -> p t e", e=256) for i in range(2)]
        zt = [f32v(RC, i * 4096, 2048) for i in range(6)]
        hdn = [RC[:, i * 8192:(i + 1) * 8192].rearrange("p (f t) -> p f t", f=8) for i in range(2)]
        tmpr = [tmpr_t[:, i, :] for i in range(2)]
        g2bc = f32v(RC, 0, 2048)
        b2bc = f32v(RC, 4096, 2048)
        ztmp2 = [f32v(RC, 8192, 2048), f32v(RC, 20480, 2048)]
        ostage = [f32v(RC, 12288 + i * 4096, 2048) for i in range(2)]
        mixT = RB[:, :].rearrange("p (k t) -> p k t", k=16)
        acc = f32v(RA, 0, 16384).rearrange("p (k t) -> p k t", k=16)

        def record(P, wplan, xplan):
            T = P.t
            wlog = []
            xlog = []
            xissued = [0]
            bank_ctr = [0]

            held_banks = set()

            def next_bank():
                while True:
                    i = bank_ctr[0] % 8
                    bank_ctr[0] += 1
                    if i not in held_banks:
                        return ps[i], T("ps", i)

            small_ctr = [0]

            def next_small(n=1):
                i = small_ctr[0] % (64 // n)
                small_ctr[0] += 1
                return small[:, i * n:(i + 1) * n], T("small", n, i)

            def mm(out, lhsT, rhs, start, stop, reads, writes):
                P.add("pe", lambda h: h.matmul(out, lhsT=lhsT, rhs=rhs, start=start, stop=stop), reads, writes)

            def tr(out, in_, ident, reads, writes):
                P.add("pe", lambda h: h.transpose(out, in_, ident), reads, writes)

            def act(out, in_, func, reads, writes, bias=None, scale=None, accum_out=None):
                kw = {}
                if bias is not None:
                    kw["bias"] = bias
                if scale is not None:
                    kw["scale"] = scale
                if accum_out is not None:
                    kw["accum_out"] = accum_out
                P.add("act", lambda h: h.activation(out, in_, func, **kw), reads, writes)

            def tt(eng, out, in0, in1, op, reads, writes):
                P.add(eng, lambda h: h.tensor_tensor(out, in0, in1, op), reads, writes)

            def ts(eng, out, in0, s1, s2, op0, op1, reads, writes):
                P.add(eng, lambda h: h.tensor_scalar(out, in0, s1, s2, op0, op1), reads, writes)

            def stt(out, in0, scalar, in1, op0, op1, reads, writes):
                P.add("dve", lambda h: h.scalar_tensor_tensor(out, in0, scalar, in1, op0, op1), reads, writes)

            def cp(eng, out, in_, reads, writes):
                if eng == "act":
                    P.add("act", lambda h: h.copy(out, in_), reads, writes)
                else:
                    P.add(eng, lambda h: h.tensor_copy(out, in_), reads, writes)

            def dma(q, stream, out, in_, reads, writes):
                P.add(q, lambda h: h.dma_start(out=out, in_=in_), reads, writes, stream=stream)

            wslot_ctr = [0]

            wassign = []
            wptr = [0]
            wowner = [-1, -1, -1, -1]
            wissued = [0]

            def assign_w(j, req):
                while len(wassign) <= j:
                    jj = len(wassign)
                    _, nk, ncols = (wlog[jj] if wplan is None else wplan[jj])
                    if nk * ncols <= 4096:
                        hs = [wptr[0] % 4]
                        wptr[0] += 1
                    else:
                        if wptr[0] % 2:
                            wptr[0] += 1
                        hs = [wptr[0] % 4, wptr[0] % 4 + 1]
                        wptr[0] += 2
                    wassign.append(hs)
                return wassign[j]

            def w_view(hs, nk, ncols):
                base = (hs[0] % 2) * 4096
                return Wr[hs[0] // 2][:, base:base + nk * ncols].rearrange("p (k n) -> p k n", n=ncols)

            def issue_w(j, req):
                src_ap, nk, ncols = req
                hs = assign_w(j, req)
                dma("pool", f"w{hs[0]}", w_view(hs, nk, ncols), src_ap, [], [T("wh", h_) for h_ in hs])
                for h_ in hs:
                    wowner[h_] = j

            def load_w(src_ap, nk, ncols):
                i = len(wlog)
                wlog.append((src_ap, nk, ncols))
                if wplan is None:
                    issue_w(i, wlog[i])
                    wissued[0] = i + 1
                else:
                    while wissued[0] < len(wplan) and wissued[0] <= i + 3:
                        j = wissued[0]
                        hs = assign_w(j, wplan[j])
                        if j > i and any(wowner[h_] >= i for h_ in hs):
                            break
                        issue_w(j, wplan[j])
                        wissued[0] += 1
                hs = assign_w(i, wlog[i])
                return w_view(hs, nk, ncols), [T("wh", h_) for h_ in hs]

            tc = T("const")
            tcp = T("constp")
            dma("pool", "constp", ident_b[:], ident_in, [], [tcp])
            dma("sp", "consts", ident_f[:], ident_in, [], [tc])
            dma("pool", "constp", masks_b[:].rearrange("p a b -> p (a b)"), masks, [], [tcp])
            dma("sp", "consts", pat_f[:], scanpat, [], [tc])
            dma("pool", "constp", wgk2_b[:], w_gk2, [], [tcp])
            dma("sp", "consts", negb[:], b_gk, [], [tc])
            dma("sp", "consts", normw_bc[:], normw.partition_broadcast(128), [], [tc])
            dma("sp", "consts", sinkexp[:], sinks, [], [tc])
            dma("sp", "consts", lnp_s[:], lnp, [], [tc])
            tc2 = T("const2")
            ts("dve", negb[:], negb[:], -1.0, None, ALU.mult, ALU.bypass, [tc, tcp], [tc2])
            P.add("dve", lambda h: h.memset(ones_b[:], 1.0), [tc2], [tc2])
            P.add("dve", lambda h: h.memset(S_f[:].rearrange("p a b -> p (a b)"), 0.0), [tc2], [T("S", hh) for hh in range(4)])
            P.add("dve", lambda h: h.memset(S_b[:].rearrange("p a b -> p (a b)"), 0.0), [tc2], [T("Sb", hh) for hh in range(4)])
            P.add("dve", lambda h: h.memset(ksT[:].rearrange("p a b -> p (a b)"), 0.0), [tc2], [T("ksT", g, "carry") for g in range(2)])
            P.add("dve", lambda h: h.memset(vs[:].rearrange("p a b -> p (a b)"), 0.0), [tc2], [T("vs", 0)])
            ts("dve", lnpa[:], lnp_s[:, 0:32], ALPHA, None, ALU.mult, ALU.bypass, [tc], [tc2])
            act(sinkexp[:], sinkexp[:], AF.Exp, [tc], [tc2])
            CONST = [tc, tcp, tc2]

            def issue_x_loads(xsrc, pi):
                for t in range(8):
                    rows = xsrc[pi * NT + t * 128: pi * NT + (t + 1) * 128, :]
                    alias = [T("mixT", 2 * t), T("mixT", 2 * t + 1)] + [T("x1T", 2 * t + a, hf) for a in range(2) for hf in range(2)]
                    dma("pool", f"xtok{t}", xtok[t].rearrange("p (a b) -> p a b", a=2), rows.rearrange("p (a b) -> p a b", a=2), [], alias)

            ZSLOT = [0, 1, 2, 3, 4, 5, 2, 3]

            def do_pass(xsrc, pi, full, first_own, nxt):
                P.barrier()
                tok0 = pi * NT
                P.label = f"{int(full)}{pi}:xT"
                def x_tile(t):
                    sl = t
                    for g8 in range(2):
                        bk, bt = next_bank()
                        bkb = bk[:, :].bitcast(BF16)
                        for jx in range(8):
                            dc = g8 * 8 + jx
                            tr(bkb[:, jx * 128:(jx + 1) * 128], xtok[sl][:, dc * 128:(dc + 1) * 128], ident_b[:],
                               [T("mixT", 2 * sl), T("mixT", 2 * sl + 1)] + CONST, [bt])
                        cp("act" if g8 == 0 else "dve", xT[:, g8 * 8:(g8 + 1) * 8, t * 128:(t + 1) * 128],
                           bkb.rearrange("p (k t) -> p k t", k=8), [bt], [T("xT", g8, t)])

                for t in range(4):
                    x_tile(t)

                def xT_reads(half):
                    return [T("xT", g8, t) for g8 in range(2) for t in range(half * 4, half * 4 + 4)]

                def proj_fm(wv, wt, c0, m, half, bk, bt):
                    for kc in range(16):
                        mm(bk[0:m, :], wv[:, kc, c0:c0 + m], xT[:, kc, half * 512:(half + 1) * 512],
                           kc == 0, kc == 15, wt + xT_reads(half), [bt])

                def proj_tm(wv, wt, c0, n, t, out_ap, bt):
                    for kc in range(16):
                        mm(out_ap, xT[:, kc, t * 128:(t + 1) * 128], wv[:, kc, c0:c0 + n],
                           kc == 0, kc == 15, wt + [T("xT", 0, t), T("xT", 1, t)], [bt])

                P.label = f"{int(full)}{pi}:misc"
                wv, wt = load_w(w_in_v[:, :, 0:400], 16, 400)

                def misc_half(half):
                    if full:
                        for g in range(2):
                            bk, bt = next_bank()
                            proj_fm(wv, wt, g * 128, 128, half, bk, bt)
                            cp("act", ksT[:, g, 128 + half * 512: 128 + (half + 1) * 512], bk[:, :], [bt], [T("ksT", g, half)])
                        bk, bt = next_bank()
                        for j in range(4):
                            t = half * 4 + j
                            proj_tm(wv, wt, 256, 128, t, bk[:, j * 128:(j + 1) * 128], bt)
                        cp("dve", vs[:, 1 + half * 4: 5 + half * 4, :], bk[:, :].rearrange("p (t e) -> p t e", t=4), [bt], [T("vs", 1 + half)])
                    elif pi == 1 and half == 1:
                        for g in range(2):
                            bk, bt = next_bank()
                            for kc in range(16):
                                mm(bk[:, 0:128], wv[:, kc, g * 128:(g + 1) * 128], xT[:, kc, 896:1024], kc == 0, kc == 15,
                                   wt + [T("xT", 0, 7), T("xT", 1, 7)], [bt])
                            cp("act", ksT[:, g, 1024:1152], bk[:, 0:128], [bt], [T("ksT", g, 1)])
                        bk, bt = next_bank()
                        proj_tm(wv, wt, 256, 128, 7, bk[:, 0:128], bt)
                        cp("dve", vs[:, 8, :], bk[:, 0:128], [bt], [T("vs", 2)])
                    bk, bt = next_bank()
                    proj_fm(wv, wt, 384, 16, half, bk, bt)
                    cp("act", gkloT[:, half * 512:(half + 1) * 512], bk[0:16, :], [bt], [T("gklo", half)])
                    for h in range(4):
                        bk, bt = next_bank()
                        mm(bk[:, :], wgk2_b[:, h * 128:(h + 1) * 128], gkloT[:, half * 512:(half + 1) * 512], True, True,
                           [T("gklo", half)] + CONST, [bt])
                        tf = (h * 2 + half) % 4
                        act(tmpf[tf], bk[:, :], AF.Exp, [bt] + CONST, [T("tmpf", tf)], bias=negb[:, h:h + 1], scale=-1.0)
                        act(tmpf[tf], tmpf[tf], AF.Ln, [T("tmpf", tf)], [T("tmpf", tf)], bias=1.0)
                        P.add("dve", lambda hh, o=c_f[:, h, half * 512:(half + 1) * 512], d0=pat_f[:, :], d1=tmpf[tf]:
                              hh.tensor_tensor_scan(o, d0, d1, 0.0, ALU.mult, ALU.add), [T("tmpf", tf)] + CONST, [T("c", h, half)])

                misc_half(0)
                P.label = f"{int(full)}{pi}:xT"
                for t in range(4, 8):
                    x_tile(t)
                P.label = f"{int(full)}{pi}:misc"
                misc_half(1)
                call = [T("c", h, half) for h in range(4) for half in range(2)]
                tcc = T("cC")
                ts("dve", negcC[:].rearrange("p h c -> p (h c)"),
                   c_f.rearrange("p h (c t) -> p (h c) t", t=128)[:, :, 127], -1.0 / 16.0, None, ALU.mult, ALU.bypass, call, [tcc])
                act(eC[:].rearrange("p h c -> p (h c)"), negcC[:].rearrange("p h c -> p (h c)"), AF.Exp, [tcc], [T("eC")])

                if not full and nxt is not None:
                    issue_x_loads(*nxt)
                P.label = f"{int(full)}{pi}:gla"
                def gla_proj(h):
                    hb = h % 2
                    base = 400 + h * 768

                    def kd_transposes(half):
                        bk2, bt2 = next_bank()
                        bkb = bk2[:, :].bitcast(BF16)
                        for j in range(4):
                            tr(bkb[:, j * 128:(j + 1) * 128], kdT[half][:, j * 128:(j + 1) * 128], ident_b[:],
                               [T("kdT", half)] + CONST, [bt2])
                        cp("act", kd_tok[hb][:, half * 4:(half + 1) * 4, :], bkb[:, 0:512].rearrange("p (t d) -> p t d", t=4),
                           [bt2], [T("kd_tok", hb, half)])

                    wv, wt = load_w(w_in_v[:, :, base:base + 384], 16, 384)

                    def k_step(half):
                        bk, bt = next_bank()
                        proj_fm(wv, wt, 0, 128, half, bk, bt)
                        if full:
                            tf = half
                            act(tmpf[tf], c_f[:, h, half * 512:(half + 1) * 512], AF.Exp, [T("c", h, half)], [T("tmpf", tf)], scale=1.0 / 16.0)
                            tt("dve", ktT[hb][:, half * 512:(half + 1) * 512], bk[:, :], tmpf[tf], ALU.mult,
                               [bt, T("tmpf", tf)], [T("ktT", hb, half)])
                        tf = 2 + half
                        for j in range(4):
                            cj = half * 4 + j
                            act(tmpf[tf][:, j * 128:(j + 1) * 128], c_f[:, h, cj * 128:(cj + 1) * 128], AF.Exp,
                                [T("c", h, half), tcc], [T("tmpf", tf)], bias=negcC[:, h, cj:cj + 1], scale=1.0 / 16.0)
                        tt("dve", kdT[half], bk[:, :], tmpf[tf], ALU.mult, [bt, T("tmpf", tf)], [T("kdT", half)])

                    def v_step(tq):
                        bk, bt = next_bank()
                        for j in range(2):
                            t = tq * 2 + j
                            proj_tm(wv, wt, 128, 256, t, bk[:, j * 256:(j + 1) * 256], bt)
                        cp("dve" if tq % 2 else "act", v_tok[hb][:, tq * 2:tq * 2 + 2, :], bk[:, :].rearrange("p (t e) -> p t e", t=2),
                           [bt], [T("v_tok", hb, tq)])

                    if h == 0:
                        for tq in range(4):
                            v_step(tq)
                            yield
                        k_step(0)
                        yield
                        k_step(1)
                        kd_transposes(0)
                        yield
                        kd_transposes(1)
                        yield
                    else:
                        k_step(0)
                        yield
                        k_step(1)
                        kd_transposes(0)
                        yield
                        for tq in range(4):
                            v_step(tq)
                            if tq == 0:
                                kd_transposes(1)
                            yield
                    if full:
                        wv2, wt2 = load_w(w_in_v[:, :, base + 384:base + 768], 16, 384)
                        for half in range(2):
                            bk, bt = next_bank()
                            proj_fm(wv2, wt2, 0, 128, half, bk, bt)
                            tf = half
                            act(tmpf[tf], c_f[:, h, half * 512:(half + 1) * 512], AF.Exp, [T("c", h, half)], [T("tmpf", tf)], scale=-1.0 / 16.0)
                            stt(qtT[hb][:, half * 512:(half + 1) * 512], bk[:, :], 128.0 ** -0.5, tmpf[tf], ALU.mult, ALU.mult,
                                [bt, T("tmpf", tf)], [T("qtT", hb, half)])
                            yield
                        for tq in range(4):
                            bk, bt = next_bank()
                            for j in range(2):
                                t = tq * 2 + j
                                proj_tm(wv2, wt2, 128, 256, t, bk[:, j * 256:(j + 1) * 256], bt)
                            tf = 2 + tq % 2
                            act(tmpf[tf], bk[:, :], AF.Silu, [bt], [T("tmpf", tf)])
                            tt("dve", gw[hb][:, tq * 2:tq * 2 + 2, :], tmpf[tf].rearrange("p (t e) -> p t e", t=2),
                               normw_bc[:].unsqueeze(1).to_broadcast([128, 2, 256]), ALU.mult,
                               [T("tmpf", tf)] + CONST, [T("gw", hb, tq)])
                            yield

                def gla_chunks(h):
                    hb = h % 2
                    if full:
                        for half in range(2):
                            bk, bt = next_bank()
                            for j in range(4):
                                t = half * 4 + j
                                mm(bk[:, j * 128:(j + 1) * 128], ktT[hb][:, t * 128:(t + 1) * 128], qtT[hb][:, t * 128:(t + 1) * 128],
                                   True, True, [T("ktT", hb, half), T("qtT", hb, half)], [bt])
                            tt("dve", At4[half].rearrange("p (c t) -> p c t", c=4), bk[:, :].rearrange("p (c t) -> p c t", c=4),
                               masks_b[:, 0, :].unsqueeze(1).to_broadcast([128, 4, 128]), ALU.mult, [bt] + CONST, [T("At", half)])
                        yield
                    for t in range(8):
                        half = t // 4
                        tq = t // 2
                        if full:
                            a = t
                            bo, bot = next_bank()
                            mm(bo[:, 0:256], At[a], v_tok[hb][:, t, :], True, False, [T("At", half), T("v_tok", hb, tq)], [bot])
                            mm(bo[:, 0:256], qtT[hb][:, t * 128:(t + 1) * 128], S_b[:, h, :], False, True,
                               [T("qtT", hb, half), T("Sb", h)], [bot])
                            ss, sst = next_small()
                            act(junk[:], bo[:, 0:256], AF.Square, [bot], [T("junk"), sst], accum_out=ss)
                            ln_, lnt = next_small()
                            act(ln_, ss, AF.Ln, [sst], [lnt], bias=RMS_EPS, scale=1.0 / 256.0)
                            rs, rst = next_small()
                            act(rs, ln_, AF.Exp, [lnt], [rst], scale=-0.5)
                            stt(gla_h[hb][:, t, :], bo[:, 0:256], rs, gw[hb][:, t, :], ALU.mult, ALU.mult,
                                [bot, rst, T("gw", hb, tq)], [T("gla_h", hb, t)])
                        bu, but = next_bank()
                        mm(bu[:, 0:256], kd_tok[hb][:, t, :], v_tok[hb][:, t, :], True, True,
                           [T("kd_tok", hb, half), T("v_tok", hb, tq)], [but])
                        stt(S_f[:, h, :], S_f[:, h, :], eC[:, h, t:t + 1], bu[:, 0:256], ALU.mult, ALU.add,
                            [T("S", h), T("eC"), but], [T("S", h)])
                        cp("act", S_b[:, h, :], S_f[:, h, :], [T("S", h)], [T("Sb", h)])
                        yield
                    if full:
                        for ec in range(2):
                            bk, bt = next_bank()
                            bkb = bk[:, :].bitcast(BF16)
                            for t in range(8):
                                tr(bkb[:, t * 128:(t + 1) * 128], gla_h[hb][:, t, ec * 128:(ec + 1) * 128], ident_b[:],
                                   [T("gla_h", hb, t)] + CONST, [bt])
                            cp("act" if ec else "dve", mixT[:, 2 * h + ec, :], bkb, [bt], [T("mixT", 2 * h + ec)])
                        yield

                def swa_proj(g, alias_tmpf):
                    gb = g % 2
                    base = 400 + 4 * 768 + g * 512
                    wv, wt = load_w(w_in_v[:, :, base:base + 512], 16, 512)
                    for c in range(4):
                        for half in range(2):
                            bk, bt = next_bank()
                            proj_fm(wv, wt, c * 128, 128, half, bk, bt)
                            extra = [T("tmpf", c)] if alias_tmpf else []
                            cp("act" if half else "dve", qsT[gb][:, c, half * 512:(half + 1) * 512], bk[:, :], [bt],
                               [T("qsT", gb, c, half)] + extra)
                            yield

                def swa_blocks(g):
                    gb = g % 2
                    pts_all = {}

                    def st1(b):
                        half = b // 4
                        pts = {}
                        for p in range(2):
                            for kb in range(2):
                                bk, bt = next_bank()
                                kcol = (b + kb) * 128
                                kread = [T("ksT", g, "carry")] if kcol < 128 else [T("ksT", g, (kcol - 128) // 512)]
                                mm(bk[:, :], ksT[p * 64:(p + 1) * 64, g, kcol:kcol + 128],
                                   qsT[gb][p * 64:(p + 1) * 64, :, b * 128:(b + 1) * 128], True, False,
                                   kread + [T("qsT", gb, c, half) for c in range(4)], [bt])
                                mi = 1 if kb == 1 else (3 if (first_own and b == 0) else 2)
                                mm(bk[:, :], ident_b[:], masks_b[:, mi, :].unsqueeze(1).to_broadcast([128, 4, 128]), False, True, CONST, [bt])
                                pi_ = (b % 2) * 4 + p * 2 + kb
                                act(PT[pi_], bk[:, :], AF.Exp, [bt], [T("PT", pi_)], scale=0.125)
                                pts[(p, kb)] = pi_
                        pts_all[b] = pts

                    def st2(b):
                        pts = pts_all[b]
                        bn_, bnt = next_bank()
                        bd_, bdt = next_bank()
                        for p in range(2):
                            for kb in range(2):
                                vblk = b + kb
                                vread = [T("vs", 0)] if vblk == 0 else [T("vs", 1 + (vblk - 1) // 4)]
                                mm(bn_[p * 64:(p + 1) * 64, :], vs[:, vblk, g * 64:(g + 1) * 64], PT[pts[(p, kb)]], kb == 0, kb == 1,
                                   vread + [T("PT", pts[(p, kb)])], [bnt])
                        for p in range(2):
                            for kb in range(2):
                                mm(bd_[p * 64:(p + 1) * 64, :], ones_b[:, :], PT[pts[(p, kb)]], kb == 0, kb == 1,
                                   [T("PT", pts[(p, kb)])] + CONST, [bdt])
                        sw = b % 2
                        tt("dve", swtmp[sw].rearrange("p (c t) -> p c t", c=4), bd_[:, :].rearrange("p (c t) -> p c t", c=4),
                           sinkexp[:, g * 4:(g + 1) * 4].unsqueeze(2).to_broadcast([128, 4, 128]), ALU.add,
                           [bdt] + CONST, [T("swtmp", sw)])
                        P.add("dve", lambda hh, o=swtmp[sw]: hh.reciprocal(o, o), [T("swtmp", sw)], [T("swtmp", sw)])
                        tt("dve", mixT[:, 8 + 4 * g: 12 + 4 * g, b * 128:(b + 1) * 128],
                           bn_[:, :].rearrange("p (c t) -> p c t", c=4), swtmp[sw].rearrange("p (c t) -> p c t", c=4), ALU.mult,
                           [bnt, T("swtmp", sw)], [T("mixT", 8 + 4 * g + c) for c in range(4)])

                    st1(0)
                    yield
                    for b in range(8):
                        if b + 1 < 8:
                            st1(b + 1)
                            yield
                        st2(b)
                        yield

                def run_interleaved(a, b):
                    alive = [g_ for g_ in (a, b) if g_ is not None]
                    while alive:
                        for g_ in list(alive):
                            try:
                                next(g_)
                            except StopIteration:
                                alive.remove(g_)

                prev_chunks = None
                for h in range(4):
                    run_interleaved(prev_chunks, gla_proj(h))
                    prev_chunks = gla_chunks(h)
                if full:
                    run_interleaved(prev_chunks, swa_proj(0, True))
                    P.label = f"{int(full)}{pi}:swa"
                    P.barrier()
                    run_interleaved(swa_blocks(0), swa_proj(1, False))
                    run_interleaved(swa_blocks(1), None)
                else:
                    run_interleaved(prev_chunks, None)
                if full or pi == 1:
                    for g in range(2):
                        cp("dve", ksT[:, g, 0:128], ksT[:, g, 1024:1152], [T("ksT", g, 1)], [T("ksT", g, "carry")])
                    cp("dve", vs[:, 0, :], vs[:, 8, :], [T("vs", 2)], [T("vs", 0)])
                if not full:
                    return

                P.label = f"{int(full)}{pi}:outproj"
                def hdn_alias(hb, fc):
                    return [T("zt", hb * 2 + fc // 4, q) for q in range(4)]

                def mlp_up_gen(s, halves):
                    hb = s % 2
                    for u in range(4):
                        c0 = s * 1024 + u * 256
                        wv, wt = load_w(w_up_v[:, :, c0:c0 + 256], 16, 256)
                        for fcl in range(2):
                            fc = u * 2 + fcl
                            for half in halves:
                                bk, bt = next_bank()
                                for kc in range(16):
                                    mm(bk[:, :], wv[:, kc, fcl * 128:(fcl + 1) * 128], mixT[:, kc, half * 512:(half + 1) * 512],
                                       kc == 0, kc == 15, wt + [T("x1T", kc, half)], [bt])
                                tf = half
                                act(tmpr[tf], bk[:, :], AF.Relu, [bt], [T("tmpr", tf)])
                                tt("dve", hdn[hb][:, fc, half * 512:(half + 1) * 512], tmpr[tf], tmpr[tf], ALU.mult,
                                   [T("tmpr", tf)], [T("hdn", hb, fc, half)] + hdn_alias(hb, fc))
                        yield

                def mlp_up(s):
                    run_interleaved(mlp_up_gen(s, (0, 1)), None)

                P.barrier()

                def load_xres(t):
                    sl = ZSLOT[t]
                    row0 = tok0 + t * 128
                    dma("sp", f"z{sl}", zt[sl], x_own[row0:row0 + 128, :], [], [T("zt", sl, q) for q in range(4)])

                def op_mm_stage(half, preloaded):
                    tiles = [half * 4 + i for i in range(4)]
                    for q in range(4):
                        banks = [next_bank() for _ in tiles]
                        bidx = [bt_.name[1] for _, bt_ in banks]
                        held_banks.update(bidx)
                        for c0 in (0, 256):
                            wv, wt = load_w(w_out_v[:, :, q * 512 + c0:q * 512 + c0 + 256], 16, 256)
                            for ti, t in enumerate(tiles):
                                if q == 0 and c0 == 0 and t not in preloaded:
                                    load_xres(t)
                                bk, bt = banks[ti]
                                for kc in range(16):
                                    mm(bk[:, c0:c0 + 256], mixT[:, kc, t * 128:(t + 1) * 128], wv[:, kc, :], kc == 0, kc == 15,
                                       wt + [T("mixT", kc)], [bt])
                                if c0 == 256:
                                    sl = ZSLOT[t]
                                    zq = zt[sl][:, q * 512:(q + 1) * 512]
                                    stt(zq, zq, ALPHA, bk[:, :], ALU.mult, ALU.add, [T("zt", sl, q), bt], [T("zt", sl, q)])
                                    P.add("dve", lambda hh, o=bnst[:, sl, q, :], i=zq: hh.bn_stats(o, i), [T("zt", sl, q)], [T("bnst", sl)])
                                    held_banks.discard(bidx[ti])
                                yield

                def ln_A(t):
                    sl = ZSLOT[t]
                    mv, mvt = next_small(2)
                    P.add("dve", lambda hh, o=mv, i=bnst[:, sl].rearrange("p a b -> p (a b)"): hh.bn_aggr(o, i), [T("bnst", sl)], [mvt])
                    ln_, lnt = next_small()
                    act(ln_, mv[:, 1:2], AF.Ln, [mvt], [lnt], bias=LN_EPS)
                    rs, rst = next_small()
                    act(rs, ln_, AF.Exp, [lnt], [rst], scale=-0.5)
                    nm, nmt = next_small()
                    stt(nm, mv[:, 0:1], -1.0, rs, ALU.mult, ALU.mult, [mvt, rst], [nmt])
                    ztl = [T("zt", sl, q) for q in range(4)]
                    act(zt[sl], zt[sl], AF.Identity, ztl + [rst, nmt], ztl, bias=nm, scale=rs)

                def ln_B(t):
                    sl = ZSLOT[t]
                    half = t // 4
                    for q in range(4):
                        bk, bt = next_bank()
                        for j in range(4):
                            dc = q * 4 + j
                            tr(bk[:, j * 128:(j + 1) * 128], zt[sl][:, dc * 128:(dc + 1) * 128], ident_f[:], [T("zt", sl, q)] + CONST, [bt])
                        for j in range(4):
                            dc = q * 4 + j
                            if q % 2 == 0:
                                act(acc[:, dc, t * 128:(t + 1) * 128], bk[:, j * 128:(j + 1) * 128], AF.Identity, [bt] + CONST,
                                    [T("acc", dc, half)], bias=lnpa[:, 16 + dc:17 + dc], scale=lnpa[:, dc:dc + 1])
                            else:
                                ts("dve", acc[:, dc, t * 128:(t + 1) * 128], bk[:, j * 128:(j + 1) * 128], lnpa[:, dc:dc + 1],
                                   lnpa[:, 16 + dc:17 + dc], ALU.mult, ALU.add, [bt] + CONST, [T("acc", dc, half)])

                def ln_stage(half):
                    for t in [half * 4 + i for i in range(4)]:
                        ln_A(t)
                        ln_B(t)
                        yield

                def ln_B_stage(half):
                    for t in [half * 4 + i for i in range(4)]:
                        ln_B(t)
                        yield

                def x1t_conv(half):
                    for dc in range(16):
                        if dc % 2:
                            act(mixT[:, dc, half * 512:(half + 1) * 512], acc[:, dc, half * 512:(half + 1) * 512], AF.Copy,
                                [T("acc", dc, half)], [T("x1T", dc, half), T("mixT", dc)], scale=1.0 / ALPHA)
                        else:
                            ts("dve", mixT[:, dc, half * 512:(half + 1) * 512], acc[:, dc, half * 512:(half + 1) * 512], 1.0 / ALPHA, None,
                               ALU.mult, ALU.bypass, [T("acc", dc, half)], [T("x1T", dc, half), T("mixT", dc)])
                        if dc % 4 == 3:
                            yield

                for t in range(4):
                    load_xres(t)
                run_interleaved(op_mm_stage(0, [0, 1, 2, 3]), None)
                load_xres(4)
                load_xres(5)
                run_interleaved(ln_stage(0), op_mm_stage(1, [4, 5]))
                def chain(*gens):
                    for g_ in gens:
                        yield from g_

                P.label = f"{int(full)}{pi}:mlp"
                ln_A(4)
                run_interleaved(x1t_conv(0), None)
                for t in (5, 6, 7):
                    ln_A(t)
                run_interleaved(ln_B_stage(1), mlp_up_gen(0, (0,)))
                run_interleaved(x1t_conv(1), None)
                run_interleaved(mlp_up_gen(0, (1,)), None)

                P.label = f"{int(full)}{pi}:mlp"
                def mlp_down(s):
                    hb = s % 2
                    for q in range(4):
                        wv, wt = load_w(w_down_v[:, s * 8:(s + 1) * 8, q * 512:(q + 1) * 512], 8, 512)
                        for dcl in range(4):
                            dc = q * 4 + dcl
                            for half in range(2):
                                bk, bt = next_bank()
                                for fc in range(8):
                                    mm(bk[:, :], wv[:, fc, dcl * 128:(dcl + 1) * 128], hdn[hb][:, fc, half * 512:(half + 1) * 512],
                                       fc == 0, fc == 7, wt + [T("hdn", hb, fc, half)], [bt])
                                tt("dve", acc[:, dc, half * 512:(half + 1) * 512], acc[:, dc, half * 512:(half + 1) * 512], bk[:, :], ALU.add,
                                   [bt, T("acc", dc, half)], [T("acc", dc, half)])

                for s in range(8):
                    if s + 1 < 8:
                        mlp_up(s + 1)
                    elif nxt is not None:
                        issue_x_loads(*nxt)
                    mlp_down(s)

                P.label = f"{int(full)}{pi}:epi"
                P.barrier()
                dma("sp", "bc2", g2bc, ln2row[0:1, :].partition_broadcast(128), [], [T("g2bc")])
                dma("sp", "bc2", b2bc, ln2row[1:2, :].partition_broadcast(128), [], [T("g2bc")])
                def epi_A(t):
                    half = t // 4
                    zb = t % 2
                    z2 = ztmp2[zb]
                    for q in range(4):
                        bk, bt = next_bank()
                        for j in range(4):
                            dc = q * 4 + j
                            tr(bk[:, j * 128:(j + 1) * 128], acc[:, dc, t * 128:(t + 1) * 128], ident_f[:], [T("acc", dc, half)] + CONST, [bt])
                        cp("act", z2[:, q * 512:(q + 1) * 512], bk[:, :], [bt], [T("z2", zb, q)])
                        P.add("dve", lambda hh, o=bnst[:, zb, q, :], i=z2[:, q * 512:(q + 1) * 512]: hh.bn_stats(o, i),
                              [T("z2", zb, q)], [T("bnst", zb)])

                def epi_B(t):
                    zb = t % 2
                    z2 = ztmp2[zb]
                    mv, mvt = next_small(2)
                    P.add("dve", lambda hh, o=mv, i=bnst[:, zb].rearrange("p a b -> p (a b)"): hh.bn_aggr(o, i), [T("bnst", zb)], [mvt])
                    ln_, lnt = next_small()
                    act(ln_, mv[:, 1:2], AF.Ln, [mvt], [lnt], bias=LN_EPS)
                    rs, rst = next_small()
                    act(rs, ln_, AF.Exp, [lnt], [rst], scale=-0.5)
                    nm, nmt = next_small()
                    stt(nm, mv[:, 0:1], -1.0, rs, ALU.mult, ALU.mult, [mvt, rst], [nmt])
                    z2l = [T("z2", zb, q) for q in range(4)]
                    act(z2, z2, AF.Identity, z2l + [rst, nmt], z2l, bias=nm, scale=rs)
                    os_ = t % 2
                    tt("dve", z2, z2, g2bc, ALU.mult, z2l + [T("g2bc")], z2l)
                    tt("pool", ostage[os_], z2, b2bc, ALU.add, z2l + [T("g2bc")], [T("ost", os_)])
                    row0 = tok0 + t * 128
                    dma("sp", f"ost{os_}", y[row0:row0 + 128, :], ostage[os_], [T("ost", os_)], [T("ystore", os_)])

                epi_A(0)
                for t in range(8):
                    if t + 1 < 8:
                        epi_A(t + 1)
                    epi_B(t)

            issue_x_loads(x_prev, 0)
            do_pass(x_prev, 0, False, False, (x_prev, 1))
            do_pass(x_prev, 1, False, False, (x_own, 0))
            do_pass(x_own, 0, True, True, (x_own, 1))
            do_pass(x_own, 1, True, False, None)
            P.add("sp", lambda h: h.nop(), [T("ystore", 0), T("ystore", 1)], [])
            return wlog, xlog

        plan, xpl = record(Prog(), None, None)
        P = Prog()
        record(P, plan, xpl)
        nc._pe_labels = [op.label for op in P.ops["pe"] if op.fn is not None]
        P.emit(nc, block, sems_eng, sems_stream)
    return nc


def host_layout(x, w_in, w_gk2, b_gk, gla_norm_w, swa_sinks, w_out, ln1_g, ln1_b, w_up, w_down, ln2_g, ln2_b):
    f = np.float32
    w = np.asarray(w_in[0], f)
    qg, kg, vg, gg = w[:, 0:512], w[:, 512:1024], w[:, 1024:2048], w[:, 2048:3072]
    gk = w[:, 3072:3088]
    qs, ks, vsw = w[:, 3088:4112], w[:, 4112:4240], w[:, 4240:4368]
    cols = [ks[:, 0:64], ks[:, 0:64], ks[:, 64:128], ks[:, 64:128], vsw, gk]
    for h in range(4):
        cols += [kg[:, h * 128:(h + 1) * 128], vg[:, h * 256:(h + 1) * 256], qg[:, h * 128:(h + 1) * 128], gg[:, h * 256:(h + 1) * 256]]
    cols.append(qs)
    w_in_r = np.ascontiguousarray(np.concatenate(cols, axis=1))
    assert w_in_r.shape == (D, WIN_COLS)
    sinks = np.zeros((128, 8), f)
    sk = np.asarray(swa_sinks[0], f)
    for g in range(2):
        for c in range(4):
            for p in range(2):
                sinks[p * 64:(p + 1) * 64, g * 4 + c] = sk[8 * g + 2 * c + p]
    lnp = np.concatenate([np.asarray(a[0], f).reshape(16, 128).T for a in (ln1_g, ln1_b, ln2_g, ln2_b)], axis=1)
    ln2row = np.stack([np.asarray(ln2_g[0], f), np.asarray(ln2_b[0], f)])
    j = np.arange(128)[:, None]
    i = np.arange(128)[None, :]
    causal = (j <= i).astype(f)
    cur = np.where(j <= i, 0.0, NEG).astype(f)
    prev = np.where(j > i, 0.0, NEG).astype(f)
    scanpat = np.ones((128, 512), f)
    scanpat[:, ::128] = 0.0
    common = {
        "w_in": w_in_r,
        "w_gk2": np.ascontiguousarray(np.asarray(w_gk2[0], f)),
        "b_gk": np.ascontiguousarray(np.asarray(b_gk[0], f).reshape(4, 128).T),
        "normw": np.ascontiguousarray(np.asarray(gla_norm_w[0], f)[None, :]),
        "sinks": sinks,
        "w_out": np.ascontiguousarray(np.asarray(w_out[0], f)),
        "lnp": np.ascontiguousarray(lnp),
        "ln2row": np.ascontiguousarray(ln2row),
        "w_up": np.ascontiguousarray(np.asarray(w_up[0], f)),
        "w_down": np.ascontiguousarray(np.asarray(w_down[0], f)),
        "scanpat": scanpat,
        "ident": np.eye(128, dtype=f),
    }
    xs = np.asarray(x, f)
    in_maps = []
    for c in range(8):
        b, half = c // 2, c % 2
        m = dict(common)
        m["x_own"] = np.ascontiguousarray(xs[b, half * NTOK:(half + 1) * NTOK])
        m["x_prev"] = np.ascontiguousarray(xs[b, 0:NTOK]) if half == 1 else np.zeros((NTOK, D), f)
        prev0 = prev if half == 1 else np.full((128, 128), NEG, f)
        m["masks"] = np.ascontiguousarray(np.concatenate([causal, cur, prev, prev0], axis=1))
        in_maps.append(m)
    return in_maps


_NC_CACHE = {}


def kernel(x, w_in, w_gk2, b_gk, gla_norm_w, swa_sinks, w_out, ln1_g, ln1_b, w_up, w_down, ln2_g, ln2_b):
    in_maps = host_layout(x, w_in, w_gk2, b_gk, gla_norm_w, swa_sinks, w_out, ln1_g, ln1_b, w_up, w_down, ln2_g, ln2_b)
    if "nc" not in _NC_CACHE:
        _NC_CACHE["nc"] = build_program()
    res = run_bass_kernel_spmd(_NC_CACHE["nc"], in_maps, core_ids=list(range(8)))
    out = np.empty((4, 4096, D), np.float32)
    for c in range(8):
        b, half = c // 2, c % 2
        out[b, half * NTOK:(half + 1) * NTOK] = res.results[c]["y"]
    return out
```

```python
import numpy as np
import concourse.bass as bass
import concourse.mybir as mybir
from concourse.bass_utils import run_bass_kernel_spmd

F32 = mybir.dt.float32
BF16 = mybir.dt.bfloat16
AF = mybir.ActivationFunctionType
ALU = mybir.AluOpType

D = 2048
NTOK = 2048
NT = 1024
DFF = 8192
ALPHA = 2.0 ** 0.25
LN_EPS = 1e-5
RMS_EPS = 1e-5
NEG = -30000.0
WIN_COLS = 400 + 4 * 768 + 1024

ENGS = ["pe", "act", "dve", "pool", "sp"]


class Tile:
    __slots__ = ("name", "w", "r")

    def __init__(self, name):
        self.name = name
        self.w = None
        self.r = {}


class Op:
    __slots__ = ("eng", "idx", "fn", "waits", "signal", "stream", "sval", "label")


class Prog:
    def __init__(self):
        self.ops = {e: [] for e in ENGS}
        self.waited = {e: {} for e in ENGS}
        self.streams = {}
        self.tiles = {}
        self.label = ""

    def t(self, *key):
        tl = self.tiles.get(key)
        if tl is None:
            tl = Tile(key)
            self.tiles[key] = tl
        return tl

    def add(self, eng, fn, reads=(), writes=(), stream=None):
        op = Op()
        op.eng = eng
        op.fn = fn
        op.idx = len(self.ops[eng])
        op.signal = False
        op.stream = stream
        op.sval = None
        op.label = self.label
        if stream is not None:
            self.streams[stream] = self.streams.get(stream, 0) + 1
            op.sval = 16 * self.streams[stream]
        deps = []
        for tl in reads:
            if tl.w is not None:
                deps.append(tl.w)
        for tl in writes:
            if tl.w is not None:
                deps.append(tl.w)
            deps.extend(tl.r.values())
        waits = []
        wd = self.waited[eng]
        for d in deps:
            if d.stream is not None:
                key = ("s", d.stream)
                val = d.sval
            else:
                if d.eng == eng and eng == "pe":
                    continue
                key = ("e", d.eng)
                val = d.idx
            if val <= wd.get(key, -1):
                continue
            wd[key] = val
            waits.append(d)
            if d.stream is None:
                d.signal = True
        op.waits = waits
        rkey = ("s", stream) if stream is not None else ("e", eng)
        for tl in reads:
            tl.r[rkey] = op
        for tl in writes:
            tl.w = op
            tl.r = {}
        self.ops[eng].append(op)
        return op

    def barrier(self):
        bt = self.t("__barrier__", len(self.tiles))
        lasts = []
        for e in ENGS:
            if self.ops[e]:
                lasts.append(self.ops[e][-1])
        last_stream = {}
        for e in ("pool", "sp"):
            for op in self.ops[e]:
                if op.stream is not None:
                    last_stream[op.stream] = op
        for e in ENGS:
            op = Op()
            op.eng = e
            op.fn = None
            op.idx = len(self.ops[e])
            op.signal = False
            op.stream = None
            op.sval = None
            waits = []
            wd = self.waited[e]
            for d in lasts:
                if d.eng == e or d.stream is not None:
                    continue
                key = ("e", d.eng)
                if d.idx <= wd.get(key, -1):
                    continue
                wd[key] = d.idx
                d.signal = True
                waits.append(d)
            for sname, d in last_stream.items():
                key = ("s", sname)
                if d.sval <= wd.get(key, -1):
                    continue
                wd[key] = d.sval
                waits.append(d)
            op.waits = waits
            self.ops[e].append(op)
        del bt

    def emit(self, nc, block, sems_eng, sems_stream):
        for e in ENGS:
            cnt = 0
            for op in self.ops[e]:
                if op.stream is None and op.signal:
                    cnt += 1
                    op.sval = cnt

        def run(h, e):
            for op in self.ops[e]:
                for d in op.waits:
                    sem = sems_stream[d.stream] if d.stream is not None else sems_eng[d.eng]
                    h.wait_ge(sem, d.sval)
                if op.fn is None:
                    if op.signal:
                        h.nop().then_inc(sems_eng[e], 1)
                    continue
                ins = op.fn(h)
                if op.stream is not None:
                    ins.then_inc(sems_stream[op.stream], 16)
                elif op.signal:
                    ins.then_inc(sems_eng[e], 1)

        block.tensor(lambda h: run(h, "pe"))
        block.scalar(lambda h: run(h, "act"))
        block.vector(lambda h: run(h, "dve"))
        block.gpsimd(lambda h: run(h, "pool"))
        block.sync(lambda h: run(h, "sp"))


def build_program():
    nc = bass.Bass("TRN2", target_bir_lowering=False)

    def din(name, shape):
        return nc.dram_tensor(name, shape, F32, kind="ExternalInput").ap()

    x_own = din("x_own", [NTOK, D])
    x_prev = din("x_prev", [NTOK, D])
    w_in = din("w_in", [D, WIN_COLS])
    w_gk2 = din("w_gk2", [16, 512])
    b_gk = din("b_gk", [128, 4])
    normw = din("normw", [1, 256])
    sinks = din("sinks", [128, 8])
    w_out = din("w_out", [D, D])
    lnp = din("lnp", [128, 64])
    ln2row = din("ln2row", [2, D])
    w_up = din("w_up", [D, DFF])
    w_down = din("w_down", [DFF, D])
    masks = din("masks", [128, 4 * 128])
    scanpat = din("scanpat", [128, 512])
    ident_in = din("ident", [128, 128])
    y = nc.dram_tensor("y", [NTOK, D], F32, kind="ExternalOutput").ap()

    w_in_v = w_in.rearrange("(kc p) n -> p kc n", p=128)
    w_out_v = w_out.rearrange("(kc p) n -> p kc n", p=128)
    w_up_v = w_up.rearrange("(kc p) n -> p kc n", p=128)
    w_down_v = w_down.rearrange("(fc p) n -> p fc n", p=128)

    import contextlib
    es = contextlib.ExitStack()
    with es:
        def sb(name, shape, dtype):
            return es.enter_context(nc.sbuf_tensor(name, shape, dtype))

        Wr = [sb(f"wring{i}", [128, 8192], BF16) for i in range(2)]
        RA = sb("regA", [128, 32768], BF16)
        RB = sb("regB", [128, 16384], BF16)
        RC = sb("regC", [128, 24576], BF16)
        tmpr_t = sb("tmpr_t", [128, 2, 512], F32)
        ident_b = sb("ident_b", [128, 128], BF16)
        ident_f = sb("ident_f", [128, 128], F32)
        ones_b = sb("ones_b", [128, 64], BF16)
        masks_b = sb("masks_b", [128, 4, 128], BF16)
        pat_f = sb("pat_f", [128, 512], F32)
        wgk2_b = sb("wgk2_b", [16, 512], BF16)
        negb = sb("negb", [128, 4], F32)
        normw_bc = sb("normw_bc", [128, 256], F32)
        sinkexp = sb("sinkexp", [128, 8], F32)
        lnp_s = sb("lnp_s", [128, 64], F32)
        lnpa = sb("lnpa", [128, 32], F32)
        ksT = sb("ksT", [128, 2, 1152], BF16)
        vs = sb("vs", [128, 9, 128], BF16)
        gkloT = sb("gkloT", [16, NT], BF16)
        S_f = sb("S_f", [128, 4, 256], F32)
        S_b = sb("S_b", [128, 4, 256], BF16)
        negcC = sb("negcC", [128, 4, 8], F32)
        eC = sb("eC", [128, 4, 8], F32)
        junk = sb("junk", [128, 256], BF16)
        small = sb("small", [128, 64], F32)
        bnst = sb("bnst", [128, 6, 4, 6], F32)
        ps = [es.enter_context(nc.psum_tensor(f"ps{i}", [128, 512], F32)) for i in range(8)]
        sems_eng = {e: es.enter_context(nc.semaphore(f"sem_{e}")) for e in ENGS}
        stream_names = ["w0", "w1", "w2", "w3", "xtok0", "xtok1", "xtok2", "xtok3", "xtok4", "xtok5", "xtok6", "xtok7", "z0", "z1", "z2", "z3", "z4", "z5", "ost0", "ost1", "constp", "consts", "bc2"]
        sems_stream = {s: es.enter_context(nc.semaphore(f"sem_{s}")) for s in stream_names}
        block = es.enter_context(nc.Block())

        def f32v(reg, off, n):
            return reg[:, off:off + 2 * n].bitcast(F32)

        c_f = f32v(RA, 0, 4096).rearrange("p (h t) -> p h t", h=4)
        tmpf = [f32v(RA, 8192 + i * 1024, 512) for i in range(4)]
        ktT = [RA[:, 12288 + i * 1024: 12288 + (i + 1) * 1024] for i in range(2)]
        qtT = [RA[:, 14336 + i * 1024: 14336 + (i + 1) * 1024] for i in range(2)]
        kdT = [RA[:, 16384 + i * 512: 16384 + (i + 1) * 512] for i in range(2)]
        kd_tok = [RA[:, 17408 + i * 1024: 17408 + (i + 1) * 1024].rearrange("p (t d) -> p t d", d=128) for i in range(2)]
        v_tok = [RA[:, 19456 + i * 2048: 19456 + (i + 1) * 2048].rearrange("p (t e) -> p t e", e=256) for i in range(2)]
        gw = [f32v(RA, 23552 + i * 4096, 2048).rearrange("p (t e) -> p t e", e=256) for i in range(2)]
        At = [RA[:, 31744 + i * 128: 31744 + (i + 1) * 128] for i in range(8)]
        At4 = [RA[:, 31744 + i * 512: 31744 + (i + 1) * 512] for i in range(2)]
        qsT = [RA[:, 8192 + i * 4096: 8192 + (i + 1) * 4096].rearrange("p (c t) -> p c t", c=4) for i in range(2)]
        PT = [RA[:, 16384 + i * 512: 16384 + (i + 1) * 512] for i in range(8)]
        swtmp = [f32v(RA, 20480 + i * 1024, 512) for i in range(2)]
        xT = RC[:, 0:16384].rearrange("p (k t) -> p k t", k=16)
        xtok = [RB[:, i * 2048:(i + 1) * 2048] for i in range(8)]
        gla_h = [RC[:, 20480 + i * 2048: 20480 + (i + 1) * 2048].rearrange("p (t e) -> p t e", e=256) for i in range(2)]
        zt = [f32v(RC, i * 4096, 2048) for i in range(6)]
        hdn = [RC[:, i * 8192:(i + 1) * 8192].rearrange("p (f t) -> p f t", f=8) for i in range(2)]
        tmpr = [tmpr_t[:, i, :] for i in range(2)]
        g2bc = f32v(RC, 0, 2048)
        b2bc = f32v(RC, 4096, 2048)
        ztmp2 = [f32v(RC, 8192, 2048), f32v(RC, 20480, 2048)]
        ostage = [f32v(RC, 12288 + i * 4096, 2048) for i in range(2)]
        mixT = RB[:, :].rearrange("p (k t) -> p k t", k=16)
        acc = f32v(RA, 0, 16384).rearrange("p (k t) -> p k t", k=16)

        def record(P, wplan, xplan):
            T = P.t
            wlog = []
            xlog = []
            xissued = [0]
            bank_ctr = [0]

            held_banks = set()

            def next_bank():
                while True:
                    i = bank_ctr[0] % 8
                    bank_ctr[0] += 1
                    if i not in held_banks:
                        return ps[i], T("ps", i)

            small_ctr = [0]

            def next_small(n=1):
                i = small_ctr[0] % (64 // n)
                small_ctr[0] += 1
                return small[:, i * n:(i + 1) * n], T("small", n, i)

            def mm(out, lhsT, rhs, start, stop, reads, writes):
                P.add("pe", lambda h: h.matmul(out, lhsT=lhsT, rhs=rhs, start=start, stop=stop), reads, writes)

            def tr(out, in_, ident, reads, writes):
                P.add("pe", lambda h: h.transpose(out, in_, ident), reads, writes)

            def act(out, in_, func, reads, writes, bias=None, scale=None, accum_out=None):
                kw = {}
                if bias is not None:
                    kw["bias"] = bias
                if scale is not None:
                    kw["scale"] = scale
                if accum_out is not None:
                    kw["accum_out"] = accum_out
                P.add("act", lambda h: h.activation(out, in_, func, **kw), reads, writes)

            def tt(eng, out, in0, in1, op, reads, writes):
                P.add(eng, lambda h: h.tensor_tensor(out, in0, in1, op), reads, writes)

            def ts(eng, out, in0, s1, s2, op0, op1, reads, writes):
                P.add(eng, lambda h: h.tensor_scalar(out, in0, s1, s2, op0, op1), reads, writes)

            def stt(out, in0, scalar, in1, op0, op1, reads, writes):
                P.add("dve", lambda h: h.scalar_tensor_tensor(out, in0, scalar, in1, op0, op1), reads, writes)

            def cp(eng, out, in_, reads, writes):
                if eng == "act":
                    P.add("act", lambda h: h.copy(out, in_), reads, writes)
                else:
                    P.add(eng, lambda h: h.tensor_copy(out, in_), reads, writes)

            def dma(q, stream, out, in_, reads, writes):
                P.add(q, lambda h: h.dma_start(out=out, in_=in_), reads, writes, stream=stream)

            wslot_ctr = [0]

            wassign = []
            wptr = [0]
            wowner = [-1, -1, -1, -1]
            wissued = [0]

            def assign_w(j, req):
                while len(wassign) <= j:
                    jj = len(wassign)
                    _, nk, ncols = (wlog[jj] if wplan is None else wplan[jj])
                    if nk * ncols <= 4096:
                        hs = [wptr[0] % 4]
                        wptr[0] += 1
                    else:
                        if wptr[0] % 2:
                            wptr[0] += 1
                        hs = [wptr[0] % 4, wptr[0] % 4 + 1]
                        wptr[0] += 2
                    wassign.append(hs)
                return wassign[j]

            def w_view(hs, nk, ncols):
                base = (hs[0] % 2) * 4096
                return Wr[hs[0] // 2][:, base:base + nk * ncols].rearrange("p (k n) -> p k n", n=ncols)

            def issue_w(j, req):
                src_ap, nk, ncols = req
                hs = assign_w(j, req)
                dma("pool", f"w{hs[0]}", w_view(hs, nk, ncols), src_ap, [], [T("wh", h_) for h_ in hs])
                for h_ in hs:
                    wowner[h_] = j

            def load_w(src_ap, nk, ncols):
                i = len(wlog)
                wlog.append((src_ap, nk, ncols))
                if wplan is None:
                    issue_w(i, wlog[i])
                    wissued[0] = i + 1
                else:
                    while wissued[0] < len(wplan) and wissued[0] <= i + 3:
                        j = wissued[0]
                        hs = assign_w(j, wplan[j])
                        if j > i and any(wowner[h_] >= i for h_ in hs):
                            break
                        issue_w(j, wplan[j])
                        wissued[0] += 1
                hs = assign_w(i, wlog[i])
                return w_view(hs, nk, ncols), [T("wh", h_) for h_ in hs]

            tc = T("const")
            tcp = T("constp")
            dma("pool", "constp", ident_b[:], ident_in, [], [tcp])
            dma("sp", "consts", ident_f[:], ident_in, [], [tc])
            dma("pool", "constp", masks_b[:].rearrange("p a b -> p (a b)"), masks, [], [tcp])
            dma("sp", "consts", pat_f[:], scanpat, [], [tc])
            dma("pool", "constp", wgk2_b[:], w_gk2, [], [tcp])
            dma("sp", "consts", negb[:], b_gk, [], [tc])
            dma("sp", "consts", normw_bc[:], normw.partition_broadcast(128), [], [tc])
            dma("sp", "consts", sinkexp[:], sinks, [], [tc])
            dma("sp", "consts", lnp_s[:], lnp, [], [tc])
            tc2 = T("const2")
            ts("dve", negb[:], negb[:], -1.0, None, ALU.mult, ALU.bypass, [tc, tcp], [tc2])
            P.add("dve", lambda h: h.memset(ones_b[:], 1.0), [tc2], [tc2])
            P.add("dve", lambda h: h.memset(S_f[:].rearrange("p a b -> p (a b)"), 0.0), [tc2], [T("S", hh) for hh in range(4)])
            P.add("dve", lambda h: h.memset(S_b[:].rearrange("p a b -> p (a b)"), 0.0), [tc2], [T("Sb", hh) for hh in range(4)])
            P.add("dve", lambda h: h.memset(ksT[:].rearrange("p a b -> p (a b)"), 0.0), [tc2], [T("ksT", g, "carry") for g in range(2)])
            P.add("dve", lambda h: h.memset(vs[:].rearrange("p a b -> p (a b)"), 0.0), [tc2], [T("vs", 0)])
            ts("dve", lnpa[:], lnp_s[:, 0:32], ALPHA, None, ALU.mult, ALU.bypass, [tc], [tc2])
            act(sinkexp[:], sinkexp[:], AF.Exp, [tc], [tc2])
            CONST = [tc, tcp, tc2]

            def issue_x_loads(xsrc, pi):
                for t in range(8):
                    rows = xsrc[pi * NT + t * 128: pi * NT + (t + 1) * 128, :]
                    alias = [T("mixT", 2 * t), T("mixT", 2 * t + 1)] + [T("x1T", 2 * t + a, hf) for a in range(2) for hf in range(2)]
                    dma("pool", f"xtok{t}", xtok[t].rearrange("p (a b) -> p a b", a=2), rows.rearrange("p (a b) -> p a b", a=2), [], alias)

            ZSLOT = [0, 1, 2, 3, 4, 5, 2, 3]

            def do_pass(xsrc, pi, full, first_own, nxt, after_full):
                if after_full:
                    P.barrier()
                tok0 = pi * NT
                P.label = f"{int(full)}{pi}:xT"
                def x_tile(t):
                    sl = t
                    for g8 in range(2):
                        bk, bt = next_bank()
                        bkb = bk[:, :].bitcast(BF16)
                        for jx in range(8):
                            dc = g8 * 8 + jx
                            tr(bkb[:, jx * 128:(jx + 1) * 128], xtok[sl][:, dc * 128:(dc + 1) * 128], ident_b[:],
                               [T("mixT", 2 * sl), T("mixT", 2 * sl + 1)] + CONST, [bt])
                        cp("act" if g8 == 0 else "dve", xT[:, g8 * 8:(g8 + 1) * 8, t * 128:(t + 1) * 128],
                           bkb.rearrange("p (k t) -> p k t", k=8), [bt], [T("xT", g8, t)])

                for t in range(4):
                    x_tile(t)

                def xT_reads(half):
                    return [T("xT", g8, t) for g8 in range(2) for t in range(half * 4, half * 4 + 4)]

                def proj_fm(wv, wt, c0, m, half, bk, bt):
                    for kc in range(16):
                        mm(bk[0:m, :], wv[:, kc, c0:c0 + m], xT[:, kc, half * 512:(half + 1) * 512],
                           kc == 0, kc == 15, wt + xT_reads(half), [bt])

                def proj_tm(wv, wt, c0, n, t, out_ap, bt):
                    for kc in range(16):
                        mm(out_ap, xT[:, kc, t * 128:(t + 1) * 128], wv[:, kc, c0:c0 + n],
                           kc == 0, kc == 15, wt + [T("xT", 0, t), T("xT", 1, t)], [bt])

                P.label = f"{int(full)}{pi}:misc"
                wv, wt = load_w(w_in_v[:, :, 0:400], 16, 400)

                def misc_half(half):
                    if full:
                        for g in range(2):
                            bk, bt = next_bank()
                            proj_fm(wv, wt, g * 128, 128, half, bk, bt)
                            cp("act", ksT[:, g, 128 + half * 512: 128 + (half + 1) * 512], bk[:, :], [bt], [T("ksT", g, half)])
                        bk, bt = next_bank()
                        for j in range(4):
                            t = half * 4 + j
                            proj_tm(wv, wt, 256, 128, t, bk[:, j * 128:(j + 1) * 128], bt)
                        cp("dve", vs[:, 1 + half * 4: 5 + half * 4, :], bk[:, :].rearrange("p (t e) -> p t e", t=4), [bt], [T("vs", 1 + half)])
                    elif pi == 1 and half == 1:
                        for g in range(2):
                            bk, bt = next_bank()
                            for kc in range(16):
                                mm(bk[:, 0:128], wv[:, kc, g * 128:(g + 1) * 128], xT[:, kc, 896:1024], kc == 0, kc == 15,
                                   wt + [T("xT", 0, 7), T("xT", 1, 7)], [bt])
                            cp("act", ksT[:, g, 1024:1152], bk[:, 0:128], [bt], [T("ksT", g, 1)])
                        bk, bt = next_bank()
                        proj_tm(wv, wt, 256, 128, 7, bk[:, 0:128], bt)
                        cp("dve", vs[:, 8, :], bk[:, 0:128], [bt], [T("vs", 2)])
                    bk, bt = next_bank()
                    proj_fm(wv, wt, 384, 16, half, bk, bt)
                    cp("act", gkloT[:, half * 512:(half + 1) * 512], bk[0:16, :], [bt], [T("gklo", half)])
                    for h in range(4):
                        bk, bt = next_bank()
                        mm(bk[:, :], wgk2_b[:, h * 128:(h + 1) * 128], gkloT[:, half * 512:(half + 1) * 512], True, True,
                           [T("gklo", half)] + CONST, [bt])
                        tf = (h * 2 + half) % 4
                        act(tmpf[tf], bk[:, :], AF.Exp, [bt] + CONST, [T("tmpf", tf)], bias=negb[:, h:h + 1], scale=-1.0)
                        act(tmpf[tf], tmpf[tf], AF.Ln, [T("tmpf", tf)], [T("tmpf", tf)], bias=1.0)
                        P.add("dve", lambda hh, o=c_f[:, h, half * 512:(half + 1) * 512], d0=pat_f[:, :], d1=tmpf[tf]:
                              hh.tensor_tensor_scan(o, d0, d1, 0.0, ALU.mult, ALU.add), [T("tmpf", tf)] + CONST, [T("c", h, half)])

                misc_half(0)
                P.label = f"{int(full)}{pi}:xT"
                for t in range(4, 8):
                    x_tile(t)
                P.label = f"{int(full)}{pi}:misc"
                misc_half(1)
                call = [T("c", h, half) for h in range(4) for half in range(2)]
                tcc = T("cC")
                ts("dve", negcC[:].rearrange("p h c -> p (h c)"),
                   c_f.rearrange("p h (c t) -> p (h c) t", t=128)[:, :, 127], -1.0 / 16.0, None, ALU.mult, ALU.bypass, call, [tcc])
                act(eC[:].rearrange("p h c -> p (h c)"), negcC[:].rearrange("p h c -> p (h c)"), AF.Exp, [tcc], [T("eC")])

                if not full and nxt is not None:
                    issue_x_loads(*nxt)
                P.label = f"{int(full)}{pi}:gla"
                def gla_proj(h):
                    hb = h % 2
                    base = 400 + h * 768

                    def kd_transposes(half):
                        bk2, bt2 = next_bank()
                        bkb = bk2[:, :].bitcast(BF16)
                        for j in range(4):
                            tr(bkb[:, j * 128:(j + 1) * 128], kdT[half][:, j * 128:(j + 1) * 128], ident_b[:],
                               [T("kdT", half)] + CONST, [bt2])
                        cp("act", kd_tok[hb][:, half * 4:(half + 1) * 4, :], bkb[:, 0:512].rearrange("p (t d) -> p t d", t=4),
                           [bt2], [T("kd_tok", hb, half)])

                    wv, wt = load_w(w_in_v[:, :, base:base + 384], 16, 384)

                    def k_step(half):
                        bk, bt = next_bank()
                        proj_fm(wv, wt, 0, 128, half, bk, bt)
                        if full:
                            tf = half
                            act(tmpf[tf], c_f[:, h, half * 512:(half + 1) * 512], AF.Exp, [T("c", h, half)], [T("tmpf", tf)], scale=1.0 / 16.0)
                            tt("dve", ktT[hb][:, half * 512:(half + 1) * 512], bk[:, :], tmpf[tf], ALU.mult,
                               [bt, T("tmpf", tf)], [T("ktT", hb, half)])
                        tf = 2 + half
                        for j in range(4):
                            cj = half * 4 + j
                            act(tmpf[tf][:, j * 128:(j + 1) * 128], c_f[:, h, cj * 128:(cj + 1) * 128], AF.Exp,
                                [T("c", h, half), tcc], [T("tmpf", tf)], bias=negcC[:, h, cj:cj + 1], scale=1.0 / 16.0)
                        tt("dve", kdT[half], bk[:, :], tmpf[tf], ALU.mult, [bt, T("tmpf", tf)], [T("kdT", half)])

                    def v_step(tq):
                        bk, bt = next_bank()
                        for j in range(2):
                            t = tq * 2 + j
                            proj_tm(wv, wt, 128, 256, t, bk[:, j * 256:(j + 1) * 256], bt)
                        cp("dve" if tq % 2 else "act", v_tok[hb][:, tq * 2:tq * 2 + 2, :], bk[:, :].rearrange("p (t e) -> p t e", t=2),
                           [bt], [T("v_tok", hb, tq)])

                    if h == 0:
                        for tq in range(4):
                            v_step(tq)
                            yield
                        k_step(0)
                        yield
                        k_step(1)
                        kd_transposes(0)
                        yield
                        kd_transposes(1)
                        yield
                    else:
                        k_step(0)
                        yield
                        k_step(1)
                        kd_transposes(0)
                        yield
                        for tq in range(4):
                            v_step(tq)
                            if tq == 0:
                                kd_transposes(1)
                            yield
                    if full:
                        wv2, wt2 = load_w(w_in_v[:, :, base + 384:base + 768], 16, 384)
                        for half in range(2):
                            bk, bt = next_bank()
                            proj_fm(wv2, wt2, 0, 128, half, bk, bt)
                            tf = half
                            act(tmpf[tf], c_f[:, h, half * 512:(half + 1) * 512], AF.Exp, [T("c", h, half)], [T("tmpf", tf)], scale=-1.0 / 16.0)
                            stt(qtT[hb][:, half * 512:(half + 1) * 512], bk[:, :], 128.0 ** -0.5, tmpf[tf], ALU.mult, ALU.mult,
                                [bt, T("tmpf", tf)], [T("qtT", hb, half)])
                            yield
                        for tq in range(4):
                            bk, bt = next_bank()
                            for j in range(2):
                                t = tq * 2 + j
                                proj_tm(wv2, wt2, 128, 256, t, bk[:, j * 256:(j + 1) * 256], bt)
                            tf = 2 + tq % 2
                            act(tmpf[tf], bk[:, :], AF.Silu, [bt], [T("tmpf", tf)])
                            tt("dve", gw[hb][:, tq * 2:tq * 2 + 2, :], tmpf[tf].rearrange("p (t e) -> p t e", t=2),
                               normw_bc[:].unsqueeze(1).to_broadcast([128, 2, 256]), ALU.mult,
                               [T("tmpf", tf)] + CONST, [T("gw", hb, tq)])
                            yield

                def gla_chunks(h):
                    hb = h % 2
                    if full:
                        for half in range(2):
                            bk, bt = next_bank()
                            for j in range(4):
                                t = half * 4 + j
                                mm(bk[:, j * 128:(j + 1) * 128], ktT[hb][:, t * 128:(t + 1) * 128], qtT[hb][:, t * 128:(t + 1) * 128],
                                   True, True, [T("ktT", hb, half), T("qtT", hb, half)], [bt])
                            tt("dve", At4[half].rearrange("p (c t) -> p c t", c=4), bk[:, :].rearrange("p (c t) -> p c t", c=4),
                               masks_b[:, 0, :].unsqueeze(1).to_broadcast([128, 4, 128]), ALU.mult, [bt] + CONST, [T("At", half)])
                        yield
                    for t in range(8):
                        half = t // 4
                        tq = t // 2
                        if full:
                            a = t
                            bo, bot = next_bank()
                            mm(bo[:, 0:256], At[a], v_tok[hb][:, t, :], True, False, [T("At", half), T("v_tok", hb, tq)], [bot])
                            mm(bo[:, 0:256], qtT[hb][:, t * 128:(t + 1) * 128], S_b[:, h, :], False, True,
                               [T("qtT", hb, half), T("Sb", h)], [bot])
                            ss, sst = next_small()
                            act(junk[:], bo[:, 0:256], AF.Square, [bot], [T("junk"), sst], accum_out=ss)
                            ln_, lnt = next_small()
                            act(ln_, ss, AF.Ln, [sst], [lnt], bias=RMS_EPS, scale=1.0 / 256.0)
                            rs, rst = next_small()
                            act(rs, ln_, AF.Exp, [lnt], [rst], scale=-0.5)
                            stt(gla_h[hb][:, t, :], bo[:, 0:256], rs, gw[hb][:, t, :], ALU.mult, ALU.mult,
                                [bot, rst, T("gw", hb, tq)], [T("gla_h", hb, t)])
                        bu, but = next_bank()
                        mm(bu[:, 0:256], kd_tok[hb][:, t, :], v_tok[hb][:, t, :], True, True,
                           [T("kd_tok", hb, half), T("v_tok", hb, tq)], [but])
                        stt(S_f[:, h, :], S_f[:, h, :], eC[:, h, t:t + 1], bu[:, 0:256], ALU.mult, ALU.add,
                            [T("S", h), T("eC"), but], [T("S", h)])
                        cp("act", S_b[:, h, :], S_f[:, h, :], [T("S", h)], [T("Sb", h)])
                        yield
                    if full:
                        for ec in range(2):
                            bk, bt = next_bank()
                            bkb = bk[:, :].bitcast(BF16)
                            for t in range(8):
                                tr(bkb[:, t * 128:(t + 1) * 128], gla_h[hb][:, t, ec * 128:(ec + 1) * 128], ident_b[:],
                                   [T("gla_h", hb, t)] + CONST, [bt])
                            cp("act" if ec else "dve", mixT[:, 2 * h + ec, :], bkb, [bt], [T("mixT", 2 * h + ec)])
                        yield

                def swa_proj(g, alias_tmpf):
                    gb = g % 2
                    base = 400 + 4 * 768 + g * 512
                    wv, wt = load_w(w_in_v[:, :, base:base + 512], 16, 512)
                    for c in range(4):
                        for half in range(2):
                            bk, bt = next_bank()
                            proj_fm(wv, wt, c * 128, 128, half, bk, bt)
                            extra = [T("tmpf", c)] if alias_tmpf else []
                            cp("act" if half else "dve", qsT[gb][:, c, half * 512:(half + 1) * 512], bk[:, :], [bt],
                               [T("qsT", gb, c, half)] + extra)
                            yield

                def swa_blocks(g):
                    gb = g % 2
                    pts_all = {}

                    def st1(b):
                        half = b // 4
                        pts = {}
                        for p in range(2):
                            for kb in range(2):
                                bk, bt = next_bank()
                                kcol = (b + kb) * 128
                                kread = [T("ksT", g, "carry")] if kcol < 128 else [T("ksT", g, (kcol - 128) // 512)]
                                mm(bk[:, :], ksT[p * 64:(p + 1) * 64, g, kcol:kcol + 128],
                                   qsT[gb][p * 64:(p + 1) * 64, :, b * 128:(b + 1) * 128], True, False,
                                   kread + [T("qsT", gb, c, half) for c in range(4)], [bt])
                                mi = 1 if kb == 1 else (3 if (first_own and b == 0) else 2)
                                mm(bk[:, :], ident_b[:], masks_b[:, mi, :].unsqueeze(1).to_broadcast([128, 4, 128]), False, True, CONST, [bt])
                                pi_ = (b % 2) * 4 + p * 2 + kb
                                act(PT[pi_], bk[:, :], AF.Exp, [bt], [T("PT", pi_)], scale=0.125)
                                pts[(p, kb)] = pi_
                        pts_all[b] = pts

                    def st2(b):
                        pts = pts_all[b]
                        bn_, bnt = next_bank()
                        bd_, bdt = next_bank()
                        for p in range(2):
                            for kb in range(2):
                                vblk = b + kb
                                vread = [T("vs", 0)] if vblk == 0 else [T("vs", 1 + (vblk - 1) // 4)]
                                mm(bn_[p * 64:(p + 1) * 64, :], vs[:, vblk, g * 64:(g + 1) * 64], PT[pts[(p, kb)]], kb == 0, kb == 1,
                                   vread + [T("PT", pts[(p, kb)])], [bnt])
                        for p in range(2):
                            for kb in range(2):
                                mm(bd_[p * 64:(p + 1) * 64, :], ones_b[:, :], PT[pts[(p, kb)]], kb == 0, kb == 1,
                                   [T("PT", pts[(p, kb)])] + CONST, [bdt])
                        sw = b % 2
                        tt("dve", swtmp[sw].rearrange("p (c t) -> p c t", c=4), bd_[:, :].rearrange("p (c t) -> p c t", c=4),
                           sinkexp[:, g * 4:(g + 1) * 4].unsqueeze(2).to_broadcast([128, 4, 128]), ALU.add,
                           [bdt] + CONST, [T("swtmp", sw)])
                        P.add("dve", lambda hh, o=swtmp[sw]: hh.reciprocal(o, o), [T("swtmp", sw)], [T("swtmp", sw)])
                        tt("dve", mixT[:, 8 + 4 * g: 12 + 4 * g, b * 128:(b + 1) * 128],
                           bn_[:, :].rearrange("p (c t) -> p c t", c=4), swtmp[sw].rearrange("p (c t) -> p c t", c=4), ALU.mult,
                           [bnt, T("swtmp", sw)], [T("mixT", 8 + 4 * g + c) for c in range(4)])

                    st1(0)
                    yield
                    for b in range(8):
                        if b + 1 < 8:
                            st1(b + 1)
                            yield
                        st2(b)
                        yield

                def run_interleaved(a, b):
                    alive = [g_ for g_ in (a, b) if g_ is not None]
                    while alive:
                        for g_ in list(alive):
                            try:
                                next(g_)
                            except StopIteration:
                                alive.remove(g_)

                prev_chunks = None
                for h in range(4):
                    run_interleaved(prev_chunks, gla_proj(h))
                    prev_chunks = gla_chunks(h)
                if full:
                    run_interleaved(prev_chunks, swa_proj(0, True))
                    P.label = f"{int(full)}{pi}:swa"
                    P.barrier()
                    run_interleaved(swa_blocks(0), swa_proj(1, False))
                    run_interleaved(swa_blocks(1), None)
                else:
                    run_interleaved(prev_chunks, None)
                if full or pi == 1:
                    for g in range(2):
                        cp("dve", ksT[:, g, 0:128], ksT[:, g, 1024:1152], [T("ksT", g, 1)], [T("ksT", g, "carry")])
                    cp("dve", vs[:, 0, :], vs[:, 8, :], [T("vs", 2)], [T("vs", 0)])
                if not full:
                    return

                P.label = f"{int(full)}{pi}:outproj"
                def hdn_alias(hb, fc):
                    return [T("zt", hb * 2 + fc // 4, q) for q in range(4)]

                def mlp_up_gen(s, halves):
                    hb = s % 2
                    for u in range(4):
                        c0 = s * 1024 + u * 256
                        wv, wt = load_w(w_up_v[:, :, c0:c0 + 256], 16, 256)
                        for fcl in range(2):
                            fc = u * 2 + fcl
                            for half in halves:
                                bk, bt = next_bank()
                                for kc in range(16):
                                    mm(bk[:, :], wv[:, kc, fcl * 128:(fcl + 1) * 128], mixT[:, kc, half * 512:(half + 1) * 512],
                                       kc == 0, kc == 15, wt + [T("x1T", kc, half)], [bt])
                                tf = half
                                act(tmpr[tf], bk[:, :], AF.Relu, [bt], [T("tmpr", tf)])
                                tt("dve", hdn[hb][:, fc, half * 512:(half + 1) * 512], tmpr[tf], tmpr[tf], ALU.mult,
                                   [T("tmpr", tf)], [T("hdn", hb, fc, half)] + hdn_alias(hb, fc))
                        yield

                def mlp_up(s):
                    run_interleaved(mlp_up_gen(s, (0, 1)), None)

                P.barrier()

                def load_xres(t):
                    sl = ZSLOT[t]
                    row0 = tok0 + t * 128
                    dma("sp", f"z{sl}", zt[sl], x_own[row0:row0 + 128, :], [], [T("zt", sl, q) for q in range(4)])

                def op_mm_stage(half, preloaded):
                    tiles = [half * 4 + i for i in range(4)]
                    for q in range(4):
                        banks = [next_bank() for _ in tiles]
                        bidx = [bt_.name[1] for _, bt_ in banks]
                        held_banks.update(bidx)
                        for c0 in (0, 256):
                            wv, wt = load_w(w_out_v[:, :, q * 512 + c0:q * 512 + c0 + 256], 16, 256)
                            for ti, t in enumerate(tiles):
                                if q == 0 and c0 == 0 and t not in preloaded:
                                    load_xres(t)
                                bk, bt = banks[ti]
                                for kc in range(16):
                                    mm(bk[:, c0:c0 + 256], mixT[:, kc, t * 128:(t + 1) * 128], wv[:, kc, :], kc == 0, kc == 15,
                                       wt + [T("mixT", kc)], [bt])
                                if c0 == 256:
                                    sl = ZSLOT[t]
                                    zq = zt[sl][:, q * 512:(q + 1) * 512]
                                    stt(zq, zq, ALPHA, bk[:, :], ALU.mult, ALU.add, [T("zt", sl, q), bt], [T("zt", sl, q)])
                                    P.add("dve", lambda hh, o=bnst[:, sl, q, :], i=zq: hh.bn_stats(o, i), [T("zt", sl, q)], [T("bnst", sl)])
                                    held_banks.discard(bidx[ti])
                                yield

                def ln_A(t):
                    sl = ZSLOT[t]
                    mv, mvt = next_small(2)
                    P.add("dve", lambda hh, o=mv, i=bnst[:, sl].rearrange("p a b -> p (a b)"): hh.bn_aggr(o, i), [T("bnst", sl)], [mvt])
                    ln_, lnt = next_small()
                    act(ln_, mv[:, 1:2], AF.Ln, [mvt], [lnt], bias=LN_EPS)
                    rs, rst = next_small()
                    act(rs, ln_, AF.Exp, [lnt], [rst], scale=-0.5)
                    nm, nmt = next_small()
                    stt(nm, mv[:, 0:1], -1.0, rs, ALU.mult, ALU.mult, [mvt, rst], [nmt])
                    ztl = [T("zt", sl, q) for q in range(4)]
                    act(zt[sl], zt[sl], AF.Identity, ztl + [rst, nmt], ztl, bias=nm, scale=rs)

                def ln_B(t):
                    sl = ZSLOT[t]
                    half = t // 4
                    for q in range(4):
                        bk, bt = next_bank()
                        for j in range(4):
                            dc = q * 4 + j
                            tr(bk[:, j * 128:(j + 1) * 128], zt[sl][:, dc * 128:(dc + 1) * 128], ident_f[:], [T("zt", sl, q)] + CONST, [bt])
                        for j in range(4):
                            dc = q * 4 + j
                            if q % 2 == 0:
                                act(acc[:, dc, t * 128:(t + 1) * 128], bk[:, j * 128:(j + 1) * 128], AF.Identity, [bt] + CONST,
                                    [T("acc", dc, half)], bias=lnpa[:, 16 + dc:17 + dc], scale=lnpa[:, dc:dc + 1])
                            else:
                                ts("dve", acc[:, dc, t * 128:(t + 1) * 128], bk[:, j * 128:(j + 1) * 128], lnpa[:, dc:dc + 1],
                                   lnpa[:, 16 + dc:17 + dc], ALU.mult, ALU.add, [bt] + CONST, [T("acc", dc, half)])

                def ln_stage(half):
                    for t in [half * 4 + i for i in range(4)]:
                        ln_A(t)
                        ln_B(t)
                        yield

                def ln_B_stage(half):
                    for t in [half * 4 + i for i in range(4)]:
                        ln_B(t)
                        yield

                def x1t_conv(half):
                    for dc in range(16):
                        if dc % 2:
                            act(mixT[:, dc, half * 512:(half + 1) * 512], acc[:, dc, half * 512:(half + 1) * 512], AF.Copy,
                                [T("acc", dc, half)], [T("x1T", dc, half), T("mixT", dc)], scale=1.0 / ALPHA)
                        else:
                            ts("dve", mixT[:, dc, half * 512:(half + 1) * 512], acc[:, dc, half * 512:(half + 1) * 512], 1.0 / ALPHA, None,
                               ALU.mult, ALU.bypass, [T("acc", dc, half)], [T("x1T", dc, half), T("mixT", dc)])
                        if dc % 4 == 3:
                            yield

                for t in range(4):
                    load_xres(t)
                run_interleaved(op_mm_stage(0, [0, 1, 2, 3]), None)
                load_xres(4)
                load_xres(5)
                run_interleaved(ln_stage(0), op_mm_stage(1, [4, 5]))
                def chain(*gens):
                    for g_ in gens:
                        yield from g_

                P.label = f"{int(full)}{pi}:mlp"
                ln_A(4)
                run_interleaved(x1t_conv(0), None)
                for t in (5, 6, 7):
                    ln_A(t)
                run_interleaved(ln_B_stage(1), mlp_up_gen(0, (0,)))
                run_interleaved(x1t_conv(1), None)
                run_interleaved(mlp_up_gen(0, (1,)), None)

                P.label = f"{int(full)}{pi}:mlp"
                def mlp_down(s):
                    hb = s % 2
                    for q in range(4):
                        wv, wt = load_w(w_down_v[:, s * 8:(s + 1) * 8, q * 512:(q + 1) * 512], 8, 512)
                        for dcl in range(4):
                            dc = q * 4 + dcl
                            for half in range(2):
                                bk, bt = next_bank()
                                for fc in range(8):
                                    mm(bk[:, :], wv[:, fc, dcl * 128:(dcl + 1) * 128], hdn[hb][:, fc, half * 512:(half + 1) * 512],
                                       fc == 0, fc == 7, wt + [T("hdn", hb, fc, half)], [bt])
                                tt("dve", acc[:, dc, half * 512:(half + 1) * 512], acc[:, dc, half * 512:(half + 1) * 512], bk[:, :], ALU.add,
                                   [bt, T("acc", dc, half)], [T("acc", dc, half)])

                for s in range(8):
                    if s + 1 < 8:
                        mlp_up(s + 1)
                    elif nxt is not None:
                        issue_x_loads(*nxt)
                    mlp_down(s)

                P.label = f"{int(full)}{pi}:epi"
                P.barrier()
                dma("sp", "bc2", g2bc, ln2row[0:1, :].partition_broadcast(128), [], [T("g2bc")])
                dma("sp", "bc2", b2bc, ln2row[1:2, :].partition_broadcast(128), [], [T("g2bc")])
                for t in range(8):
                    half = t // 4
                    zb = t % 2
                    z2 = ztmp2[zb]
                    for q in range(4):
                        bk, bt = next_bank()
                        for j in range(4):
                            dc = q * 4 + j
                            tr(bk[:, j * 128:(j + 1) * 128], acc[:, dc, t * 128:(t + 1) * 128], ident_f[:], [T("acc", dc, half)] + CONST, [bt])
                        cp("act", z2[:, q * 512:(q + 1) * 512], bk[:, :], [bt], [T("z2", zb, q)])
                        P.add("dve", lambda hh, o=bnst[:, zb, q, :], i=z2[:, q * 512:(q + 1) * 512]: hh.bn_stats(o, i),
                              [T("z2", zb, q)], [T("bnst", zb)])
                    mv, mvt = next_small(2)
                    P.add("dve", lambda hh, o=mv, i=bnst[:, zb].rearrange("p a b -> p (a b)"): hh.bn_aggr(o, i), [T("bnst", zb)], [mvt])
                    ln_, lnt = next_small()
                    act(ln_, mv[:, 1:2], AF.Ln, [mvt], [lnt], bias=LN_EPS)
                    rs, rst = next_small()
                    act(rs, ln_, AF.Exp, [lnt], [rst], scale=-0.5)
                    nm, nmt = next_small()
                    stt(nm, mv[:, 0:1], -1.0, rs, ALU.mult, ALU.mult, [mvt, rst], [nmt])
                    z2l = [T("z2", zb, q) for q in range(4)]
                    act(z2, z2, AF.Identity, z2l + [rst, nmt], z2l, bias=nm, scale=rs)
                    os_ = t % 2
                    tt("dve", z2, z2, g2bc, ALU.mult, z2l + [T("g2bc")], z2l)
                    tt("pool", ostage[os_], z2, b2bc, ALU.add, z2l + [T("g2bc")], [T("ost", os_)])
                    row0 = tok0 + t * 128
                    dma("sp", f"ost{os_}", y[row0:row0 + 128, :], ostage[os_], [T("ost", os_)], [T("ystore", os_)])

            issue_x_loads(x_prev, 0)
            do_pass(x_prev, 0, False, False, (x_prev, 1), False)
            do_pass(x_prev, 1, False, False, (x_own, 0), False)
            do_pass(x_own, 0, True, True, (x_own, 1), False)
            do_pass(x_own, 1, True, False, None, True)
            P.add("sp", lambda h: h.nop(), [T("ystore", 0), T("ystore", 1)], [])
            return wlog, xlog

        plan, xpl = record(Prog(), None, None)
        P = Prog()
        record(P, plan, xpl)
        nc._pe_labels = [op.label for op in P.ops["pe"] if op.fn is not None]
        P.emit(nc, block, sems_eng, sems_stream)
    return nc


def host_layout(x, w_in, w_gk2, b_gk, gla_norm_w, swa_sinks, w_out, ln1_g, ln1_b, w_up, w_down, ln2_g, ln2_b):
    f = np.float32
    w = np.asarray(w_in[0], f)
    qg, kg, vg, gg = w[:, 0:512], w[:, 512:1024], w[:, 1024:2048], w[:, 2048:3072]
    gk = w[:, 3072:3088]
    qs, ks, vsw = w[:, 3088:4112], w[:, 4112:4240], w[:, 4240:4368]
    cols = [ks[:, 0:64], ks[:, 0:64], ks[:, 64:128], ks[:, 64:128], vsw, gk]
    for h in range(4):
        cols += [kg[:, h * 128:(h + 1) * 128], vg[:, h * 256:(h + 1) * 256], qg[:, h * 128:(h + 1) * 128], gg[:, h * 256:(h + 1) * 256]]
    cols.append(qs)
    w_in_r = np.ascontiguousarray(np.concatenate(cols, axis=1))
    assert w_in_r.shape == (D, WIN_COLS)
    sinks = np.zeros((128, 8), f)
    sk = np.asarray(swa_sinks[0], f)
    for g in range(2):
        for c in range(4):
            for p in range(2):
                sinks[p * 64:(p + 1) * 64, g * 4 + c] = sk[8 * g + 2 * c + p]
    lnp = np.concatenate([np.asarray(a[0], f).reshape(16, 128).T for a in (ln1_g, ln1_b, ln2_g, ln2_b)], axis=1)
    ln2row = np.stack([np.asarray(ln2_g[0], f), np.asarray(ln2_b[0], f)])
    j = np.arange(128)[:, None]
    i = np.arange(128)[None, :]
    causal = (j <= i).astype(f)
    cur = np.where(j <= i, 0.0, NEG).astype(f)
    prev = np.where(j > i, 0.0, NEG).astype(f)
    scanpat = np.ones((128, 512), f)
    scanpat[:, ::128] = 0.0
    common = {
        "w_in": w_in_r,
        "w_gk2": np.ascontiguousarray(np.asarray(w_gk2[0], f)),
        "b_gk": np.ascontiguousarray(np.asarray(b_gk[0], f).reshape(4, 128).T),
        "normw": np.ascontiguousarray(np.asarray(gla_norm_w[0], f)[None, :]),
        "sinks": sinks,
        "w_out": np.ascontiguousarray(np.asarray(w_out[0], f)),
        "lnp": np.ascontiguousarray(lnp),
        "ln2row": np.ascontiguousarray(ln2row),
        "w_up": np.ascontiguousarray(np.asarray(w_up[0], f)),
        "w_down": np.ascontiguousarray(np.asarray(w_down[0], f)),
        "scanpat": scanpat,
        "ident": np.eye(128, dtype=f),
    }
    xs = np.asarray(x, f)
    in_maps = []
    for c in range(8):
        b, half = c // 2, c % 2
        m = dict(common)
        m["x_own"] = np.ascontiguousarray(xs[b, half * NTOK:(half + 1) * NTOK])
        m["x_prev"] = np.ascontiguousarray(xs[b, 0:NTOK]) if half == 1 else np.zeros((NTOK, D), f)
        prev0 = prev if half == 1 else np.full((128, 128), NEG, f)
        m["masks"] = np.ascontiguousarray(np.concatenate([causal, cur, prev, prev0], axis=1))
        in_maps.append(m)
    return in_maps


_NC_CACHE = {}


def kernel(x, w_in, w_gk2, b_gk, gla_norm_w, swa_sinks, w_out, ln1_g, ln1_b, w_up, w_down, ln2_g, ln2_b):
    in_maps = host_layout(x, w_in, w_gk2, b_gk, gla_norm_w, swa_sinks, w_out, ln1_g, ln1_b, w_up, w_down, ln2_g, ln2_b)
    if "nc" not in _NC_CACHE:
        _NC_CACHE["nc"] = build_program()
    res = run_bass_kernel_spmd(_NC_CACHE["nc"], in_maps, core_ids=list(range(8)))
    out = np.empty((4, 4096, D), np.float32)
    for c in range(8):
        b, half = c // 2, c % 2
        out[b, half * NTOK:(half + 1) * NTOK] = res.results[c]["y"]
    return out
```

```python
import numpy as np
import concourse.bass as bass
import concourse.mybir as mybir
from concourse.bass_utils import run_bass_kernel_spmd

F32 = mybir.dt.float32
BF16 = mybir.dt.bfloat16
AF = mybir.ActivationFunctionType
ALU = mybir.AluOpType

D = 2048
NTOK = 2048
NT = 1024
DFF = 8192
ALPHA = 2.0 ** 0.25
LN_EPS = 1e-5
RMS_EPS = 1e-5
NEG = -30000.0
WIN_COLS = 400 + 4 * 768 + 1024

ENGS = ["pe", "act", "dve", "pool", "sp"]


class Tile:
    __slots__ = ("name", "w", "r")

    def __init__(self, name):
        self.name = name
        self.w = None
        self.r = {}


class Op:
    __slots__ = ("eng", "idx", "fn", "waits", "signal", "stream", "sval", "label")


class Prog:
    def __init__(self):
        self.ops = {e: [] for e in ENGS}
        self.waited = {e: {} for e in ENGS}
        self.streams = {}
        self.tiles = {}
        self.label = ""

    def t(self, *key):
        tl = self.tiles.get(key)
        if tl is None:
            tl = Tile(key)
            self.tiles[key] = tl
        return tl

    def add(self, eng, fn, reads=(), writes=(), stream=None):
        op = Op()
        op.eng = eng
        op.fn = fn
        op.idx = len(self.ops[eng])
        op.signal = False
        op.stream = stream
        op.sval = None
        op.label = self.label
        if stream is not None:
            self.streams[stream] = self.streams.get(stream, 0) + 1
            op.sval = 16 * self.streams[stream]
        deps = []
        for tl in reads:
            if tl.w is not None:
                deps.append(tl.w)
        for tl in writes:
            if tl.w is not None:
                deps.append(tl.w)
            deps.extend(tl.r.values())
        waits = []
        wd = self.waited[eng]
        for d in deps:
            if d.stream is not None:
                key = ("s", d.stream)
                val = d.sval
            else:
                if d.eng == eng and eng == "pe":
                    continue
                key = ("e", d.eng)
                val = d.idx
            if val <= wd.get(key, -1):
                continue
            wd[key] = val
            waits.append(d)
            if d.stream is None:
                d.signal = True
        op.waits = waits
        rkey = ("s", stream) if stream is not None else ("e", eng)
        for tl in reads:
            tl.r[rkey] = op
        for tl in writes:
            tl.w = op
            tl.r = {}
        self.ops[eng].append(op)
        return op

    def barrier(self):
        bt = self.t("__barrier__", len(self.tiles))
        lasts = []
        for e in ENGS:
            if self.ops[e]:
                lasts.append(self.ops[e][-1])
        last_stream = {}
        for e in ("pool", "sp"):
            for op in self.ops[e]:
                if op.stream is not None:
                    last_stream[op.stream] = op
        for e in ENGS:
            op = Op()
            op.eng = e
            op.fn = None
            op.idx = len(self.ops[e])
            op.signal = False
            op.stream = None
            op.sval = None
            waits = []
            wd = self.waited[e]
            for d in lasts:
                if d.eng == e or d.stream is not None:
                    continue
                key = ("e", d.eng)
                if d.idx <= wd.get(key, -1):
                    continue
                wd[key] = d.idx
                d.signal = True
                waits.append(d)
            for sname, d in last_stream.items():
                key = ("s", sname)
                if d.sval <= wd.get(key, -1):
                    continue
                wd[key] = d.sval
                waits.append(d)
            op.waits = waits
            self.ops[e].append(op)
        del bt

    def emit(self, nc, block, sems_eng, sems_stream):
        for e in ENGS:
            cnt = 0
            for op in self.ops[e]:
                if op.stream is None and op.signal:
                    cnt += 1
                    op.sval = cnt

        def run(h, e):
            for op in self.ops[e]:
                for d in op.waits:
                    sem = sems_stream[d.stream] if d.stream is not None else sems_eng[d.eng]
                    h.wait_ge(sem, d.sval)
                if op.fn is None:
                    if op.signal:
                        h.nop().then_inc(sems_eng[e], 1)
                    continue
                ins = op.fn(h)
                if op.stream is not None:
                    ins.then_inc(sems_stream[op.stream], 16)
                elif op.signal:
                    ins.then_inc(sems_eng[e], 1)

        block.tensor(lambda h: run(h, "pe"))
        block.scalar(lambda h: run(h, "act"))
        block.vector(lambda h: run(h, "dve"))
        block.gpsimd(lambda h: run(h, "pool"))
        block.sync(lambda h: run(h, "sp"))


def build_program():
    nc = bass.Bass("TRN2", target_bir_lowering=False)

    def din(name, shape):
        return nc.dram_tensor(name, shape, F32, kind="ExternalInput").ap()

    x_own = din("x_own", [NTOK, D])
    x_prev = din("x_prev", [NTOK, D])
    w_in = din("w_in", [D, WIN_COLS])
    w_gk2 = din("w_gk2", [16, 512])
    b_gk = din("b_gk", [128, 4])
    normw = din("normw", [1, 256])
    sinks = din("sinks", [128, 8])
    w_out = din("w_out", [D, D])
    lnp = din("lnp", [128, 64])
    ln2row = din("ln2row", [2, D])
    w_up = din("w_up", [D, DFF])
    w_down = din("w_down", [DFF, D])
    masks = din("masks", [128, 4 * 128])
    scanpat = din("scanpat", [128, 512])
    ident_in = din("ident", [128, 128])
    y = nc.dram_tensor("y", [NTOK, D], F32, kind="ExternalOutput").ap()

    w_in_v = w_in.rearrange("(kc p) n -> p kc n", p=128)
    w_out_v = w_out.rearrange("(kc p) n -> p kc n", p=128)
    w_up_v = w_up.rearrange("(kc p) n -> p kc n", p=128)
    w_down_v = w_down.rearrange("(fc p) n -> p fc n", p=128)

    import contextlib
    es = contextlib.ExitStack()
    with es:
        def sb(name, shape, dtype):
            return es.enter_context(nc.sbuf_tensor(name, shape, dtype))

        Wr = [sb(f"wring{i}", [128, 8192], BF16) for i in range(2)]
        RA = sb("regA", [128, 32768], BF16)
        RB = sb("regB", [128, 16384], BF16)
        RC = sb("regC", [128, 24576], BF16)
        tmpr_t = sb("tmpr_t", [128, 2, 512], F32)
        ident_b = sb("ident_b", [128, 128], BF16)
        ident_f = sb("ident_f", [128, 128], F32)
        ones_b = sb("ones_b", [128, 64], BF16)
        masks_b = sb("masks_b", [128, 4, 128], BF16)
        pat_f = sb("pat_f", [128, 512], F32)
        wgk2_b = sb("wgk2_b", [16, 512], BF16)
        negb = sb("negb", [128, 4], F32)
        normw_bc = sb("normw_bc", [128, 256], F32)
        sinkexp = sb("sinkexp", [128, 8], F32)
        lnp_s = sb("lnp_s", [128, 64], F32)
        lnpa = sb("lnpa", [128, 32], F32)
        ksT = sb("ksT", [128, 2, 1152], BF16)
        vs = sb("vs", [128, 9, 128], BF16)
        gkloT = sb("gkloT", [16, NT], BF16)
        S_f = sb("S_f", [128, 4, 256], F32)
        S_b = sb("S_b", [128, 4, 256], BF16)
        negcC = sb("negcC", [128, 4, 8], F32)
        eC = sb("eC", [128, 4, 8], F32)
        junk = sb("junk", [128, 256], BF16)
        small = sb("small", [128, 64], F32)
        bnst = sb("bnst", [128, 6, 4, 6], F32)
        ps = [es.enter_context(nc.psum_tensor(f"ps{i}", [128, 512], F32)) for i in range(8)]
        sems_eng = {e: es.enter_context(nc.semaphore(f"sem_{e}")) for e in ENGS}
        stream_names = ["w0", "w1", "w2", "w3", "xtok0", "xtok1", "xtok2", "xtok3", "xtok4", "xtok5", "xtok6", "xtok7", "z0", "z1", "z2", "z3", "z4", "z5", "ost0", "ost1", "constp", "consts", "bc2"]
        sems_stream = {s: es.enter_context(nc.semaphore(f"sem_{s}")) for s in stream_names}
        block = es.enter_context(nc.Block())

        def f32v(reg, off, n):
            return reg[:, off:off + 2 * n].bitcast(F32)

        c_f = f32v(RA, 0, 4096).rearrange("p (h t) -> p h t", h=4)
        tmpf = [f32v(RA, 8192 + i * 1024, 512) for i in range(4)]
        ktT = [RA[:, 12288 + i * 1024: 12288 + (i + 1) * 1024] for i in range(2)]
        qtT = [RA[:, 14336 + i * 1024: 14336 + (i + 1) * 1024] for i in range(2)]
        kdT = [RA[:, 16384 + i * 512: 16384 + (i + 1) * 512] for i in range(2)]
        kd_tok = [RA[:, 17408 + i * 1024: 17408 + (i + 1) * 1024].rearrange("p (t d) -> p t d", d=128) for i in range(2)]
        v_tok = [RA[:, 19456 + i * 2048: 19456 + (i + 1) * 2048].rearrange("p (t e) -> p t e", e=256) for i in range(2)]
        gw = [f32v(RA, 23552 + i * 4096, 2048).rearrange("p (t e) -> p t e", e=256) for i in range(2)]
        At = [RA[:, 31744 + i * 128: 31744 + (i + 1) * 128] for i in range(8)]
        At4 = [RA[:, 31744 + i * 512: 31744 + (i + 1) * 512] for i in range(2)]
        qsT = [RA[:, 8192 + i * 4096: 8192 + (i + 1) * 4096].rearrange("p (c t) -> p c t", c=4) for i in range(2)]
        PT = [RA[:, 16384 + i * 512: 16384 + (i + 1) * 512] for i in range(8)]
        swtmp = [f32v(RA, 20480 + i * 1024, 512) for i in range(2)]
        xT = RC[:, 0:16384].rearrange("p (k t) -> p k t", k=16)
        xtok = [RB[:, i * 2048:(i + 1) * 2048] for i in range(8)]
        gla_h = [RC[:, 20480 + i * 2048: 20480 + (i + 1) * 2048].rearrange("p (t e) -> p t e", e=256) for i in range(2)]
        zt = [f32v(RC, i * 4096, 2048) for i in range(6)]
        hdn = [RC[:, i * 8192:(i + 1) * 8192].rearrange("p (f t) -> p f t", f=8) for i in range(2)]
        tmpr = [tmpr_t[:, i, :] for i in range(2)]
        g2bc = f32v(RC, 0, 2048)
        b2bc = f32v(RC, 4096, 2048)
        ztmp2 = [f32v(RC, 8192, 2048), f32v(RC, 20480, 2048)]
        ostage = [f32v(RC, 12288 + i * 4096, 2048) for i in range(2)]
        mixT = RB[:, :].rearrange("p (k t) -> p k t", k=16)
        acc = f32v(RA, 0, 16384).rearrange("p (k t) -> p k t", k=16)

        def record(P, wplan, xplan):
            T = P.t
            wlog = []
            xlog = []
            xissued = [0]
            bank_ctr = [0]

            held_banks = set()

            def next_bank():
                while True:
                    i = bank_ctr[0] % 8
                    bank_ctr[0] += 1
                    if i not in held_banks:
                        return ps[i], T("ps", i)

            small_ctr = [0]

            def next_small(n=1):
                i = small_ctr[0] % (64 // n)
                small_ctr[0] += 1
                return small[:, i * n:(i + 1) * n], T("small", n, i)

            def mm(out, lhsT, rhs, start, stop, reads, writes):
                P.add("pe", lambda h: h.matmul(out, lhsT=lhsT, rhs=rhs, start=start, stop=stop), reads, writes)

            def tr(out, in_, ident, reads, writes):
                P.add("pe", lambda h: h.transpose(out, in_, ident), reads, writes)

            def act(out, in_, func, reads, writes, bias=None, scale=None, accum_out=None):
                kw = {}
                if bias is not None:
                    kw["bias"] = bias
                if scale is not None:
                    kw["scale"] = scale
                if accum_out is not None:
                    kw["accum_out"] = accum_out
                P.add("act", lambda h: h.activation(out, in_, func, **kw), reads, writes)

            def tt(eng, out, in0, in1, op, reads, writes):
                P.add(eng, lambda h: h.tensor_tensor(out, in0, in1, op), reads, writes)

            def ts(eng, out, in0, s1, s2, op0, op1, reads, writes):
                P.add(eng, lambda h: h.tensor_scalar(out, in0, s1, s2, op0, op1), reads, writes)

            def stt(out, in0, scalar, in1, op0, op1, reads, writes):
                P.add("dve", lambda h: h.scalar_tensor_tensor(out, in0, scalar, in1, op0, op1), reads, writes)

            def cp(eng, out, in_, reads, writes):
                if eng == "act":
                    P.add("act", lambda h: h.copy(out, in_), reads, writes)
                else:
                    P.add(eng, lambda h: h.tensor_copy(out, in_), reads, writes)

            def dma(q, stream, out, in_, reads, writes):
                P.add(q, lambda h: h.dma_start(out=out, in_=in_), reads, writes, stream=stream)

            wslot_ctr = [0]

            wassign = []
            wptr = [0]
            wowner = [-1, -1, -1, -1]
            wissued = [0]

            def assign_w(j, req):
                while len(wassign) <= j:
                    jj = len(wassign)
                    _, nk, ncols = (wlog[jj] if wplan is None else wplan[jj])
                    if nk * ncols <= 4096:
                        hs = [wptr[0] % 4]
                        wptr[0] += 1
                    else:
                        if wptr[0] % 2:
                            wptr[0] += 1
                        hs = [wptr[0] % 4, wptr[0] % 4 + 1]
                        wptr[0] += 2
                    wassign.append(hs)
                return wassign[j]

            def w_view(hs, nk, ncols):
                base = (hs[0] % 2) * 4096
                return Wr[hs[0] // 2][:, base:base + nk * ncols].rearrange("p (k n) -> p k n", n=ncols)

            def issue_w(j, req):
                src_ap, nk, ncols = req
                hs = assign_w(j, req)
                dma("pool", f"w{hs[0]}", w_view(hs, nk, ncols), src_ap, [], [T("wh", h_) for h_ in hs])
                for h_ in hs:
                    wowner[h_] = j

            def load_w(src_ap, nk, ncols):
                i = len(wlog)
                wlog.append((src_ap, nk, ncols))
                if wplan is None:
                    issue_w(i, wlog[i])
                    wissued[0] = i + 1
                else:
                    while wissued[0] < len(wplan) and wissued[0] <= i + 3:
                        j = wissued[0]
                        hs = assign_w(j, wplan[j])
                        if j > i and any(wowner[h_] >= i for h_ in hs):
                            break
                        issue_w(j, wplan[j])
                        wissued[0] += 1
                hs = assign_w(i, wlog[i])
                return w_view(hs, nk, ncols), [T("wh", h_) for h_ in hs]

            tc = T("const")
            tcp = T("constp")
            dma("pool", "constp", ident_b[:], ident_in, [], [tcp])
            dma("sp", "consts", ident_f[:], ident_in, [], [tc])
            dma("pool", "constp", masks_b[:].rearrange("p a b -> p (a b)"), masks, [], [tcp])
            dma("sp", "consts", pat_f[:], scanpat, [], [tc])
            dma("pool", "constp", wgk2_b[:], w_gk2, [], [tcp])
            dma("sp", "consts", negb[:], b_gk, [], [tc])
            dma("sp", "consts", normw_bc[:], normw.partition_broadcast(128), [], [tc])
            dma("sp", "consts", sinkexp[:], sinks, [], [tc])
            dma("sp", "consts", lnp_s[:], lnp, [], [tc])
            tc2 = T("const2")
            ts("dve", negb[:], negb[:], -1.0, None, ALU.mult, ALU.bypass, [tc, tcp], [tc2])
            P.add("dve", lambda h: h.memset(ones_b[:], 1.0), [tc2], [tc2])
            P.add("dve", lambda h: h.memset(S_f[:].rearrange("p a b -> p (a b)"), 0.0), [tc2], [T("S", hh) for hh in range(4)])
            P.add("dve", lambda h: h.memset(S_b[:].rearrange("p a b -> p (a b)"), 0.0), [tc2], [T("Sb", hh) for hh in range(4)])
            P.add("dve", lambda h: h.memset(ksT[:].rearrange("p a b -> p (a b)"), 0.0), [tc2], [T("ksT", g, "carry") for g in range(2)])
            P.add("dve", lambda h: h.memset(vs[:].rearrange("p a b -> p (a b)"), 0.0), [tc2], [T("vs", 0)])
            ts("dve", lnpa[:], lnp_s[:, 0:32], ALPHA, None, ALU.mult, ALU.bypass, [tc], [tc2])
            act(sinkexp[:], sinkexp[:], AF.Exp, [tc], [tc2])
            CONST = [tc, tcp, tc2]

            def issue_x_loads(xsrc, pi):
                for t in range(8):
                    rows = xsrc[pi * NT + t * 128: pi * NT + (t + 1) * 128, :]
                    alias = [T("mixT", 2 * t), T("mixT", 2 * t + 1)] + [T("x1T", 2 * t + a, hf) for a in range(2) for hf in range(2)]
                    dma("pool", f"xtok{t}", xtok[t].rearrange("p (a b) -> p a b", a=2), rows.rearrange("p (a b) -> p a b", a=2), [], alias)

            ZSLOT = [0, 1, 2, 3, 4, 5, 2, 3]

            def do_pass(xsrc, pi, full, first_own, nxt, after_full):
                if after_full:
                    P.barrier()
                tok0 = pi * NT
                P.label = f"{int(full)}{pi}:xT"
                def x_tile(t):
                    sl = t
                    for g8 in range(2):
                        bk, bt = next_bank()
                        bkb = bk[:, :].bitcast(BF16)
                        for jx in range(8):
                            dc = g8 * 8 + jx
                            tr(bkb[:, jx * 128:(jx + 1) * 128], xtok[sl][:, dc * 128:(dc + 1) * 128], ident_b[:],
                               [T("mixT", 2 * sl), T("mixT", 2 * sl + 1)] + CONST, [bt])
                        cp("act" if g8 == 0 else "dve", xT[:, g8 * 8:(g8 + 1) * 8, t * 128:(t + 1) * 128],
                           bkb.rearrange("p (k t) -> p k t", k=8), [bt], [T("xT", g8, t)])

                for t in range(4):
                    x_tile(t)

                def xT_reads(half):
                    return [T("xT", g8, t) for g8 in range(2) for t in range(half * 4, half * 4 + 4)]

                def proj_fm(wv, wt, c0, m, half, bk, bt):
                    for kc in range(16):
                        mm(bk[0:m, :], wv[:, kc, c0:c0 + m], xT[:, kc, half * 512:(half + 1) * 512],
                           kc == 0, kc == 15, wt + xT_reads(half), [bt])

                def proj_tm(wv, wt, c0, n, t, out_ap, bt):
                    for kc in range(16):
                        mm(out_ap, xT[:, kc, t * 128:(t + 1) * 128], wv[:, kc, c0:c0 + n],
                           kc == 0, kc == 15, wt + [T("xT", 0, t), T("xT", 1, t)], [bt])

                P.label = f"{int(full)}{pi}:misc"
                wv, wt = load_w(w_in_v[:, :, 0:400], 16, 400)

                def misc_half(half):
                    if full:
                        for g in range(2):
                            bk, bt = next_bank()
                            proj_fm(wv, wt, g * 128, 128, half, bk, bt)
                            cp("act", ksT[:, g, 128 + half * 512: 128 + (half + 1) * 512], bk[:, :], [bt], [T("ksT", g, half)])
                        bk, bt = next_bank()
                        for j in range(4):
                            t = half * 4 + j
                            proj_tm(wv, wt, 256, 128, t, bk[:, j * 128:(j + 1) * 128], bt)
                        cp("dve", vs[:, 1 + half * 4: 5 + half * 4, :], bk[:, :].rearrange("p (t e) -> p t e", t=4), [bt], [T("vs", 1 + half)])
                    elif pi == 1 and half == 1:
                        for g in range(2):
                            bk, bt = next_bank()
                            for kc in range(16):
                                mm(bk[:, 0:128], wv[:, kc, g * 128:(g + 1) * 128], xT[:, kc, 896:1024], kc == 0, kc == 15,
                                   wt + [T("xT", 0, 7), T("xT", 1, 7)], [bt])
                            cp("act", ksT[:, g, 1024:1152], bk[:, 0:128], [bt], [T("ksT", g, 1)])
                        bk, bt = next_bank()
                        proj_tm(wv, wt, 256, 128, 7, bk[:, 0:128], bt)
                        cp("dve", vs[:, 8, :], bk[:, 0:128], [bt], [T("vs", 2)])
                    bk, bt = next_bank()
                    proj_fm(wv, wt, 384, 16, half, bk, bt)
                    cp("act", gkloT[:, half * 512:(half + 1) * 512], bk[0:16, :], [bt], [T("gklo", half)])
                    for h in range(4):
                        bk, bt = next_bank()
                        mm(bk[:, :], wgk2_b[:, h * 128:(h + 1) * 128], gkloT[:, half * 512:(half + 1) * 512], True, True,
                           [T("gklo", half)] + CONST, [bt])
                        tf = (h * 2 + half) % 4
                        act(tmpf[tf], bk[:, :], AF.Exp, [bt] + CONST, [T("tmpf", tf)], bias=negb[:, h:h + 1], scale=-1.0)
                        act(tmpf[tf], tmpf[tf], AF.Ln, [T("tmpf", tf)], [T("tmpf", tf)], bias=1.0)
                        P.add("dve", lambda hh, o=c_f[:, h, half * 512:(half + 1) * 512], d0=pat_f[:, :], d1=tmpf[tf]:
                              hh.tensor_tensor_scan(o, d0, d1, 0.0, ALU.mult, ALU.add), [T("tmpf", tf)] + CONST, [T("c", h, half)])

                misc_half(0)
                P.label = f"{int(full)}{pi}:xT"
                for t in range(4, 8):
                    x_tile(t)
                P.label = f"{int(full)}{pi}:misc"
                misc_half(1)
                call = [T("c", h, half) for h in range(4) for half in range(2)]
                tcc = T("cC")
                ts("dve", negcC[:].rearrange("p h c -> p (h c)"),
                   c_f.rearrange("p h (c t) -> p (h c) t", t=128)[:, :, 127], -1.0 / 16.0, None, ALU.mult, ALU.bypass, call, [tcc])
                act(eC[:].rearrange("p h c -> p (h c)"), negcC[:].rearrange("p h c -> p (h c)"), AF.Exp, [tcc], [T("eC")])

                if not full and nxt is not None:
                    issue_x_loads(*nxt)
                P.label = f"{int(full)}{pi}:gla"
                def gla_proj(h):
                    hb = h % 2
                    base = 400 + h * 768

                    def kd_transposes(half):
                        bk2, bt2 = next_bank()
                        bkb = bk2[:, :].bitcast(BF16)
                        for j in range(4):
                            tr(bkb[:, j * 128:(j + 1) * 128], kdT[half][:, j * 128:(j + 1) * 128], ident_b[:],
                               [T("kdT", half)] + CONST, [bt2])
                        cp("act", kd_tok[hb][:, half * 4:(half + 1) * 4, :], bkb[:, 0:512].rearrange("p (t d) -> p t d", t=4),
                           [bt2], [T("kd_tok", hb, half)])

                    wv, wt = load_w(w_in_v[:, :, base:base + 384], 16, 384)

                    def k_step(half):
                        bk, bt = next_bank()
                        proj_fm(wv, wt, 0, 128, half, bk, bt)
                        if full:
                            tf = half
                            act(tmpf[tf], c_f[:, h, half * 512:(half + 1) * 512], AF.Exp, [T("c", h, half)], [T("tmpf", tf)], scale=1.0 / 16.0)
                            tt("dve", ktT[hb][:, half * 512:(half + 1) * 512], bk[:, :], tmpf[tf], ALU.mult,
                               [bt, T("tmpf", tf)], [T("ktT", hb, half)])
                        tf = 2 + half
                        for j in range(4):
                            cj = half * 4 + j
                            act(tmpf[tf][:, j * 128:(j + 1) * 128], c_f[:, h, cj * 128:(cj + 1) * 128], AF.Exp,
                                [T("c", h, half), tcc], [T("tmpf", tf)], bias=negcC[:, h, cj:cj + 1], scale=1.0 / 16.0)
                        tt("dve", kdT[half], bk[:, :], tmpf[tf], ALU.mult, [bt, T("tmpf", tf)], [T("kdT", half)])

                    def v_step(tq):
                        bk, bt = next_bank()
                        for j in range(2):
                            t = tq * 2 + j
                            proj_tm(wv, wt, 128, 256, t, bk[:, j * 256:(j + 1) * 256], bt)
                        cp("dve" if tq % 2 else "act", v_tok[hb][:, tq * 2:tq * 2 + 2, :], bk[:, :].rearrange("p (t e) -> p t e", t=2),
                           [bt], [T("v_tok", hb, tq)])

                    if h == 0:
                        for tq in range(4):
                            v_step(tq)
                            yield
                        k_step(0)
                        yield
                        k_step(1)
                        kd_transposes(0)
                        yield
                        kd_transposes(1)
                        yield
                    else:
                        k_step(0)
                        yield
                        k_step(1)
                        kd_transposes(0)
                        yield
                        for tq in range(4):
                            v_step(tq)
                            if tq == 0:
                                kd_transposes(1)
                            yield
                    if full:
                        wv2, wt2 = load_w(w_in_v[:, :, base + 384:base + 768], 16, 384)
                        for half in range(2):
                            bk, bt = next_bank()
                            proj_fm(wv2, wt2, 0, 128, half, bk, bt)
                            tf = half
                            act(tmpf[tf], c_f[:, h, half * 512:(half + 1) * 512], AF.Exp, [T("c", h, half)], [T("tmpf", tf)], scale=-1.0 / 16.0)
                            stt(qtT[hb][:, half * 512:(half + 1) * 512], bk[:, :], 128.0 ** -0.5, tmpf[tf], ALU.mult, ALU.mult,
                                [bt, T("tmpf", tf)], [T("qtT", hb, half)])
                            yield
                        for tq in range(4):
                            bk, bt = next_bank()
                            for j in range(2):
                                t = tq * 2 + j
                                proj_tm(wv2, wt2, 128, 256, t, bk[:, j * 256:(j + 1) * 256], bt)
                            tf = 2 + tq % 2
                            act(tmpf[tf], bk[:, :], AF.Silu, [bt], [T("tmpf", tf)])
                            tt("dve", gw[hb][:, tq * 2:tq * 2 + 2, :], tmpf[tf].rearrange("p (t e) -> p t e", t=2),
                               normw_bc[:].unsqueeze(1).to_broadcast([128, 2, 256]), ALU.mult,
                               [T("tmpf", tf)] + CONST, [T("gw", hb, tq)])
                            yield

                def gla_chunks(h):
                    hb = h % 2
                    if full:
                        for half in range(2):
                            bk, bt = next_bank()
                            for j in range(4):
                                t = half * 4 + j
                                mm(bk[:, j * 128:(j + 1) * 128], ktT[hb][:, t * 128:(t + 1) * 128], qtT[hb][:, t * 128:(t + 1) * 128],
                                   True, True, [T("ktT", hb, half), T("qtT", hb, half)], [bt])
                            tt("dve", At4[half].rearrange("p (c t) -> p c t", c=4), bk[:, :].rearrange("p (c t) -> p c t", c=4),
                               masks_b[:, 0, :].unsqueeze(1).to_broadcast([128, 4, 128]), ALU.mult, [bt] + CONST, [T("At", half)])
                        yield
                    for t in range(8):
                        half = t // 4
                        tq = t // 2
                        if full:
                            a = t
                            bo, bot = next_bank()
                            mm(bo[:, 0:256], At[a], v_tok[hb][:, t, :], True, False, [T("At", half), T("v_tok", hb, tq)], [bot])
                            mm(bo[:, 0:256], qtT[hb][:, t * 128:(t + 1) * 128], S_b[:, h, :], False, True,
                               [T("qtT", hb, half), T("Sb", h)], [bot])
                            ss, sst = next_small()
                            act(junk[:], bo[:, 0:256], AF.Square, [bot], [T("junk"), sst], accum_out=ss)
                            ln_, lnt = next_small()
                            act(ln_, ss, AF.Ln, [sst], [lnt], bias=RMS_EPS, scale=1.0 / 256.0)
                            rs, rst = next_small()
                            act(rs, ln_, AF.Exp, [lnt], [rst], scale=-0.5)
                            stt(gla_h[hb][:, t, :], bo[:, 0:256], rs, gw[hb][:, t, :], ALU.mult, ALU.mult,
                                [bot, rst, T("gw", hb, tq)], [T("gla_h", hb, t)])
                        bu, but = next_bank()
                        mm(bu[:, 0:256], kd_tok[hb][:, t, :], v_tok[hb][:, t, :], True, True,
                           [T("kd_tok", hb, half), T("v_tok", hb, tq)], [but])
                        stt(S_f[:, h, :], S_f[:, h, :], eC[:, h, t:t + 1], bu[:, 0:256], ALU.mult, ALU.add,
                            [T("S", h), T("eC"), but], [T("S", h)])
                        cp("act", S_b[:, h, :], S_f[:, h, :], [T("S", h)], [T("Sb", h)])
                        yield
                    if full:
                        for ec in range(2):
                            bk, bt = next_bank()
                            bkb = bk[:, :].bitcast(BF16)
                            for t in range(8):
                                tr(bkb[:, t * 128:(t + 1) * 128], gla_h[hb][:, t, ec * 128:(ec + 1) * 128], ident_b[:],
                                   [T("gla_h", hb, t)] + CONST, [bt])
                            cp("act" if ec else "dve", mixT[:, 2 * h + ec, :], bkb, [bt], [T("mixT", 2 * h + ec)])
                        yield

                def swa_proj(g, alias_tmpf):
                    gb = g % 2
                    base = 400 + 4 * 768 + g * 512
                    wv, wt = load_w(w_in_v[:, :, base:base + 512], 16, 512)
                    for c in range(4):
                        for half in range(2):
                            bk, bt = next_bank()
                            proj_fm(wv, wt, c * 128, 128, half, bk, bt)
                            extra = [T("tmpf", c)] if alias_tmpf else []
                            cp("act" if half else "dve", qsT[gb][:, c, half * 512:(half + 1) * 512], bk[:, :], [bt],
                               [T("qsT", gb, c, half)] + extra)
                            yield

                def swa_blocks(g):
                    gb = g % 2
                    pts_all = {}

                    def st1(b):
                        half = b // 4
                        pts = {}
                        for p in range(2):
                            for kb in range(2):
                                bk, bt = next_bank()
                                kcol = (b + kb) * 128
                                kread = [T("ksT", g, "carry")] if kcol < 128 else [T("ksT", g, (kcol - 128) // 512)]
                                mm(bk[:, :], ksT[p * 64:(p + 1) * 64, g, kcol:kcol + 128],
                                   qsT[gb][p * 64:(p + 1) * 64, :, b * 128:(b + 1) * 128], True, False,
                                   kread + [T("qsT", gb, c, half) for c in range(4)], [bt])
                                mi = 1 if kb == 1 else (3 if (first_own and b == 0) else 2)
                                mm(bk[:, :], ident_b[:], masks_b[:, mi, :].unsqueeze(1).to_broadcast([128, 4, 128]), False, True, CONST, [bt])
                                pi_ = (b % 2) * 4 + p * 2 + kb
                                act(PT[pi_], bk[:, :], AF.Exp, [bt], [T("PT", pi_)], scale=0.125)
                                pts[(p, kb)] = pi_
                        pts_all[b] = pts

                    def st2(b):
                        pts = pts_all[b]
                        bn_, bnt = next_bank()
                        bd_, bdt = next_bank()
                        for p in range(2):
                            for kb in range(2):
                                vblk = b + kb
                                vread = [T("vs", 0)] if vblk == 0 else [T("vs", 1 + (vblk - 1) // 4)]
                                mm(bn_[p * 64:(p + 1) * 64, :], vs[:, vblk, g * 64:(g + 1) * 64], PT[pts[(p, kb)]], kb == 0, kb == 1,
                                   vread + [T("PT", pts[(p, kb)])], [bnt])
                        for p in range(2):
                            for kb in range(2):
                                mm(bd_[p * 64:(p + 1) * 64, :], ones_b[:, :], PT[pts[(p, kb)]], kb == 0, kb == 1,
                                   [T("PT", pts[(p, kb)])] + CONST, [bdt])
                        sw = b % 2
                        tt("dve", swtmp[sw].rearrange("p (c t) -> p c t", c=4), bd_[:, :].rearrange("p (c t) -> p c t", c=4),
                           sinkexp[:, g * 4:(g + 1) * 4].unsqueeze(2).to_broadcast([128, 4, 128]), ALU.add,
                           [bdt] + CONST, [T("swtmp", sw)])
                        P.add("dve", lambda hh, o=swtmp[sw]: hh.reciprocal(o, o), [T("swtmp", sw)], [T("swtmp", sw)])
                        tt("dve", mixT[:, 8 + 4 * g: 12 + 4 * g, b * 128:(b + 1) * 128],
                           bn_[:, :].rearrange("p (c t) -> p c t", c=4), swtmp[sw].rearrange("p (c t) -> p c t", c=4), ALU.mult,
                           [bnt, T("swtmp", sw)], [T("mixT", 8 + 4 * g + c) for c in range(4)])

                    st1(0)
                    yield
                    for b in range(8):
                        if b + 1 < 8:
                            st1(b + 1)
                            yield
                        st2(b)
                        yield

                def load_xres(t, alias_xT=False):
                    sl = ZSLOT[t]
                    row0 = tok0 + t * 128
                    extra = [T("xT", sl // 2, tt_) for tt_ in range(8)] if alias_xT else []
                    dma("sp", f"z{sl}", zt[sl], x_own[row0:row0 + 128, :], [], [T("zt", sl, q) for q in range(4)] + extra)

                def run_interleaved(a, b):
                    alive = [g_ for g_ in (a, b) if g_ is not None]
                    while alive:
                        for g_ in list(alive):
                            try:
                                next(g_)
                            except StopIteration:
                                alive.remove(g_)

                prev_chunks = None
                for h in range(4):
                    run_interleaved(prev_chunks, gla_proj(h))
                    prev_chunks = gla_chunks(h)
                if full:
                    run_interleaved(prev_chunks, swa_proj(0, True))
                    P.label = f"{int(full)}{pi}:swa"
                    P.barrier()
                    run_interleaved(swa_blocks(0), swa_proj(1, False))
                    for t in range(4):
                        load_xres(t, alias_xT=True)
                    run_interleaved(swa_blocks(1), None)
                else:
                    run_interleaved(prev_chunks, None)
                if full or pi == 1:
                    for g in range(2):
                        cp("dve", ksT[:, g, 0:128], ksT[:, g, 1024:1152], [T("ksT", g, 1)], [T("ksT", g, "carry")])
                    cp("dve", vs[:, 0, :], vs[:, 8, :], [T("vs", 2)], [T("vs", 0)])
                if not full:
                    return

                P.label = f"{int(full)}{pi}:outproj"
                def hdn_alias(hb, fc):
                    return [T("zt", hb * 2 + fc // 4, q) for q in range(4)]

                def mlp_up_gen(s, halves):
                    hb = s % 2
                    for u in range(4):
                        c0 = s * 1024 + u * 256
                        wv, wt = load_w(w_up_v[:, :, c0:c0 + 256], 16, 256)
                        for fcl in range(2):
                            fc = u * 2 + fcl
                            for half in halves:
                                bk, bt = next_bank()
                                for kc in range(16):
                                    mm(bk[:, :], wv[:, kc, fcl * 128:(fcl + 1) * 128], mixT[:, kc, half * 512:(half + 1) * 512],
                                       kc == 0, kc == 15, wt + [T("x1T", kc, half)], [bt])
                                tf = half
                                act(tmpr[tf], bk[:, :], AF.Relu, [bt], [T("tmpr", tf)])
                                tt("dve", hdn[hb][:, fc, half * 512:(half + 1) * 512], tmpr[tf], tmpr[tf], ALU.mult,
                                   [T("tmpr", tf)], [T("hdn", hb, fc, half)] + hdn_alias(hb, fc))
                        yield

                def mlp_up(s):
                    run_interleaved(mlp_up_gen(s, (0, 1)), None)

                P.barrier()

                def op_mm_stage(half, preloaded):
                    tiles = [half * 4 + i for i in range(4)]
                    for q in range(4):
                        banks = [next_bank() for _ in tiles]
                        bidx = [bt_.name[1] for _, bt_ in banks]
                        held_banks.update(bidx)
                        for c0 in (0, 256):
                            wv, wt = load_w(w_out_v[:, :, q * 512 + c0:q * 512 + c0 + 256], 16, 256)
                            for ti, t in enumerate(tiles):
                                if q == 0 and c0 == 0 and t not in preloaded:
                                    load_xres(t)
                                bk, bt = banks[ti]
                                for kc in range(16):
                                    mm(bk[:, c0:c0 + 256], mixT[:, kc, t * 128:(t + 1) * 128], wv[:, kc, :], kc == 0, kc == 15,
                                       wt + [T("mixT", kc)], [bt])
                                if c0 == 256:
                                    sl = ZSLOT[t]
                                    zq = zt[sl][:, q * 512:(q + 1) * 512]
                                    stt(zq, zq, ALPHA, bk[:, :], ALU.mult, ALU.add, [T("zt", sl, q), bt], [T("zt", sl, q)])
                                    P.add("dve", lambda hh, o=bnst[:, sl, q, :], i=zq: hh.bn_stats(o, i), [T("zt", sl, q)], [T("bnst", sl)])
                                    held_banks.discard(bidx[ti])
                                yield

                def ln_A(t):
                    sl = ZSLOT[t]
                    mv, mvt = next_small(2)
                    P.add("dve", lambda hh, o=mv, i=bnst[:, sl].rearrange("p a b -> p (a b)"): hh.bn_aggr(o, i), [T("bnst", sl)], [mvt])
                    ln_, lnt = next_small()
                    act(ln_, mv[:, 1:2], AF.Ln, [mvt], [lnt], bias=LN_EPS)
                    rs, rst = next_small()
                    act(rs, ln_, AF.Exp, [lnt], [rst], scale=-0.5)
                    nm, nmt = next_small()
                    stt(nm, mv[:, 0:1], -1.0, rs, ALU.mult, ALU.mult, [mvt, rst], [nmt])
                    ztl = [T("zt", sl, q) for q in range(4)]
                    act(zt[sl], zt[sl], AF.Identity, ztl + [rst, nmt], ztl, bias=nm, scale=rs)

                def ln_B(t):
                    sl = ZSLOT[t]
                    half = t // 4
                    for q in range(4):
                        bk, bt = next_bank()
                        for j in range(4):
                            dc = q * 4 + j
                            tr(bk[:, j * 128:(j + 1) * 128], zt[sl][:, dc * 128:(dc + 1) * 128], ident_f[:], [T("zt", sl, q)] + CONST, [bt])
                        for j in range(4):
                            dc = q * 4 + j
                            if q % 2 == 0:
                                act(acc[:, dc, t * 128:(t + 1) * 128], bk[:, j * 128:(j + 1) * 128], AF.Identity, [bt] + CONST,
                                    [T("acc", dc, half)], bias=lnpa[:, 16 + dc:17 + dc], scale=lnpa[:, dc:dc + 1])
                            else:
                                ts("dve", acc[:, dc, t * 128:(t + 1) * 128], bk[:, j * 128:(j + 1) * 128], lnpa[:, dc:dc + 1],
                                   lnpa[:, 16 + dc:17 + dc], ALU.mult, ALU.add, [bt] + CONST, [T("acc", dc, half)])

                def ln_stage(half):
                    for t in [half * 4 + i for i in range(4)]:
                        ln_A(t)
                        ln_B(t)
                        yield

                def ln_B_stage(half):
                    for t in [half * 4 + i for i in range(4)]:
                        ln_B(t)
                        yield

                def x1t_conv(half):
                    for dc in range(16):
                        if dc % 2:
                            act(mixT[:, dc, half * 512:(half + 1) * 512], acc[:, dc, half * 512:(half + 1) * 512], AF.Copy,
                                [T("acc", dc, half)], [T("x1T", dc, half), T("mixT", dc)], scale=1.0 / ALPHA)
                        else:
                            ts("dve", mixT[:, dc, half * 512:(half + 1) * 512], acc[:, dc, half * 512:(half + 1) * 512], 1.0 / ALPHA, None,
                               ALU.mult, ALU.bypass, [T("acc", dc, half)], [T("x1T", dc, half), T("mixT", dc)])
                        if dc % 4 == 3:
                            yield

                run_interleaved(op_mm_stage(0, [0, 1, 2, 3]), None)
                load_xres(4)
                load_xres(5)
                for t in range(4):
                    ln_A(t)
                run_interleaved(ln_B_stage(0), op_mm_stage(1, [4, 5]))
                def chain(*gens):
                    for g_ in gens:
                        yield from g_

                P.label = f"{int(full)}{pi}:mlp"
                ln_A(4)
                run_interleaved(x1t_conv(0), None)
                for t in (5, 6, 7):
                    ln_A(t)
                run_interleaved(ln_B_stage(1), mlp_up_gen(0, (0,)))
                run_interleaved(x1t_conv(1), None)
                run_interleaved(mlp_up_gen(0, (1,)), None)

                P.label = f"{int(full)}{pi}:mlp"
                def mlp_down(s):
                    hb = s % 2
                    for q in range(4):
                        wv, wt = load_w(w_down_v[:, s * 8:(s + 1) * 8, q * 512:(q + 1) * 512], 8, 512)
                        for dcl in range(4):
                            dc = q * 4 + dcl
                            for half in range(2):
                                bk, bt = next_bank()
                                for fc in range(8):
                                    mm(bk[:, :], wv[:, fc, dcl * 128:(dcl + 1) * 128], hdn[hb][:, fc, half * 512:(half + 1) * 512],
                                       fc == 0, fc == 7, wt + [T("hdn", hb, fc, half)], [bt])
                                tt("dve", acc[:, dc, half * 512:(half + 1) * 512], acc[:, dc, half * 512:(half + 1) * 512], bk[:, :], ALU.add,
                                   [bt, T("acc", dc, half)], [T("acc", dc, half)])

                for s in range(8):
                    if s + 1 < 8:
                        mlp_up(s + 1)
                    elif nxt is not None:
                        issue_x_loads(*nxt)
                    mlp_down(s)

                P.label = f"{int(full)}{pi}:epi"
                P.barrier()
                dma("sp", "bc2", g2bc, ln2row[0:1, :].partition_broadcast(128), [], [T("g2bc")])
                dma("sp", "bc2", b2bc, ln2row[1:2, :].partition_broadcast(128), [], [T("g2bc")])
                for t in range(8):
                    half = t // 4
                    zb = t % 2
                    z2 = ztmp2[zb]
                    for q in range(4):
                        bk, bt = next_bank()
                        for j in range(4):
                            dc = q * 4 + j
                            tr(bk[:, j * 128:(j + 1) * 128], acc[:, dc, t * 128:(t + 1) * 128], ident_f[:], [T("acc", dc, half)] + CONST, [bt])
                        cp("act", z2[:, q * 512:(q + 1) * 512], bk[:, :], [bt], [T("z2", zb, q)])
                        P.add("dve", lambda hh, o=bnst[:, zb, q, :], i=z2[:, q * 512:(q + 1) * 512]: hh.bn_stats(o, i),
                              [T("z2", zb, q)], [T("bnst", zb)])
                    mv, mvt = next_small(2)
                    P.add("dve", lambda hh, o=mv, i=bnst[:, zb].rearrange("p a b -> p (a b)"): hh.bn_aggr(o, i), [T("bnst", zb)], [mvt])
                    ln_, lnt = next_small()
                    act(ln_, mv[:, 1:2], AF.Ln, [mvt], [lnt], bias=LN_EPS)
                    rs, rst = next_small()
                    act(rs, ln_, AF.Exp, [lnt], [rst], scale=-0.5)
                    nm, nmt = next_small()
                    stt(nm, mv[:, 0:1], -1.0, rs, ALU.mult, ALU.mult, [mvt, rst], [nmt])
                    z2l = [T("z2", zb, q) for q in range(4)]
                    act(z2, z2, AF.Identity, z2l + [rst, nmt], z2l, bias=nm, scale=rs)
                    os_ = t % 2
                    tt("dve", z2, z2, g2bc, ALU.mult, z2l + [T("g2bc")], z2l)
                    tt("pool", ostage[os_], z2, b2bc, ALU.add, z2l + [T("g2bc")], [T("ost", os_)])
                    row0 = tok0 + t * 128
                    dma("sp", f"ost{os_}", y[row0:row0 + 128, :], ostage[os_], [T("ost", os_)], [T("ystore", os_)])

            issue_x_loads(x_prev, 0)
            do_pass(x_prev, 0, False, False, (x_prev, 1), False)
            do_pass(x_prev, 1, False, False, (x_own, 0), False)
            do_pass(x_own, 0, True, True, (x_own, 1), False)
            do_pass(x_own, 1, True, False, None, True)
            P.add("sp", lambda h: h.nop(), [T("ystore", 0), T("ystore", 1)], [])
            return wlog, xlog

        plan, xpl = record(Prog(), None, None)
        P = Prog()
        record(P, plan, xpl)
        nc._pe_labels = [op.label for op in P.ops["pe"] if op.fn is not None]
        P.emit(nc, block, sems_eng, sems_stream)
    return nc


def host_layout(x, w_in, w_gk2, b_gk, gla_norm_w, swa_sinks, w_out, ln1_g, ln1_b, w_up, w_down, ln2_g, ln2_b):
    f = np.float32
    w = np.asarray(w_in[0], f)
    qg, kg, vg, gg = w[:, 0:512], w[:, 512:1024], w[:, 1024:2048], w[:, 2048:3072]
    gk = w[:, 3072:3088]
    qs, ks, vsw = w[:, 3088:4112], w[:, 4112:4240], w[:, 4240:4368]
    cols = [ks[:, 0:64], ks[:, 0:64], ks[:, 64:128], ks[:, 64:128], vsw, gk]
    for h in range(4):
        cols += [kg[:, h * 128:(h + 1) * 128], vg[:, h * 256:(h + 1) * 256], qg[:, h * 128:(h + 1) * 128], gg[:, h * 256:(h + 1) * 256]]
    cols.append(qs)
    w_in_r = np.ascontiguousarray(np.concatenate(cols, axis=1))
    assert w_in_r.shape == (D, WIN_COLS)
    sinks = np.zeros((128, 8), f)
    sk = np.asarray(swa_sinks[0], f)
    for g in range(2):
        for c in range(4):
            for p in range(2):
                sinks[p * 64:(p + 1) * 64, g * 4 + c] = sk[8 * g + 2 * c + p]
    lnp = np.concatenate([np.asarray(a[0], f).reshape(16, 128).T for a in (ln1_g, ln1_b, ln2_g, ln2_b)], axis=1)
    ln2row = np.stack([np.asarray(ln2_g[0], f), np.asarray(ln2_b[0], f)])
    j = np.arange(128)[:, None]
    i = np.arange(128)[None, :]
    causal = (j <= i).astype(f)
    cur = np.where(j <= i, 0.0, NEG).astype(f)
    prev = np.where(j > i, 0.0, NEG).astype(f)
    scanpat = np.ones((128, 512), f)
    scanpat[:, ::128] = 0.0
    common = {
        "w_in": w_in_r,
        "w_gk2": np.ascontiguousarray(np.asarray(w_gk2[0], f)),
        "b_gk": np.ascontiguousarray(np.asarray(b_gk[0], f).reshape(4, 128).T),
        "normw": np.ascontiguousarray(np.asarray(gla_norm_w[0], f)[None, :]),
        "sinks": sinks,
        "w_out": np.ascontiguousarray(np.asarray(w_out[0], f)),
        "lnp": np.ascontiguousarray(lnp),
        "ln2row": np.ascontiguousarray(ln2row),
        "w_up": np.ascontiguousarray(np.asarray(w_up[0], f)),
        "w_down": np.ascontiguousarray(np.asarray(w_down[0], f)),
        "scanpat": scanpat,
        "ident": np.eye(128, dtype=f),
    }
    xs = np.asarray(x, f)
    in_maps = []
    for c in range(8):
        b, half = c // 2, c % 2
        m = dict(common)
        m["x_own"] = np.ascontiguousarray(xs[b, half * NTOK:(half + 1) * NTOK])
        m["x_prev"] = np.ascontiguousarray(xs[b, 0:NTOK]) if half == 1 else np.zeros((NTOK, D), f)
        prev0 = prev if half == 1 else np.full((128, 128), NEG, f)
        m["masks"] = np.ascontiguousarray(np.concatenate([causal, cur, prev, prev0], axis=1))
        in_maps.append(m)
    return in_maps


_NC_CACHE = {}


def kernel(x, w_in, w_gk2, b_gk, gla_norm_w, swa_sinks, w_out, ln1_g, ln1_b, w_up, w_down, ln2_g, ln2_b):
    in_maps = host_layout(x, w_in, w_gk2, b_gk, gla_norm_w, swa_sinks, w_out, ln1_g, ln1_b, w_up, w_down, ln2_g, ln2_b)
    if "nc" not in _NC_CACHE:
        _NC_CACHE["nc"] = build_program()
    res = run_bass_kernel_spmd(_NC_CACHE["nc"], in_maps, core_ids=list(range(8)))
    out = np.empty((4, 4096, D), np.float32)
    for c in range(8):
        b, half = c // 2, c % 2
        out[b, half * NTOK:(half + 1) * NTOK] = res.results[c]["y"]
    return out
```

```python
import numpy as np
import concourse.bass as bass
import concourse.mybir as mybir
from concourse.bass_utils import run_bass_kernel_spmd

F32 = mybir.dt.float32
BF16 = mybir.dt.bfloat16
AF = mybir.ActivationFunctionType
ALU = mybir.AluOpType

D = 2048
NTOK = 2048
NT = 1024
DFF = 8192
ALPHA = 2.0 ** 0.25
LN_EPS = 1e-5
RMS_EPS = 1e-5
NEG = -30000.0
WIN_COLS = 400 + 4 * 768 + 1024

ENGS = ["pe", "act", "dve", "pool", "sp"]


class Tile:
    __slots__ = ("name", "w", "r")

    def __init__(self, name):
        self.name = name
        self.w = None
        self.r = {}


class Op:
    __slots__ = ("eng", "idx", "fn", "waits", "signal", "stream", "sval", "label")


class Prog:
    def __init__(self):
        self.ops = {e: [] for e in ENGS}
        self.waited = {e: {} for e in ENGS}
        self.streams = {}
        self.tiles = {}
        self.label = ""

    def t(self, *key):
        tl = self.tiles.get(key)
        if tl is None:
            tl = Tile(key)
            self.tiles[key] = tl
        return tl

    def add(self, eng, fn, reads=(), writes=(), stream=None):
        op = Op()
        op.eng = eng
        op.fn = fn
        op.idx = len(self.ops[eng])
        op.signal = False
        op.stream = stream
        op.sval = None
        op.label = self.label
        if stream is not None:
            self.streams[stream] = self.streams.get(stream, 0) + 1
            op.sval = 16 * self.streams[stream]
        deps = []
        for tl in reads:
            if tl.w is not None:
                deps.append(tl.w)
        for tl in writes:
            if tl.w is not None:
                deps.append(tl.w)
            deps.extend(tl.r.values())
        waits = []
        wd = self.waited[eng]
        for d in deps:
            if d.stream is not None:
                key = ("s", d.stream)
                val = d.sval
            else:
                if d.eng == eng and eng == "pe":
                    continue
                key = ("e", d.eng)
                val = d.idx
            if val <= wd.get(key, -1):
                continue
            wd[key] = val
            waits.append(d)
            if d.stream is None:
                d.signal = True
        op.waits = waits
        rkey = ("s", stream) if stream is not None else ("e", eng)
        for tl in reads:
            tl.r[rkey] = op
        for tl in writes:
            tl.w = op
            tl.r = {}
        self.ops[eng].append(op)
        return op

    def barrier(self):
        bt = self.t("__barrier__", len(self.tiles))
        lasts = []
        for e in ENGS:
            if self.ops[e]:
                lasts.append(self.ops[e][-1])
        last_stream = {}
        for e in ("pool", "sp"):
            for op in self.ops[e]:
                if op.stream is not None:
                    last_stream[op.stream] = op
        for e in ENGS:
            op = Op()
            op.eng = e
            op.fn = None
            op.idx = len(self.ops[e])
            op.signal = False
            op.stream = None
            op.sval = None
            waits = []
            wd = self.waited[e]
            for d in lasts:
                if d.eng == e or d.stream is not None:
                    continue
                key = ("e", d.eng)
                if d.idx <= wd.get(key, -1):
                    continue
                wd[key] = d.idx
                d.signal = True
                waits.append(d)
            for sname, d in last_stream.items():
                key = ("s", sname)
                if d.sval <= wd.get(key, -1):
                    continue
                wd[key] = d.sval
                waits.append(d)
            op.waits = waits
            self.ops[e].append(op)
        del bt

    def emit(self, nc, block, sems_eng, sems_stream):
        for e in ENGS:
            cnt = 0
            for op in self.ops[e]:
                if op.stream is None and op.signal:
                    cnt += 1
                    op.sval = cnt

        def run(h, e):
            for op in self.ops[e]:
                for d in op.waits:
                    sem = sems_stream[d.stream] if d.stream is not None else sems_eng[d.eng]
                    h.wait_ge(sem, d.sval)
                if op.fn is None:
                    if op.signal:
                        h.nop().then_inc(sems_eng[e], 1)
                    continue
                ins = op.fn(h)
                if op.stream is not None:
                    ins.then_inc(sems_stream[op.stream], 16)
                elif op.signal:
                    ins.then_inc(sems_eng[e], 1)

        block.tensor(lambda h: run(h, "pe"))
        block.scalar(lambda h: run(h, "act"))
        block.vector(lambda h: run(h, "dve"))
        block.gpsimd(lambda h: run(h, "pool"))
        block.sync(lambda h: run(h, "sp"))


def build_program():
    nc = bass.Bass("TRN2", target_bir_lowering=False)

    def din(name, shape):
        return nc.dram_tensor(name, shape, F32, kind="ExternalInput").ap()

    x_own = din("x_own", [NTOK, D])
    x_prev = din("x_prev", [NTOK, D])
    w_in = din("w_in", [D, WIN_COLS])
    w_gk2 = din("w_gk2", [16, 512])
    b_gk = din("b_gk", [128, 4])
    normw = din("normw", [1, 256])
    sinks = din("sinks", [128, 8])
    w_out = din("w_out", [D, D])
    lnp = din("lnp", [128, 64])
    ln2row = din("ln2row", [2, D])
    w_up = din("w_up", [D, DFF])
    w_down = din("w_down", [DFF, D])
    masks = din("masks", [128, 4 * 128])
    scanpat = din("scanpat", [128, 512])
    ident_in = din("ident", [128, 128])
    y = nc.dram_tensor("y", [NTOK, D], F32, kind="ExternalOutput").ap()

    w_in_v = w_in.rearrange("(kc p) n -> p kc n", p=128)
    w_out_v = w_out.rearrange("(kc p) n -> p kc n", p=128)
    w_up_v = w_up.rearrange("(kc p) n -> p kc n", p=128)
    w_down_v = w_down.rearrange("(fc p) n -> p fc n", p=128)

    import contextlib
    es = contextlib.ExitStack()
    with es:
        def sb(name, shape, dtype):
            return es.enter_context(nc.sbuf_tensor(name, shape, dtype))

        Wr = [sb(f"wring{i}", [128, 8192], BF16) for i in range(2)]
        RA = sb("regA", [128, 32768], BF16)
        RB = sb("regB", [128, 16384], BF16)
        RC = sb("regC", [128, 24576], BF16)
        tmpr_t = sb("tmpr_t", [128, 2, 512], F32)
        ident_b = sb("ident_b", [128, 128], BF16)
        ident_f = sb("ident_f", [128, 128], F32)
        ones_b = sb("ones_b", [128, 64], BF16)
        masks_b = sb("masks_b", [128, 4, 128], BF16)
        pat_f = sb("pat_f", [128, 512], F32)
        wgk2_b = sb("wgk2_b", [16, 512], BF16)
        negb = sb("negb", [128, 4], F32)
        normw_bc = sb("normw_bc", [128, 256], F32)
        sinkexp = sb("sinkexp", [128, 8], F32)
        lnp_s = sb("lnp_s", [128, 64], F32)
        lnpa = sb("lnpa", [128, 32], F32)
        ksT = sb("ksT", [128, 2, 1152], BF16)
        vs = sb("vs", [128, 9, 128], BF16)
        gkloT = sb("gkloT", [16, NT], BF16)
        S_f = sb("S_f", [128, 4, 256], F32)
        S_b = sb("S_b", [128, 4, 256], BF16)
        negcC = sb("negcC", [128, 4, 8], F32)
        eC = sb("eC", [128, 4, 8], F32)
        junk = sb("junk", [128, 256], BF16)
        small = sb("small", [128, 64], F32)
        bnst = sb("bnst", [128, 6, 4, 6], F32)
        ps = [es.enter_context(nc.psum_tensor(f"ps{i}", [128, 512], F32)) for i in range(8)]
        sems_eng = {e: es.enter_context(nc.semaphore(f"sem_{e}")) for e in ENGS}
        stream_names = ["w0", "w1", "w2", "w3", "xtok0", "xtok1", "xtok2", "xtok3", "xtok4", "xtok5", "xtok6", "xtok7", "z0", "z1", "z2", "z3", "z4", "z5", "ost0", "ost1", "constp", "consts", "bc2"]
        sems_stream = {s: es.enter_context(nc.semaphore(f"sem_{s}")) for s in stream_names}
        block = es.enter_context(nc.Block())

        def f32v(reg, off, n):
            return reg[:, off:off + 2 * n].bitcast(F32)

        c_f = f32v(RA, 0, 4096).rearrange("p (h t) -> p h t", h=4)
        tmpf = [f32v(RA, 8192 + i * 1024, 512) for i in range(4)]
        ktT = [RA[:, 12288 + i * 1024: 12288 + (i + 1) * 1024] for i in range(2)]
        qtT = [RA[:, 14336 + i * 1024: 14336 + (i + 1) * 1024] for i in range(2)]
        kdT = [RA[:, 16384 + i * 512: 16384 + (i + 1) * 512] for i in range(2)]
        kd_tok = [RA[:, 17408 + i * 1024: 17408 + (i + 1) * 1024].rearrange("p (t d) -> p t d", d=128) for i in range(2)]
        v_tok = [RA[:, 19456 + i * 2048: 19456 + (i + 1) * 2048].rearrange("p (t e) -> p t e", e=256) for i in range(2)]
        gw = [f32v(RA, 23552 + i * 4096, 2048).rearrange("p (t e) -> p t e", e=256) for i in range(2)]
        At = [RA[:, 31744 + i * 128: 31744 + (i + 1) * 128] for i in range(8)]
        At4 = [RA[:, 31744 + i * 512: 31744 + (i + 1) * 512] for i in range(2)]
        qsT = [RA[:, 8192 + i * 4096: 8192 + (i + 1) * 4096].rearrange("p (c t) -> p c t", c=4) for i in range(2)]
        PT = [RA[:, 16384 + i * 512: 16384 + (i + 1) * 512] for i in range(8)]
        swtmp = [f32v(RA, 20480 + i * 1024, 512) for i in range(2)]
        xT = RC[:, 0:16384].rearrange("p (k t) -> p k t", k=16)
        xtok = [RB[:, i * 2048:(i + 1) * 2048] for i in range(8)]
        gla_h = [RC[:, 20480 + i * 2048: 20480 + (i + 1) * 2048].rearrange("p (t e) -> p t e", e=256) for i in range(2)]
        zt = [f32v(RC, i * 4096, 2048) for i in range(6)]
        hdn = [RC[:, i * 8192:(i + 1) * 8192].rearrange("p (f t) -> p f t", f=8) for i in range(2)]
        tmpr = [tmpr_t[:, i, :] for i in range(2)]
        g2bc = f32v(RC, 0, 2048)
        b2bc = f32v(RC, 4096, 2048)
        ztmp2 = [f32v(RC, 8192, 2048), f32v(RC, 20480, 2048)]
        ostage = [f32v(RC, 12288 + i * 4096, 2048) for i in range(2)]
        mixT = RB[:, :].rearrange("p (k t) -> p k t", k=16)
        acc = f32v(RA, 0, 16384).rearrange("p (k t) -> p k t", k=16)

        def record(P, wplan, xplan):
            T = P.t
            wlog = []
            xlog = []
            xissued = [0]
            bank_ctr = [0]

            held_banks = set()

            def next_bank():
                while True:
                    i = bank_ctr[0] % 8
                    bank_ctr[0] += 1
                    if i not in held_banks:
                        return ps[i], T("ps", i)

            small_ctr = [0]

            def next_small(n=1):
                i = small_ctr[0] % (64 // n)
                small_ctr[0] += 1
                return small[:, i * n:(i + 1) * n], T("small", n, i)

            def mm(out, lhsT, rhs, start, stop, reads, writes):
                P.add("pe", lambda h: h.matmul(out, lhsT=lhsT, rhs=rhs, start=start, stop=stop), reads, writes)

            def tr(out, in_, ident, reads, writes):
                P.add("pe", lambda h: h.transpose(out, in_, ident), reads, writes)

            def act(out, in_, func, reads, writes, bias=None, scale=None, accum_out=None):
                kw = {}
                if bias is not None:
                    kw["bias"] = bias
                if scale is not None:
                    kw["scale"] = scale
                if accum_out is not None:
                    kw["accum_out"] = accum_out
                P.add("act", lambda h: h.activation(out, in_, func, **kw), reads, writes)

            def tt(eng, out, in0, in1, op, reads, writes):
                P.add(eng, lambda h: h.tensor_tensor(out, in0, in1, op), reads, writes)

            def ts(eng, out, in0, s1, s2, op0, op1, reads, writes):
                P.add(eng, lambda h: h.tensor_scalar(out, in0, s1, s2, op0, op1), reads, writes)

            def stt(out, in0, scalar, in1, op0, op1, reads, writes):
                P.add("dve", lambda h: h.scalar_tensor_tensor(out, in0, scalar, in1, op0, op1), reads, writes)

            def cp(eng, out, in_, reads, writes):
                if eng == "act":
                    P.add("act", lambda h: h.copy(out, in_), reads, writes)
                else:
                    P.add(eng, lambda h: h.tensor_copy(out, in_), reads, writes)

            def dma(q, stream, out, in_, reads, writes):
                P.add(q, lambda h: h.dma_start(out=out, in_=in_), reads, writes, stream=stream)

            wslot_ctr = [0]

            wassign = []
            wptr = [0]
            wowner = [-1, -1, -1, -1]
            wissued = [0]

            def assign_w(j, req):
                while len(wassign) <= j:
                    jj = len(wassign)
                    _, nk, ncols = (wlog[jj] if wplan is None else wplan[jj])
                    if nk * ncols <= 4096:
                        hs = [wptr[0] % 4]
                        wptr[0] += 1
                    else:
                        if wptr[0] % 2:
                            wptr[0] += 1
                        hs = [wptr[0] % 4, wptr[0] % 4 + 1]
                        wptr[0] += 2
                    wassign.append(hs)
                return wassign[j]

            def w_view(hs, nk, ncols):
                base = (hs[0] % 2) * 4096
                return Wr[hs[0] // 2][:, base:base + nk * ncols].rearrange("p (k n) -> p k n", n=ncols)

            def issue_w(j, req):
                src_ap, nk, ncols = req
                hs = assign_w(j, req)
                dma("pool", f"w{hs[0]}", w_view(hs, nk, ncols), src_ap, [], [T("wh", h_) for h_ in hs])
                for h_ in hs:
                    wowner[h_] = j

            def load_w(src_ap, nk, ncols):
                i = len(wlog)
                wlog.append((src_ap, nk, ncols))
                if wplan is None:
                    issue_w(i, wlog[i])
                    wissued[0] = i + 1
                else:
                    while wissued[0] < len(wplan) and wissued[0] <= i + 3:
                        j = wissued[0]
                        hs = assign_w(j, wplan[j])
                        if j > i and any(wowner[h_] >= i for h_ in hs):
                            break
                        issue_w(j, wplan[j])
                        wissued[0] += 1
                hs = assign_w(i, wlog[i])
                return w_view(hs, nk, ncols), [T("wh", h_) for h_ in hs]

            tc = T("const")
            tcp = T("constp")
            dma("pool", "constp", ident_b[:], ident_in, [], [tcp])
            dma("sp", "consts", ident_f[:], ident_in, [], [tc])
            dma("pool", "constp", masks_b[:].rearrange("p a b -> p (a b)"), masks, [], [tcp])
            dma("sp", "consts", pat_f[:], scanpat, [], [tc])
            dma("pool", "constp", wgk2_b[:], w_gk2, [], [tcp])
            dma("sp", "consts", negb[:], b_gk, [], [tc])
            dma("sp", "consts", normw_bc[:], normw.partition_broadcast(128), [], [tc])
            dma("sp", "consts", sinkexp[:], sinks, [], [tc])
            dma("sp", "consts", lnp_s[:], lnp, [], [tc])
            tc2 = T("const2")
            ts("dve", negb[:], negb[:], -1.0, None, ALU.mult, ALU.bypass, [tc, tcp], [tc2])
            P.add("dve", lambda h: h.memset(ones_b[:], 1.0), [tc2], [tc2])
            P.add("dve", lambda h: h.memset(S_f[:].rearrange("p a b -> p (a b)"), 0.0), [tc2], [T("S", hh) for hh in range(4)])
            P.add("dve", lambda h: h.memset(S_b[:].rearrange("p a b -> p (a b)"), 0.0), [tc2], [T("Sb", hh) for hh in range(4)])
            P.add("dve", lambda h: h.memset(ksT[:].rearrange("p a b -> p (a b)"), 0.0), [tc2], [T("ksT", g, "carry") for g in range(2)])
            P.add("dve", lambda h: h.memset(vs[:].rearrange("p a b -> p (a b)"), 0.0), [tc2], [T("vs", 0)])
            ts("dve", lnpa[:], lnp_s[:, 0:32], ALPHA, None, ALU.mult, ALU.bypass, [tc], [tc2])
            act(sinkexp[:], sinkexp[:], AF.Exp, [tc], [tc2])
            CONST = [tc, tcp, tc2]

            def issue_x_loads(xsrc, pi):
                for t in range(8):
                    rows = xsrc[pi * NT + t * 128: pi * NT + (t + 1) * 128, :]
                    alias = [T("mixT", 2 * t), T("mixT", 2 * t + 1)] + [T("x1T", 2 * t + a, hf) for a in range(2) for hf in range(2)]
                    dma("pool", f"xtok{t}", xtok[t].rearrange("p (a b) -> p a b", a=2), rows.rearrange("p (a b) -> p a b", a=2), [], alias)

            ZSLOT = [0, 1, 2, 3, 4, 5, 2, 3]

            def do_pass(xsrc, pi, full, first_own, nxt, after_full):
                if after_full:
                    P.barrier()
                tok0 = pi * NT
                P.label = f"{int(full)}{pi}:xT"
                def x_tile(t):
                    sl = t
                    for g8 in range(2):
                        bk, bt = next_bank()
                        bkb = bk[:, :].bitcast(BF16)
                        for jx in range(8):
                            dc = g8 * 8 + jx
                            tr(bkb[:, jx * 128:(jx + 1) * 128], xtok[sl][:, dc * 128:(dc + 1) * 128], ident_b[:],
                               [T("mixT", 2 * sl), T("mixT", 2 * sl + 1)] + CONST, [bt])
                        cp("act" if g8 == 0 else "dve", xT[:, g8 * 8:(g8 + 1) * 8, t * 128:(t + 1) * 128],
                           bkb.rearrange("p (k t) -> p k t", k=8), [bt], [T("xT", g8, t)])

                for t in range(4):
                    x_tile(t)

                def xT_reads(half):
                    return [T("xT", g8, t) for g8 in range(2) for t in range(half * 4, half * 4 + 4)]

                def proj_fm(wv, wt, c0, m, half, bk, bt):
                    for kc in range(16):
                        mm(bk[0:m, :], wv[:, kc, c0:c0 + m], xT[:, kc, half * 512:(half + 1) * 512],
                           kc == 0, kc == 15, wt + xT_reads(half), [bt])

                def proj_tm(wv, wt, c0, n, t, out_ap, bt):
                    for kc in range(16):
                        mm(out_ap, xT[:, kc, t * 128:(t + 1) * 128], wv[:, kc, c0:c0 + n],
                           kc == 0, kc == 15, wt + [T("xT", 0, t), T("xT", 1, t)], [bt])

                P.label = f"{int(full)}{pi}:misc"
                wv, wt = load_w(w_in_v[:, :, 0:400], 16, 400)

                def misc_half(half):
                    if full:
                        for g in range(2):
                            bk, bt = next_bank()
                            proj_fm(wv, wt, g * 128, 128, half, bk, bt)
                            cp("act", ksT[:, g, 128 + half * 512: 128 + (half + 1) * 512], bk[:, :], [bt], [T("ksT", g, half)])
                        bk, bt = next_bank()
                        for j in range(4):
                            t = half * 4 + j
                            proj_tm(wv, wt, 256, 128, t, bk[:, j * 128:(j + 1) * 128], bt)
                        cp("dve", vs[:, 1 + half * 4: 5 + half * 4, :], bk[:, :].rearrange("p (t e) -> p t e", t=4), [bt], [T("vs", 1 + half)])
                    elif pi == 1 and half == 1:
                        for g in range(2):
                            bk, bt = next_bank()
                            for kc in range(16):
                                mm(bk[:, 0:128], wv[:, kc, g * 128:(g + 1) * 128], xT[:, kc, 896:1024], kc == 0, kc == 15,
                                   wt + [T("xT", 0, 7), T("xT", 1, 7)], [bt])
                            cp("act", ksT[:, g, 1024:1152], bk[:, 0:128], [bt], [T("ksT", g, 1)])
                        bk, bt = next_bank()
                        proj_tm(wv, wt, 256, 128, 7, bk[:, 0:128], bt)
                        cp("dve", vs[:, 8, :], bk[:, 0:128], [bt], [T("vs", 2)])
                    bk, bt = next_bank()
                    proj_fm(wv, wt, 384, 16, half, bk, bt)
                    cp("act", gkloT[:, half * 512:(half + 1) * 512], bk[0:16, :], [bt], [T("gklo", half)])
                    for h in range(4):
                        bk, bt = next_bank()
                        mm(bk[:, :], wgk2_b[:, h * 128:(h + 1) * 128], gkloT[:, half * 512:(half + 1) * 512], True, True,
                           [T("gklo", half)] + CONST, [bt])
                        tf = (h * 2 + half) % 4
                        act(tmpf[tf], bk[:, :], AF.Exp, [bt] + CONST, [T("tmpf", tf)], bias=negb[:, h:h + 1], scale=-1.0)
                        act(tmpf[tf], tmpf[tf], AF.Ln, [T("tmpf", tf)], [T("tmpf", tf)], bias=1.0)
                        P.add("dve", lambda hh, o=c_f[:, h, half * 512:(half + 1) * 512], d0=pat_f[:, :], d1=tmpf[tf]:
                              hh.tensor_tensor_scan(o, d0, d1, 0.0, ALU.mult, ALU.add), [T("tmpf", tf)] + CONST, [T("c", h, half)])

                misc_half(0)
                P.label = f"{int(full)}{pi}:xT"
                for t in range(4, 8):
                    x_tile(t)
                P.label = f"{int(full)}{pi}:misc"
                misc_half(1)
                call = [T("c", h, half) for h in range(4) for half in range(2)]
                tcc = T("cC")
                ts("dve", negcC[:].rearrange("p h c -> p (h c)"),
                   c_f.rearrange("p h (c t) -> p (h c) t", t=128)[:, :, 127], -1.0 / 16.0, None, ALU.mult, ALU.bypass, call, [tcc])
                act(eC[:].rearrange("p h c -> p (h c)"), negcC[:].rearrange("p h c -> p (h c)"), AF.Exp, [tcc], [T("eC")])

                if not full and nxt is not None:
                    issue_x_loads(*nxt)
                P.label = f"{int(full)}{pi}:gla"
                def gla_proj(h):
                    hb = h % 2
                    base = 400 + h * 768

                    def kd_transposes(half):
                        bk2, bt2 = next_bank()
                        bkb = bk2[:, :].bitcast(BF16)
                        for j in range(4):
                            tr(bkb[:, j * 128:(j + 1) * 128], kdT[half][:, j * 128:(j + 1) * 128], ident_b[:],
                               [T("kdT", half)] + CONST, [bt2])
                        cp("act", kd_tok[hb][:, half * 4:(half + 1) * 4, :], bkb[:, 0:512].rearrange("p (t d) -> p t d", t=4),
                           [bt2], [T("kd_tok", hb, half)])

                    wv, wt = load_w(w_in_v[:, :, base:base + 384], 16, 384)

                    def k_step(half):
                        bk, bt = next_bank()
                        proj_fm(wv, wt, 0, 128, half, bk, bt)
                        if full:
                            tf = half
                            act(tmpf[tf], c_f[:, h, half * 512:(half + 1) * 512], AF.Exp, [T("c", h, half)], [T("tmpf", tf)], scale=1.0 / 16.0)
                            tt("dve", ktT[hb][:, half * 512:(half + 1) * 512], bk[:, :], tmpf[tf], ALU.mult,
                               [bt, T("tmpf", tf)], [T("ktT", hb, half)])
                        tf = 2 + half
                        for j in range(4):
                            cj = half * 4 + j
                            act(tmpf[tf][:, j * 128:(j + 1) * 128], c_f[:, h, cj * 128:(cj + 1) * 128], AF.Exp,
                                [T("c", h, half), tcc], [T("tmpf", tf)], bias=negcC[:, h, cj:cj + 1], scale=1.0 / 16.0)
                        tt("dve", kdT[half], bk[:, :], tmpf[tf], ALU.mult, [bt, T("tmpf", tf)], [T("kdT", half)])

                    def v_step(tq):
                        bk, bt = next_bank()
                        for j in range(2):
                            t = tq * 2 + j
                            proj_tm(wv, wt, 128, 256, t, bk[:, j * 256:(j + 1) * 256], bt)
                        cp("dve" if tq % 2 else "act", v_tok[hb][:, tq * 2:tq * 2 + 2, :], bk[:, :].rearrange("p (t e) -> p t e", t=2),
                           [bt], [T("v_tok", hb, tq)])

                    if h == 0:
                        for tq in range(4):
                            v_step(tq)
                            yield
                        k_step(0)
                        yield
                        k_step(1)
                        kd_transposes(0)
                        yield
                        kd_transposes(1)
                        yield
                    else:
                        k_step(0)
                        yield
                        k_step(1)
                        kd_transposes(0)
                        yield
                        for tq in range(4):
                            v_step(tq)
                            if tq == 0:
                                kd_transposes(1)
                            yield
                    if full:
                        wv2, wt2 = load_w(w_in_v[:, :, base + 384:base + 768], 16, 384)
                        for half in range(2):
                            bk, bt = next_bank()
                            proj_fm(wv2, wt2, 0, 128, half, bk, bt)
                            tf = half
                            act(tmpf[tf], c_f[:, h, half * 512:(half + 1) * 512], AF.Exp, [T("c", h, half)], [T("tmpf", tf)], scale=-1.0 / 16.0)
                            stt(qtT[hb][:, half * 512:(half + 1) * 512], bk[:, :], 128.0 ** -0.5, tmpf[tf], ALU.mult, ALU.mult,
                                [bt, T("tmpf", tf)], [T("qtT", hb, half)])
                            yield
                        for tq in range(4):
                            bk, bt = next_bank()
                            for j in range(2):
                                t = tq * 2 + j
                                proj_tm(wv2, wt2, 128, 256, t, bk[:, j * 256:(j + 1) * 256], bt)
                            tf = 2 + tq % 2
                            act(tmpf[tf], bk[:, :], AF.Silu, [bt], [T("tmpf", tf)])
                            tt("dve", gw[hb][:, tq * 2:tq * 2 + 2, :], tmpf[tf].rearrange("p (t e) -> p t e", t=2),
                               normw_bc[:].unsqueeze(1).to_broadcast([128, 2, 256]), ALU.mult,
                               [T("tmpf", tf)] + CONST, [T("gw", hb, tq)])
                            yield

                def gla_chunks(h):
                    hb = h % 2
                    if full:
                        for half in range(2):
                            bk, bt = next_bank()
                            for j in range(4):
                                t = half * 4 + j
                                mm(bk[:, j * 128:(j + 1) * 128], ktT[hb][:, t * 128:(t + 1) * 128], qtT[hb][:, t * 128:(t + 1) * 128],
                                   True, True, [T("ktT", hb, half), T("qtT", hb, half)], [bt])
                            tt("dve", At4[half].rearrange("p (c t) -> p c t", c=4), bk[:, :].rearrange("p (c t) -> p c t", c=4),
                               masks_b[:, 0, :].unsqueeze(1).to_broadcast([128, 4, 128]), ALU.mult, [bt] + CONST, [T("At", half)])
                        yield
                    for t in range(8):
                        half = t // 4
                        tq = t // 2
                        if full:
                            a = t
                            bo, bot = next_bank()
                            mm(bo[:, 0:256], At[a], v_tok[hb][:, t, :], True, False, [T("At", half), T("v_tok", hb, tq)], [bot])
                            mm(bo[:, 0:256], qtT[hb][:, t * 128:(t + 1) * 128], S_b[:, h, :], False, True,
                               [T("qtT", hb, half), T("Sb", h)], [bot])
                            ss, sst = next_small()
                            act(junk[:], bo[:, 0:256], AF.Square, [bot], [T("junk"), sst], accum_out=ss)
                            ln_, lnt = next_small()
                            act(ln_, ss, AF.Ln, [sst], [lnt], bias=RMS_EPS, scale=1.0 / 256.0)
                            rs, rst = next_small()
                            act(rs, ln_, AF.Exp, [lnt], [rst], scale=-0.5)
                            stt(gla_h[hb][:, t, :], bo[:, 0:256], rs, gw[hb][:, t, :], ALU.mult, ALU.mult,
                                [bot, rst, T("gw", hb, tq)], [T("gla_h", hb, t)])
                        bu, but = next_bank()
                        mm(bu[:, 0:256], kd_tok[hb][:, t, :], v_tok[hb][:, t, :], True, True,
                           [T("kd_tok", hb, half), T("v_tok", hb, tq)], [but])
                        stt(S_f[:, h, :], S_f[:, h, :], eC[:, h, t:t + 1], bu[:, 0:256], ALU.mult, ALU.add,
                            [T("S", h), T("eC"), but], [T("S", h)])
                        cp("act", S_b[:, h, :], S_f[:, h, :], [T("S", h)], [T("Sb", h)])
                        yield
                    if full:
                        for ec in range(2):
                            bk, bt = next_bank()
                            bkb = bk[:, :].bitcast(BF16)
                            for t in range(8):
                                tr(bkb[:, t * 128:(t + 1) * 128], gla_h[hb][:, t, ec * 128:(ec + 1) * 128], ident_b[:],
                                   [T("gla_h", hb, t)] + CONST, [bt])
                            cp("act" if ec else "dve", mixT[:, 2 * h + ec, :], bkb, [bt], [T("mixT", 2 * h + ec)])
                        yield

                def swa_proj(g, alias_tmpf):
                    gb = g % 2
                    base = 400 + 4 * 768 + g * 512
                    wv, wt = load_w(w_in_v[:, :, base:base + 512], 16, 512)
                    for c in range(4):
                        for half in range(2):
                            bk, bt = next_bank()
                            proj_fm(wv, wt, c * 128, 128, half, bk, bt)
                            extra = [T("tmpf", c)] if alias_tmpf else []
                            cp("act" if half else "dve", qsT[gb][:, c, half * 512:(half + 1) * 512], bk[:, :], [bt],
                               [T("qsT", gb, c, half)] + extra)
                            yield

                def swa_blocks(g):
                    gb = g % 2
                    pts_all = {}

                    def st1(b):
                        half = b // 4
                        pts = {}
                        for p in range(2):
                            for kb in range(2):
                                bk, bt = next_bank()
                                kcol = (b + kb) * 128
                                kread = [T("ksT", g, "carry")] if kcol < 128 else [T("ksT", g, (kcol - 128) // 512)]
                                mm(bk[:, :], ksT[p * 64:(p + 1) * 64, g, kcol:kcol + 128],
                                   qsT[gb][p * 64:(p + 1) * 64, :, b * 128:(b + 1) * 128], True, True,
                                   kread + [T("qsT", gb, c, half) for c in range(4)], [bt])
                                mi = 1 if kb == 1 else (3 if (first_own and b == 0) else 2)
                                pi_ = (b % 2) * 4 + p * 2 + kb
                                act(PT[pi_], bk[:, :], AF.Exp, [bt], [T("PT", pi_)], scale=0.125)
                                tt("pool", PT[pi_].rearrange("p (c t) -> p c t", c=4), PT[pi_].rearrange("p (c t) -> p c t", c=4),
                                   masks_b[:, mi, :].unsqueeze(1).to_broadcast([128, 4, 128]), ALU.mult,
                                   [T("PT", pi_)] + CONST, [T("PT", pi_)])
                                pts[(p, kb)] = pi_
                        pts_all[b] = pts

                    def st2(b):
                        pts = pts_all[b]
                        bn_, bnt = next_bank()
                        bd_, bdt = next_bank()
                        for p in range(2):
                            for kb in range(2):
                                vblk = b + kb
                                vread = [T("vs", 0)] if vblk == 0 else [T("vs", 1 + (vblk - 1) // 4)]
                                mm(bn_[p * 64:(p + 1) * 64, :], vs[:, vblk, g * 64:(g + 1) * 64], PT[pts[(p, kb)]], kb == 0, kb == 1,
                                   vread + [T("PT", pts[(p, kb)])], [bnt])
                        for p in range(2):
                            for kb in range(2):
                                mm(bd_[p * 64:(p + 1) * 64, :], ones_b[:, :], PT[pts[(p, kb)]], kb == 0, kb == 1,
                                   [T("PT", pts[(p, kb)])] + CONST, [bdt])
                        sw = b % 2
                        tt("dve", swtmp[sw].rearrange("p (c t) -> p c t", c=4), bd_[:, :].rearrange("p (c t) -> p c t", c=4),
                           sinkexp[:, g * 4:(g + 1) * 4].unsqueeze(2).to_broadcast([128, 4, 128]), ALU.add,
                           [bdt] + CONST, [T("swtmp", sw)])
                        P.add("dve", lambda hh, o=swtmp[sw]: hh.reciprocal(o, o), [T("swtmp", sw)], [T("swtmp", sw)])
                        tt("dve", mixT[:, 8 + 4 * g: 12 + 4 * g, b * 128:(b + 1) * 128],
                           bn_[:, :].rearrange("p (c t) -> p c t", c=4), swtmp[sw].rearrange("p (c t) -> p c t", c=4), ALU.mult,
                           [bnt, T("swtmp", sw)], [T("mixT", 8 + 4 * g + c) for c in range(4)])

                    st1(0)
                    yield
                    for b in range(8):
                        if b + 1 < 8:
                            st1(b + 1)
                            yield
                        st2(b)
                        yield

                def load_xres(t, alias_xT=False):
                    sl = ZSLOT[t]
                    row0 = tok0 + t * 128
                    extra = [T("xT", sl // 2, tt_) for tt_ in range(8)] if alias_xT else []
                    dma("sp", f"z{sl}", zt[sl], x_own[row0:row0 + 128, :], [], [T("zt", sl, q) for q in range(4)] + extra)

                def run_interleaved(a, b):
                    alive = [g_ for g_ in (a, b) if g_ is not None]
                    while alive:
                        for g_ in list(alive):
                            try:
                                next(g_)
                            except StopIteration:
                                alive.remove(g_)

                prev_chunks = None
                for h in range(4):
                    run_interleaved(prev_chunks, gla_proj(h))
                    prev_chunks = gla_chunks(h)
                if full:
                    run_interleaved(prev_chunks, swa_proj(0, True))
                    P.label = f"{int(full)}{pi}:swa"
                    P.barrier()
                    run_interleaved(swa_blocks(0), swa_proj(1, False))
                    for t in range(4):
                        load_xres(t, alias_xT=True)
                    run_interleaved(swa_blocks(1), None)
                else:
                    run_interleaved(prev_chunks, None)
                if full or pi == 1:
                    for g in range(2):
                        cp("dve", ksT[:, g, 0:128], ksT[:, g, 1024:1152], [T("ksT", g, 1)], [T("ksT", g, "carry")])
                    cp("dve", vs[:, 0, :], vs[:, 8, :], [T("vs", 2)], [T("vs", 0)])
                if not full:
                    return

                P.label = f"{int(full)}{pi}:outproj"
                def hdn_alias(hb, fc):
                    return [T("zt", hb * 2 + fc // 4, q) for q in range(4)]

                def mlp_up_gen(s, halves):
                    hb = s % 2
                    for u in range(4):
                        c0 = s * 1024 + u * 256
                        wv, wt = load_w(w_up_v[:, :, c0:c0 + 256], 16, 256)
                        for fcl in range(2):
                            fc = u * 2 + fcl
                            for half in halves:
                                bk, bt = next_bank()
                                for kc in range(16):
                                    mm(bk[:, :], wv[:, kc, fcl * 128:(fcl + 1) * 128], mixT[:, kc, half * 512:(half + 1) * 512],
                                       kc == 0, kc == 15, wt + [T("x1T", kc, half)], [bt])
                                tf = half
                                act(tmpr[tf], bk[:, :], AF.Relu, [bt], [T("tmpr", tf)])
                                tt("dve", hdn[hb][:, fc, half * 512:(half + 1) * 512], tmpr[tf], tmpr[tf], ALU.mult,
                                   [T("tmpr", tf)], [T("hdn", hb, fc, half)] + hdn_alias(hb, fc))
                        yield

                def mlp_up(s):
                    run_interleaved(mlp_up_gen(s, (0, 1)), None)

                P.barrier()

                def op_mm_stage(half, preloaded):
                    tiles = [half * 4 + i for i in range(4)]
                    for q in range(4):
                        banks = [next_bank() for _ in tiles]
                        bidx = [bt_.name[1] for _, bt_ in banks]
                        held_banks.update(bidx)
                        for c0 in (0, 256):
                            wv, wt = load_w(w_out_v[:, :, q * 512 + c0:q * 512 + c0 + 256], 16, 256)
                            for ti, t in enumerate(tiles):
                                if q == 0 and c0 == 0 and t not in preloaded:
                                    load_xres(t)
                                bk, bt = banks[ti]
                                for kc in range(16):
                                    mm(bk[:, c0:c0 + 256], mixT[:, kc, t * 128:(t + 1) * 128], wv[:, kc, :], kc == 0, kc == 15,
                                       wt + [T("mixT", kc)], [bt])
                                if c0 == 256:
                                    sl = ZSLOT[t]
                                    zq = zt[sl][:, q * 512:(q + 1) * 512]
                                    stt(zq, zq, ALPHA, bk[:, :], ALU.mult, ALU.add, [T("zt", sl, q), bt], [T("zt", sl, q)])
                                    P.add("dve", lambda hh, o=bnst[:, sl, q, :], i=zq: hh.bn_stats(o, i), [T("zt", sl, q)], [T("bnst", sl)])
                                    held_banks.discard(bidx[ti])
                                yield

                def ln_A(t):
                    sl = ZSLOT[t]
                    mv, mvt = next_small(2)
                    P.add("dve", lambda hh, o=mv, i=bnst[:, sl].rearrange("p a b -> p (a b)"): hh.bn_aggr(o, i), [T("bnst", sl)], [mvt])
                    ln_, lnt = next_small()
                    act(ln_, mv[:, 1:2], AF.Ln, [mvt], [lnt], bias=LN_EPS)
                    rs, rst = next_small()
                    act(rs, ln_, AF.Exp, [lnt], [rst], scale=-0.5)
                    nm, nmt = next_small()
                    stt(nm, mv[:, 0:1], -1.0, rs, ALU.mult, ALU.mult, [mvt, rst], [nmt])
                    ztl = [T("zt", sl, q) for q in range(4)]
                    act(zt[sl], zt[sl], AF.Identity, ztl + [rst, nmt], ztl, bias=nm, scale=rs)

                def ln_B(t):
                    sl = ZSLOT[t]
                    half = t // 4
                    for q in range(4):
                        bk, bt = next_bank()
                        for j in range(4):
                            dc = q * 4 + j
                            tr(bk[:, j * 128:(j + 1) * 128], zt[sl][:, dc * 128:(dc + 1) * 128], ident_f[:], [T("zt", sl, q)] + CONST, [bt])
                        for j in range(4):
                            dc = q * 4 + j
                            if q % 2 == 0:
                                act(acc[:, dc, t * 128:(t + 1) * 128], bk[:, j * 128:(j + 1) * 128], AF.Identity, [bt] + CONST,
                                    [T("acc", dc, half)], bias=lnpa[:, 16 + dc:17 + dc], scale=lnpa[:, dc:dc + 1])
                            else:
                                ts("dve", acc[:, dc, t * 128:(t + 1) * 128], bk[:, j * 128:(j + 1) * 128], lnpa[:, dc:dc + 1],
                                   lnpa[:, 16 + dc:17 + dc], ALU.mult, ALU.add, [bt] + CONST, [T("acc", dc, half)])

                def ln_stage(half):
                    for t in [half * 4 + i for i in range(4)]:
                        ln_A(t)
                        ln_B(t)
                        yield

                def ln_B_stage(half):
                    for t in [half * 4 + i for i in range(4)]:
                        ln_B(t)
                        yield

                def x1t_conv(half):
                    for dc in range(16):
                        if dc % 2:
                            act(mixT[:, dc, half * 512:(half + 1) * 512], acc[:, dc, half * 512:(half + 1) * 512], AF.Copy,
                                [T("acc", dc, half)], [T("x1T", dc, half), T("mixT", dc)], scale=1.0 / ALPHA)
                        else:
                            ts("dve", mixT[:, dc, half * 512:(half + 1) * 512], acc[:, dc, half * 512:(half + 1) * 512], 1.0 / ALPHA, None,
                               ALU.mult, ALU.bypass, [T("acc", dc, half)], [T("x1T", dc, half), T("mixT", dc)])
                        if dc % 4 == 3:
                            yield

                run_interleaved(op_mm_stage(0, [0, 1, 2, 3]), None)
                load_xres(4)
                load_xres(5)
                for t in range(4):
                    ln_A(t)
                run_interleaved(ln_B_stage(0), op_mm_stage(1, [4, 5]))
                def chain(*gens):
                    for g_ in gens:
                        yield from g_

                P.label = f"{int(full)}{pi}:mlp"
                ln_A(4)
                run_interleaved(x1t_conv(0), None)
                for t in (5, 6, 7):
                    ln_A(t)
                run_interleaved(ln_B_stage(1), mlp_up_gen(0, (0,)))
                run_interleaved(x1t_conv(1), None)
                run_interleaved(mlp_up_gen(0, (1,)), None)

                P.label = f"{int(full)}{pi}:mlp"
                def mlp_down(s):
                    hb = s % 2
                    for q in range(4):
                        wv, wt = load_w(w_down_v[:, s * 8:(s + 1) * 8, q * 512:(q + 1) * 512], 8, 512)
                        for dcl in range(4):
                            dc = q * 4 + dcl
                            for half in range(2):
                                bk, bt = next_bank()
                                for fc in range(8):
                                    mm(bk[:, :], wv[:, fc, dcl * 128:(dcl + 1) * 128], hdn[hb][:, fc, half * 512:(half + 1) * 512],
                                       fc == 0, fc == 7, wt + [T("hdn", hb, fc, half)], [bt])
                                tt("dve", acc[:, dc, half * 512:(half + 1) * 512], acc[:, dc, half * 512:(half + 1) * 512], bk[:, :], ALU.add,
                                   [bt, T("acc", dc, half)], [T("acc", dc, half)])

                for s in range(8):
                    if s + 1 < 8:
                        mlp_up(s + 1)
                    elif nxt is not None:
                        issue_x_loads(*nxt)
                    mlp_down(s)

                P.label = f"{int(full)}{pi}:epi"
                P.barrier()
                dma("sp", "bc2", g2bc, ln2row[0:1, :].partition_broadcast(128), [], [T("g2bc")])
                dma("sp", "bc2", b2bc, ln2row[1:2, :].partition_broadcast(128), [], [T("g2bc")])
                for t in range(8):
                    half = t // 4
                    zb = t % 2
                    z2 = ztmp2[zb]
                    for q in range(4):
                        bk, bt = next_bank()
                        for j in range(4):
                            dc = q * 4 + j
                            tr(bk[:, j * 128:(j + 1) * 128], acc[:, dc, t * 128:(t + 1) * 128], ident_f[:], [T("acc", dc, half)] + CONST, [bt])
                        cp("act", z2[:, q * 512:(q + 1) * 512], bk[:, :], [bt], [T("z2", zb, q)])
                        P.add("dve", lambda hh, o=bnst[:, zb, q, :], i=z2[:, q * 512:(q + 1) * 512]: hh.bn_stats(o, i),
                              [T("z2", zb, q)], [T("bnst", zb)])
                    mv, mvt = next_small(2)
                    P.add("dve", lambda hh, o=mv, i=bnst[:, zb].rearrange("p a b -> p (a b)"): hh.bn_aggr(o, i), [T("bnst", zb)], [mvt])
                    ln_, lnt = next_small()
                    act(ln_, mv[:, 1:2], AF.Ln, [mvt], [lnt], bias=LN_EPS)
                    rs, rst = next_small()
                    act(rs, ln_, AF.Exp, [lnt], [rst], scale=-0.5)
                    nm, nmt = next_small()
                    stt(nm, mv[:, 0:1], -1.0, rs, ALU.mult, ALU.mult, [mvt, rst], [nmt])
                    z2l = [T("z2", zb, q) for q in range(4)]
                    act(z2, z2, AF.Identity, z2l + [rst, nmt], z2l, bias=nm, scale=rs)
                    os_ = t % 2
                    tt("dve", z2, z2, g2bc, ALU.mult, z2l + [T("g2bc")], z2l)
                    tt("pool", ostage[os_], z2, b2bc, ALU.add, z2l + [T("g2bc")], [T("ost", os_)])
                    row0 = tok0 + t * 128
                    dma("sp", f"ost{os_}", y[row0:row0 + 128, :], ostage[os_], [T("ost", os_)], [T("ystore", os_)])

            issue_x_loads(x_prev, 0)
            do_pass(x_prev, 0, False, False, (x_prev, 1), False)
            do_pass(x_prev, 1, False, False, (x_own, 0), False)
            do_pass(x_own, 0, True, True, (x_own, 1), False)
            do_pass(x_own, 1, True, False, None, True)
            P.add("sp", lambda h: h.nop(), [T("ystore", 0), T("ystore", 1)], [])
            return wlog, xlog

        plan, xpl = record(Prog(), None, None)
        P = Prog()
        record(P, plan, xpl)
        nc._pe_labels = [op.label for op in P.ops["pe"] if op.fn is not None]
        P.emit(nc, block, sems_eng, sems_stream)
    return nc


def host_layout(x, w_in, w_gk2, b_gk, gla_norm_w, swa_sinks, w_out, ln1_g, ln1_b, w_up, w_down, ln2_g, ln2_b):
    f = np.float32
    w = np.asarray(w_in[0], f)
    qg, kg, vg, gg = w[:, 0:512], w[:, 512:1024], w[:, 1024:2048], w[:, 2048:3072]
    gk = w[:, 3072:3088]
    qs, ks, vsw = w[:, 3088:4112], w[:, 4112:4240], w[:, 4240:4368]
    cols = [ks[:, 0:64], ks[:, 0:64], ks[:, 64:128], ks[:, 64:128], vsw, gk]
    for h in range(4):
        cols += [kg[:, h * 128:(h + 1) * 128], vg[:, h * 256:(h + 1) * 256], qg[:, h * 128:(h + 1) * 128], gg[:, h * 256:(h + 1) * 256]]
    cols.append(qs)
    w_in_r = np.ascontiguousarray(np.concatenate(cols, axis=1))
    assert w_in_r.shape == (D, WIN_COLS)
    sinks = np.zeros((128, 8), f)
    sk = np.asarray(swa_sinks[0], f)
    for g in range(2):
        for c in range(4):
            for p in range(2):
                sinks[p * 64:(p + 1) * 64, g * 4 + c] = sk[8 * g + 2 * c + p]
    lnp = np.concatenate([np.asarray(a[0], f).reshape(16, 128).T for a in (ln1_g, ln1_b, ln2_g, ln2_b)], axis=1)
    ln2row = np.stack([np.asarray(ln2_g[0], f), np.asarray(ln2_b[0], f)])
    j = np.arange(128)[:, None]
    i = np.arange(128)[None, :]
    causal = (j <= i).astype(f)
    cur = (j <= i).astype(f)
    prev = (j > i).astype(f)
    scanpat = np.ones((128, 512), f)
    scanpat[:, ::128] = 0.0
    common = {
        "w_in": w_in_r,
        "w_gk2": np.ascontiguousarray(np.asarray(w_gk2[0], f)),
        "b_gk": np.ascontiguousarray(np.asarray(b_gk[0], f).reshape(4, 128).T),
        "normw": np.ascontiguousarray(np.asarray(gla_norm_w[0], f)[None, :]),
        "sinks": sinks,
        "w_out": np.ascontiguousarray(np.asarray(w_out[0], f)),
        "lnp": np.ascontiguousarray(lnp),
        "ln2row": np.ascontiguousarray(ln2row),
        "w_up": np.ascontiguousarray(np.asarray(w_up[0], f)),
        "w_down": np.ascontiguousarray(np.asarray(w_down[0], f)),
        "scanpat": scanpat,
        "ident": np.eye(128, dtype=f),
    }
    xs = np.asarray(x, f)
    in_maps = []
    for c in range(8):
        b, half = c // 2, c % 2
        m = dict(common)
        m["x_own"] = np.ascontiguousarray(xs[b, half * NTOK:(half + 1) * NTOK])
        m["x_prev"] = np.ascontiguousarray(xs[b, 0:NTOK]) if half == 1 else np.zeros((NTOK, D), f)
        prev0 = prev if half == 1 else np.zeros((128, 128), f)
        m["masks"] = np.ascontiguousarray(np.concatenate([causal, cur, prev, prev0], axis=1))
        in_maps.append(m)
    return in_maps


_NC_CACHE = {}


def kernel(x, w_in, w_gk2, b_gk, gla_norm_w, swa_sinks, w_out, ln1_g, ln1_b, w_up, w_down, ln2_g, ln2_b):
    in_maps = host_layout(x, w_in, w_gk2, b_gk, gla_norm_w, swa_sinks, w_out, ln1_g, ln1_b, w_up, w_down, ln2_g, ln2_b)
    if "nc" not in _NC_CACHE:
        _NC_CACHE["nc"] = build_program()
    res = run_bass_kernel_spmd(_NC_CACHE["nc"], in_maps, core_ids=list(range(8)))
    out = np.empty((4, 4096, D), np.float32)
    for c in range(8):
        b, half = c // 2, c % 2
        out[b, half * NTOK:(half + 1) * NTOK] = res.results[c]["y"]
    return out
```

```python
import numpy as np
import concourse.bass as bass
import concourse.mybir as mybir
from concourse.bass_utils import run_bass_kernel_spmd

F32 = mybir.dt.float32
BF16 = mybir.dt.bfloat16
AF = mybir.ActivationFunctionType
ALU = mybir.AluOpType

D = 2048
NTOK = 2048
NT = 1024
DFF = 8192
ALPHA = 2.0 ** 0.25
LN_EPS = 1e-5
RMS_EPS = 1e-5
NEG = -30000.0
WIN_COLS = 400 + 4 * 768 + 1024

ENGS = ["pe", "act", "dve", "pool", "sp"]


class Tile:
    __slots__ = ("name", "w", "r")

    def __init__(self, name):
        self.name = name
        self.w = None
        self.r = {}


class Op:
    __slots__ = ("eng", "idx", "fn", "waits", "signal", "stream", "sval", "label")


class Prog:
    def __init__(self):
        self.ops = {e: [] for e in ENGS}
        self.waited = {e: {} for e in ENGS}
        self.streams = {}
        self.tiles = {}
        self.label = ""

    def t(self, *key):
        tl = self.tiles.get(key)
        if tl is None:
            tl = Tile(key)
            self.tiles[key] = tl
        return tl

    def add(self, eng, fn, reads=(), writes=(), stream=None):
        op = Op()
        op.eng = eng
        op.fn = fn
        op.idx = len(self.ops[eng])
        op.signal = False
        op.stream = stream
        op.sval = None
        op.label = self.label
        if stream is not None:
            self.streams[stream] = self.streams.get(stream, 0) + 1
            op.sval = 16 * self.streams[stream]
        deps = []
        for tl in reads:
            if tl.w is not None:
                deps.append(tl.w)
        for tl in writes:
            if tl.w is not None:
                deps.append(tl.w)
            deps.extend(tl.r.values())
        waits = []
        wd = self.waited[eng]
        for d in deps:
            if d.stream is not None:
                key = ("s", d.stream)
                val = d.sval
            else:
                if d.eng == eng and eng == "pe":
                    continue
                key = ("e", d.eng)
                val = d.idx
            if val <= wd.get(key, -1):
                continue
            wd[key] = val
            waits.append(d)
            if d.stream is None:
                d.signal = True
        op.waits = waits
        rkey = ("s", stream) if stream is not None else ("e", eng)
        for tl in reads:
            tl.r[rkey] = op
        for tl in writes:
            tl.w = op
            tl.r = {}
        self.ops[eng].append(op)
        return op

    def barrier(self):
        bt = self.t("__barrier__", len(self.tiles))
        lasts = []
        for e in ENGS:
            if self.ops[e]:
                lasts.append(self.ops[e][-1])
        last_stream = {}
        for e in ("pool", "sp"):
            for op in self.ops[e]:
                if op.stream is not None:
                    last_stream[op.stream] = op
        for e in ENGS:
            op = Op()
            op.eng = e
            op.fn = None
            op.idx = len(self.ops[e])
            op.signal = False
            op.stream = None
            op.sval = None
            waits = []
            wd = self.waited[e]
            for d in lasts:
                if d.eng == e or d.stream is not None:
                    continue
                key = ("e", d.eng)
                if d.idx <= wd.get(key, -1):
                    continue
                wd[key] = d.idx
                d.signal = True
                waits.append(d)
            for sname, d in last_stream.items():
                key = ("s", sname)
                if d.sval <= wd.get(key, -1):
                    continue
                wd[key] = d.sval
                waits.append(d)
            op.waits = waits
            self.ops[e].append(op)
        del bt

    def emit(self, nc, block, sems_eng, sems_stream):
        for e in ENGS:
            cnt = 0
            for op in self.ops[e]:
                if op.stream is None and op.signal:
                    cnt += 1
                    op.sval = cnt

        def run(h, e):
            for op in self.ops[e]:
                for d in op.waits:
                    sem = sems_stream[d.stream] if d.stream is not None else sems_eng[d.eng]
                    h.wait_ge(sem, d.sval)
                if op.fn is None:
                    if op.signal:
                        h.nop().then_inc(sems_eng[e], 1)
                    continue
                ins = op.fn(h)
                if op.stream is not None:
                    ins.then_inc(sems_stream[op.stream], 16)
                elif op.signal:
                    ins.then_inc(sems_eng[e], 1)

        block.tensor(lambda h: run(h, "pe"))
        block.scalar(lambda h: run(h, "act"))
        block.vector(lambda h: run(h, "dve"))
        block.gpsimd(lambda h: run(h, "pool"))
        block.sync(lambda h: run(h, "sp"))


def build_program():
    nc = bass.Bass("TRN2", target_bir_lowering=False)

    def din(name, shape):
        return nc.dram_tensor(name, shape, F32, kind="ExternalInput").ap()

    x_own = din("x_own", [NTOK, D])
    x_prev = din("x_prev", [NTOK, D])
    w_in = din("w_in", [D, WIN_COLS])
    w_gk2 = din("w_gk2", [16, 512])
    b_gk = din("b_gk", [128, 4])
    normw = din("normw", [1, 256])
    sinks = din("sinks", [128, 8])
    w_out = din("w_out", [D, D])
    lnp = din("lnp", [128, 64])
    ln2row = din("ln2row", [2, D])
    w_up = din("w_up", [D, DFF])
    w_down = din("w_down", [DFF, D])
    masks = din("masks", [128, 4 * 128])
    scanpat = din("scanpat", [128, 512])
    ident_in = din("ident", [128, 128])
    y = nc.dram_tensor("y", [NTOK, D], F32, kind="ExternalOutput").ap()

    w_in_v = w_in.rearrange("(kc p) n -> p kc n", p=128)
    w_out_v = w_out.rearrange("(kc p) n -> p kc n", p=128)
    w_up_v = w_up.rearrange("(kc p) n -> p kc n", p=128)
    w_down_v = w_down.rearrange("(fc p) n -> p fc n", p=128)

    import contextlib
    es = contextlib.ExitStack()
    with es:
        def sb(name, shape, dtype):
            return es.enter_context(nc.sbuf_tensor(name, shape, dtype))

        Wr = [sb(f"wring{i}", [128, 8192], BF16) for i in range(2)]
        RA = sb("regA", [128, 32768], BF16)
        RB = sb("regB", [128, 16384], BF16)
        RC = sb("regC", [128, 24576], BF16)
        tmpr_t = sb("tmpr_t", [128, 2, 512], F32)
        ident_b = sb("ident_b", [128, 128], BF16)
        ident_f = sb("ident_f", [128, 128], F32)
        ones_b = sb("ones_b", [128, 64], BF16)
        masks_b = sb("masks_b", [128, 4, 128], BF16)
        pat_f = sb("pat_f", [128, 512], F32)
        wgk2_b = sb("wgk2_b", [16, 512], BF16)
        negb = sb("negb", [128, 4], F32)
        normw_bc = sb("normw_bc", [128, 256], F32)
        sinkexp = sb("sinkexp", [128, 8], F32)
        lnp_s = sb("lnp_s", [128, 64], F32)
        lnpa = sb("lnpa", [128, 32], F32)
        ksT = sb("ksT", [128, 2, 1152], BF16)
        vs = sb("vs", [128, 9, 128], BF16)
        gkloT = sb("gkloT", [16, NT], BF16)
        S_f = sb("S_f", [128, 4, 256], F32)
        S_b = sb("S_b", [128, 4, 256], BF16)
        negcC = sb("negcC", [128, 4, 8], F32)
        eC = sb("eC", [128, 4, 8], F32)
        junk = sb("junk", [128, 256], BF16)
        small = sb("small", [128, 64], F32)
        bnst = sb("bnst", [128, 6, 4, 6], F32)
        ps = [es.enter_context(nc.psum_tensor(f"ps{i}", [128, 512], F32)) for i in range(8)]
        sems_eng = {e: es.enter_context(nc.semaphore(f"sem_{e}")) for e in ENGS}
        stream_names = ["w0", "w1", "w2", "w3", "xtok0", "xtok1", "xtok2", "xtok3", "xtok4", "xtok5", "xtok6", "xtok7", "z0", "z1", "z2", "z3", "z4", "z5", "ost0", "ost1", "constp", "consts", "bc2"]
        sems_stream = {s: es.enter_context(nc.semaphore(f"sem_{s}")) for s in stream_names}
        block = es.enter_context(nc.Block())

        def f32v(reg, off, n):
            return reg[:, off:off + 2 * n].bitcast(F32)

        c_f = f32v(RA, 0, 4096).rearrange("p (h t) -> p h t", h=4)
        tmpf = [f32v(RA, 8192 + i * 1024, 512) for i in range(4)]
        ktT = [RA[:, 12288 + i * 1024: 12288 + (i + 1) * 1024] for i in range(2)]
        qtT = [RA[:, 14336 + i * 1024: 14336 + (i + 1) * 1024] for i in range(2)]
        kdT = [RA[:, 16384 + i * 512: 16384 + (i + 1) * 512] for i in range(2)]
        kd_tok = [RA[:, 17408 + i * 1024: 17408 + (i + 1) * 1024].rearrange("p (t d) -> p t d", d=128) for i in range(2)]
        v_tok = [RA[:, 19456 + i * 2048: 19456 + (i + 1) * 2048].rearrange("p (t e) -> p t e", e=256) for i in range(2)]
        gw = [f32v(RA, 23552 + i * 4096, 2048).rearrange("p (t e) -> p t e", e=256) for i in range(2)]
        At = [RA[:, 31744 + i * 128: 31744 + (i + 1) * 128] for i in range(8)]
        At4 = [RA[:, 31744 + i * 512: 31744 + (i + 1) * 512] for i in range(2)]
        qsT = [RA[:, 8192 + i * 4096: 8192 + (i + 1) * 4096].rearrange("p (c t) -> p c t", c=4) for i in range(2)]
        PT = [RA[:, 16384 + i * 512: 16384 + (i + 1) * 512] for i in range(8)]
        swtmp = [f32v(RA, 20480 + i * 1024, 512) for i in range(2)]
        xT = RC[:, 0:16384].rearrange("p (k t) -> p k t", k=16)
        xtok = [RB[:, i * 2048:(i + 1) * 2048] for i in range(8)]
        gla_h = [RC[:, 20480 + i * 2048: 20480 + (i + 1) * 2048].rearrange("p (t e) -> p t e", e=256) for i in range(2)]
        zt = [f32v(RC, i * 4096, 2048) for i in range(6)]
        hdn = [RC[:, i * 8192:(i + 1) * 8192].rearrange("p (f t) -> p f t", f=8) for i in range(2)]
        tmpr = [tmpr_t[:, i, :] for i in range(2)]
        g2bc = f32v(RC, 0, 2048)
        b2bc = f32v(RC, 4096, 2048)
        ztmp2 = [f32v(RC, 8192, 2048), f32v(RC, 20480, 2048)]
        ostage = [f32v(RC, 12288 + i * 4096, 2048) for i in range(2)]
        mixT = RB[:, :].rearrange("p (k t) -> p k t", k=16)
        acc = f32v(RA, 0, 16384).rearrange("p (k t) -> p k t", k=16)

        def record(P, wplan, xplan):
            T = P.t
            wlog = []
            xlog = []
            xissued = [0]
            bank_ctr = [0]

            held_banks = set()

            def next_bank():
                while True:
                    i = bank_ctr[0] % 8
                    bank_ctr[0] += 1
                    if i not in held_banks:
                        return ps[i], T("ps", i)

            small_ctr = [0]

            def next_small(n=1):
                i = small_ctr[0] % (64 // n)
                small_ctr[0] += 1
                return small[:, i * n:(i + 1) * n], T("small", n, i)

            def mm(out, lhsT, rhs, start, stop, reads, writes):
                P.add("pe", lambda h: h.matmul(out, lhsT=lhsT, rhs=rhs, start=start, stop=stop), reads, writes)

            def tr(out, in_, ident, reads, writes):
                P.add("pe", lambda h: h.transpose(out, in_, ident), reads, writes)

            def act(out, in_, func, reads, writes, bias=None, scale=None, accum_out=None):
                kw = {}
                if bias is not None:
                    kw["bias"] = bias
                if scale is not None:
                    kw["scale"] = scale
                if accum_out is not None:
                    kw["accum_out"] = accum_out
                P.add("act", lambda h: h.activation(out, in_, func, **kw), reads, writes)

            def tt(eng, out, in0, in1, op, reads, writes):
                P.add(eng, lambda h: h.tensor_tensor(out, in0, in1, op), reads, writes)

            def ts(eng, out, in0, s1, s2, op0, op1, reads, writes):
                P.add(eng, lambda h: h.tensor_scalar(out, in0, s1, s2, op0, op1), reads, writes)

            def stt(out, in0, scalar, in1, op0, op1, reads, writes):
                P.add("dve", lambda h: h.scalar_tensor_tensor(out, in0, scalar, in1, op0, op1), reads, writes)

            def cp(eng, out, in_, reads, writes):
                if eng == "act":
                    P.add("act", lambda h: h.copy(out, in_), reads, writes)
                else:
                    P.add(eng, lambda h: h.tensor_copy(out, in_), reads, writes)

            def dma(q, stream, out, in_, reads, writes):
                P.add(q, lambda h: h.dma_start(out=out, in_=in_), reads, writes, stream=stream)

            wslot_ctr = [0]

            wassign = []
            wptr = [0]
            wowner = [-1, -1, -1, -1]
            wissued = [0]

            def assign_w(j, req):
                while len(wassign) <= j:
                    jj = len(wassign)
                    _, nk, ncols = (wlog[jj] if wplan is None else wplan[jj])
                    if nk * ncols <= 4096:
                        hs = [wptr[0] % 4]
                        wptr[0] += 1
                    else:
                        if wptr[0] % 2:
                            wptr[0] += 1
                        hs = [wptr[0] % 4, wptr[0] % 4 + 1]
                        wptr[0] += 2
                    wassign.append(hs)
                return wassign[j]

            def w_view(hs, nk, ncols):
                base = (hs[0] % 2) * 4096
                return Wr[hs[0] // 2][:, base:base + nk * ncols].rearrange("p (k n) -> p k n", n=ncols)

            def issue_w(j, req):
                src_ap, nk, ncols = req
                hs = assign_w(j, req)
                dma("pool", f"w{hs[0]}", w_view(hs, nk, ncols), src_ap, [], [T("wh", h_) for h_ in hs])
                for h_ in hs:
                    wowner[h_] = j

            def load_w(src_ap, nk, ncols):
                i = len(wlog)
                wlog.append((src_ap, nk, ncols))
                if wplan is None:
                    issue_w(i, wlog[i])
                    wissued[0] = i + 1
                else:
                    while wissued[0] < len(wplan) and wissued[0] <= i + 3:
                        j = wissued[0]
                        hs = assign_w(j, wplan[j])
                        if j > i and any(wowner[h_] >= i for h_ in hs):
                            break
                        issue_w(j, wplan[j])
                        wissued[0] += 1
                hs = assign_w(i, wlog[i])
                return w_view(hs, nk, ncols), [T("wh", h_) for h_ in hs]

            tc = T("const")
            tcp = T("constp")
            dma("pool", "constp", ident_b[:], ident_in, [], [tcp])
            dma("sp", "consts", ident_f[:], ident_in, [], [tc])
            dma("pool", "constp", masks_b[:].rearrange("p a b -> p (a b)"), masks, [], [tcp])
            dma("sp", "consts", pat_f[:], scanpat, [], [tc])
            dma("pool", "constp", wgk2_b[:], w_gk2, [], [tcp])
            dma("sp", "consts", negb[:], b_gk, [], [tc])
            dma("sp", "consts", normw_bc[:], normw.partition_broadcast(128), [], [tc])
            dma("sp", "consts", sinkexp[:], sinks, [], [tc])
            dma("sp", "consts", lnp_s[:], lnp, [], [tc])
            tc2 = T("const2")
            ts("dve", negb[:], negb[:], -1.0, None, ALU.mult, ALU.bypass, [tc, tcp], [tc2])
            P.add("dve", lambda h: h.memset(ones_b[:], 1.0), [tc2], [tc2])
            P.add("dve", lambda h: h.memset(S_f[:].rearrange("p a b -> p (a b)"), 0.0), [tc2], [T("S", hh) for hh in range(4)])
            P.add("dve", lambda h: h.memset(S_b[:].rearrange("p a b -> p (a b)"), 0.0), [tc2], [T("Sb", hh) for hh in range(4)])
            P.add("dve", lambda h: h.memset(ksT[:].rearrange("p a b -> p (a b)"), 0.0), [tc2], [T("ksT", g, "carry") for g in range(2)])
            P.add("dve", lambda h: h.memset(vs[:].rearrange("p a b -> p (a b)"), 0.0), [tc2], [T("vs", 0)])
            ts("dve", lnpa[:], lnp_s[:, 0:32], ALPHA, None, ALU.mult, ALU.bypass, [tc], [tc2])
            act(sinkexp[:], sinkexp[:], AF.Exp, [tc], [tc2])
            CONST = [tc, tcp, tc2]

            def issue_x_loads(xsrc, pi):
                for t in range(8):
                    rows = xsrc[pi * NT + t * 128: pi * NT + (t + 1) * 128, :]
                    alias = [T("mixT", 2 * t), T("mixT", 2 * t + 1)] + [T("x1T", 2 * t + a, hf) for a in range(2) for hf in range(2)]
                    dma("pool", f"xtok{t}", xtok[t].rearrange("p (a b) -> p a b", a=2), rows.rearrange("p (a b) -> p a b", a=2), [], alias)

            ZSLOT = [0, 1, 2, 3, 4, 5, 2, 3]

            def do_pass(xsrc, pi, full, first_own, nxt, after_full):
                if after_full:
                    P.barrier()
                tok0 = pi * NT
                P.label = f"{int(full)}{pi}:xT"
                def x_tile(t):
                    sl = t
                    for g8 in range(2):
                        bk, bt = next_bank()
                        bkb = bk[:, :].bitcast(BF16)
                        for jx in range(8):
                            dc = g8 * 8 + jx
                            tr(bkb[:, jx * 128:(jx + 1) * 128], xtok[sl][:, dc * 128:(dc + 1) * 128], ident_b[:],
                               [T("mixT", 2 * sl), T("mixT", 2 * sl + 1)] + CONST, [bt])
                        cp("act" if g8 == 0 else "dve", xT[:, g8 * 8:(g8 + 1) * 8, t * 128:(t + 1) * 128],
                           bkb.rearrange("p (k t) -> p k t", k=8), [bt], [T("xT", g8, t)])

                for t in range(4):
                    x_tile(t)

                def xT_reads(half):
                    return [T("xT", g8, t) for g8 in range(2) for t in range(half * 4, half * 4 + 4)]

                def proj_fm(wv, wt, c0, m, half, bk, bt):
                    for kc in range(16):
                        mm(bk[0:m, :], wv[:, kc, c0:c0 + m], xT[:, kc, half * 512:(half + 1) * 512],
                           kc == 0, kc == 15, wt + xT_reads(half), [bt])

                def proj_tm(wv, wt, c0, n, t, out_ap, bt):
                    for kc in range(16):
                        mm(out_ap, xT[:, kc, t * 128:(t + 1) * 128], wv[:, kc, c0:c0 + n],
                           kc == 0, kc == 15, wt + [T("xT", 0, t), T("xT", 1, t)], [bt])

                P.label = f"{int(full)}{pi}:misc"
                wv, wt = load_w(w_in_v[:, :, 0:400], 16, 400)

                def misc_half(half):
                    if full:
                        for g in range(2):
                            bk, bt = next_bank()
                            proj_fm(wv, wt, g * 128, 128, half, bk, bt)
                            cp("act", ksT[:, g, 128 + half * 512: 128 + (half + 1) * 512], bk[:, :], [bt], [T("ksT", g, half)])
                        bk, bt = next_bank()
                        for j in range(4):
                            t = half * 4 + j
                            proj_tm(wv, wt, 256, 128, t, bk[:, j * 128:(j + 1) * 128], bt)
                        cp("dve", vs[:, 1 + half * 4: 5 + half * 4, :], bk[:, :].rearrange("p (t e) -> p t e", t=4), [bt], [T("vs", 1 + half)])
                    elif pi == 1 and half == 1:
                        for g in range(2):
                            bk, bt = next_bank()
                            for kc in range(16):
                                mm(bk[:, 0:128], wv[:, kc, g * 128:(g + 1) * 128], xT[:, kc, 896:1024], kc == 0, kc == 15,
                                   wt + [T("xT", 0, 7), T("xT", 1, 7)], [bt])
                            cp("act", ksT[:, g, 1024:1152], bk[:, 0:128], [bt], [T("ksT", g, 1)])
                        bk, bt = next_bank()
                        proj_tm(wv, wt, 256, 128, 7, bk[:, 0:128], bt)
                        cp("dve", vs[:, 8, :], bk[:, 0:128], [bt], [T("vs", 2)])
                    bk, bt = next_bank()
                    proj_fm(wv, wt, 384, 16, half, bk, bt)
                    cp("act", gkloT[:, half * 512:(half + 1) * 512], bk[0:16, :], [bt], [T("gklo", half)])
                    for h in range(4):
                        bk, bt = next_bank()
                        mm(bk[:, :], wgk2_b[:, h * 128:(h + 1) * 128], gkloT[:, half * 512:(half + 1) * 512], True, True,
                           [T("gklo", half)] + CONST, [bt])
                        tf = (h * 2 + half) % 4
                        act(tmpf[tf], bk[:, :], AF.Exp, [bt] + CONST, [T("tmpf", tf)], bias=negb[:, h:h + 1], scale=-1.0)
                        act(tmpf[tf], tmpf[tf], AF.Ln, [T("tmpf", tf)], [T("tmpf", tf)], bias=1.0)
                        P.add("dve", lambda hh, o=c_f[:, h, half * 512:(half + 1) * 512], d0=pat_f[:, :], d1=tmpf[tf]:
                              hh.tensor_tensor_scan(o, d0, d1, 0.0, ALU.mult, ALU.add), [T("tmpf", tf)] + CONST, [T("c", h, half)])

                misc_half(0)
                P.label = f"{int(full)}{pi}:xT"
                for t in range(4, 8):
                    x_tile(t)
                P.label = f"{int(full)}{pi}:misc"
                misc_half(1)
                call = [T("c", h, half) for h in range(4) for half in range(2)]
                tcc = T("cC")
                ts("dve", negcC[:].rearrange("p h c -> p (h c)"),
                   c_f.rearrange("p h (c t) -> p (h c) t", t=128)[:, :, 127], -1.0 / 16.0, None, ALU.mult, ALU.bypass, call, [tcc])
                act(eC[:].rearrange("p h c -> p (h c)"), negcC[:].rearrange("p h c -> p (h c)"), AF.Exp, [tcc], [T("eC")])

                if not full and nxt is not None:
                    issue_x_loads(*nxt)
                P.label = f"{int(full)}{pi}:gla"
                def gla_proj(h):
                    hb = h % 2
                    base = 400 + h * 768

                    def kd_transposes(half):
                        bk2, bt2 = next_bank()
                        bkb = bk2[:, :].bitcast(BF16)
                        for j in range(4):
                            tr(bkb[:, j * 128:(j + 1) * 128], kdT[half][:, j * 128:(j + 1) * 128], ident_b[:],
                               [T("kdT", half)] + CONST, [bt2])
                        cp("act", kd_tok[hb][:, half * 4:(half + 1) * 4, :], bkb[:, 0:512].rearrange("p (t d) -> p t d", t=4),
                           [bt2], [T("kd_tok", hb, half)])

                    wv, wt = load_w(w_in_v[:, :, base:base + 384], 16, 384)

                    def k_step(half):
                        bk, bt = next_bank()
                        proj_fm(wv, wt, 0, 128, half, bk, bt)
                        if full:
                            tf = half
                            act(tmpf[tf], c_f[:, h, half * 512:(half + 1) * 512], AF.Exp, [T("c", h, half)], [T("tmpf", tf)], scale=1.0 / 16.0)
                            tt("dve", ktT[hb][:, half * 512:(half + 1) * 512], bk[:, :], tmpf[tf], ALU.mult,
                               [bt, T("tmpf", tf)], [T("ktT", hb, half)])
                        tf = 2 + half
                        for j in range(4):
                            cj = half * 4 + j
                            act(tmpf[tf][:, j * 128:(j + 1) * 128], c_f[:, h, cj * 128:(cj + 1) * 128], AF.Exp,
                                [T("c", h, half), tcc], [T("tmpf", tf)], bias=negcC[:, h, cj:cj + 1], scale=1.0 / 16.0)
                        tt("dve", kdT[half], bk[:, :], tmpf[tf], ALU.mult, [bt, T("tmpf", tf)], [T("kdT", half)])

                    def v_step(tq):
                        bk, bt = next_bank()
                        for j in range(2):
                            t = tq * 2 + j
                            proj_tm(wv, wt, 128, 256, t, bk[:, j * 256:(j + 1) * 256], bt)
                        cp("dve" if tq % 2 else "act", v_tok[hb][:, tq * 2:tq * 2 + 2, :], bk[:, :].rearrange("p (t e) -> p t e", t=2),
                           [bt], [T("v_tok", hb, tq)])

                    if h == 0:
                        for tq in range(4):
                            v_step(tq)
                            yield
                        k_step(0)
                        yield
                        k_step(1)
                        kd_transposes(0)
                        yield
                        kd_transposes(1)
                        yield
                    else:
                        k_step(0)
                        yield
                        k_step(1)
                        kd_transposes(0)
                        yield
                        for tq in range(4):
                            v_step(tq)
                            if tq == 0:
                                kd_transposes(1)
                            yield
                    if full:
                        wv2, wt2 = load_w(w_in_v[:, :, base + 384:base + 768], 16, 384)
                        for half in range(2):
                            bk, bt = next_bank()
                            proj_fm(wv2, wt2, 0, 128, half, bk, bt)
                            tf = half
                            act(tmpf[tf], c_f[:, h, half * 512:(half + 1) * 512], AF.Exp, [T("c", h, half)], [T("tmpf", tf)], scale=-1.0 / 16.0)
                            stt(qtT[hb][:, half * 512:(half + 1) * 512], bk[:, :], 128.0 ** -0.5, tmpf[tf], ALU.mult, ALU.mult,
                                [bt, T("tmpf", tf)], [T("qtT", hb, half)])
                            yield
                        for tq in range(4):
                            bk, bt = next_bank()
                            for j in range(2):
                                t = tq * 2 + j
                                proj_tm(wv2, wt2, 128, 256, t, bk[:, j * 256:(j + 1) * 256], bt)
                            tf = 2 + tq % 2
                            act(tmpf[tf], bk[:, :], AF.Silu, [bt], [T("tmpf", tf)])
                            tt("dve", gw[hb][:, tq * 2:tq * 2 + 2, :], tmpf[tf].rearrange("p (t e) -> p t e", t=2),
                               normw_bc[:].unsqueeze(1).to_broadcast([128, 2, 256]), ALU.mult,
                               [T("tmpf", tf)] + CONST, [T("gw", hb, tq)])
                            yield

                def gla_chunks(h):
                    hb = h % 2
                    if full:
                        for half in range(2):
                            bk, bt = next_bank()
                            for j in range(4):
                                t = half * 4 + j
                                mm(bk[:, j * 128:(j + 1) * 128], ktT[hb][:, t * 128:(t + 1) * 128], qtT[hb][:, t * 128:(t + 1) * 128],
                                   True, True, [T("ktT", hb, half), T("qtT", hb, half)], [bt])
                            tt("dve", At4[half].rearrange("p (c t) -> p c t", c=4), bk[:, :].rearrange("p (c t) -> p c t", c=4),
                               masks_b[:, 0, :].unsqueeze(1).to_broadcast([128, 4, 128]), ALU.mult, [bt] + CONST, [T("At", half)])
                        yield
                    for t in range(8):
                        half = t // 4
                        tq = t // 2
                        if full:
                            a = t
                            bo, bot = next_bank()
                            mm(bo[:, 0:256], At[a], v_tok[hb][:, t, :], True, False, [T("At", half), T("v_tok", hb, tq)], [bot])
                            mm(bo[:, 0:256], qtT[hb][:, t * 128:(t + 1) * 128], S_b[:, h, :], False, True,
                               [T("qtT", hb, half), T("Sb", h)], [bot])
                            ss, sst = next_small()
                            act(junk[:], bo[:, 0:256], AF.Square, [bot], [T("junk"), sst], accum_out=ss)
                            ln_, lnt = next_small()
                            act(ln_, ss, AF.Ln, [sst], [lnt], bias=RMS_EPS, scale=1.0 / 256.0)
                            rs, rst = next_small()
                            act(rs, ln_, AF.Exp, [lnt], [rst], scale=-0.5)
                            stt(gla_h[hb][:, t, :], bo[:, 0:256], rs, gw[hb][:, t, :], ALU.mult, ALU.mult,
                                [bot, rst, T("gw", hb, tq)], [T("gla_h", hb, t)])
                        bu, but = next_bank()
                        mm(bu[:, 0:256], kd_tok[hb][:, t, :], v_tok[hb][:, t, :], True, True,
                           [T("kd_tok", hb, half), T("v_tok", hb, tq)], [but])
                        stt(S_f[:, h, :], S_f[:, h, :], eC[:, h, t:t + 1], bu[:, 0:256], ALU.mult, ALU.add,
                            [T("S", h), T("eC"), but], [T("S", h)])
                        cp("act", S_b[:, h, :], S_f[:, h, :], [T("S", h)], [T("Sb", h)])
                        yield
                    if full:
                        for ec in range(2):
                            bk, bt = next_bank()
                            bkb = bk[:, :].bitcast(BF16)
                            for t in range(8):
                                tr(bkb[:, t * 128:(t + 1) * 128], gla_h[hb][:, t, ec * 128:(ec + 1) * 128], ident_b[:],
                                   [T("gla_h", hb, t)] + CONST, [bt])
                            cp("act" if ec else "dve", mixT[:, 2 * h + ec, :], bkb, [bt], [T("mixT", 2 * h + ec)])
                        yield

                def swa_proj(g, alias_tmpf):
                    gb = g % 2
                    base = 400 + 4 * 768 + g * 512
                    wv, wt = load_w(w_in_v[:, :, base:base + 512], 16, 512)
                    for c in range(4):
                        for half in range(2):
                            bk, bt = next_bank()
                            proj_fm(wv, wt, c * 128, 128, half, bk, bt)
                            extra = [T("tmpf", c)] if alias_tmpf else []
                            cp("act" if half else "dve", qsT[gb][:, c, half * 512:(half + 1) * 512], bk[:, :], [bt],
                               [T("qsT", gb, c, half)] + extra)
                            yield

                def swa_blocks(g):
                    gb = g % 2
                    pts_all = {}

                    def st1(b):
                        half = b // 4
                        pts = {}
                        for p in range(2):
                            for kb in range(2):
                                bk, bt = next_bank()
                                kcol = (b + kb) * 128
                                kread = [T("ksT", g, "carry")] if kcol < 128 else [T("ksT", g, (kcol - 128) // 512)]
                                mm(bk[:, :], ksT[p * 64:(p + 1) * 64, g, kcol:kcol + 128],
                                   qsT[gb][p * 64:(p + 1) * 64, :, b * 128:(b + 1) * 128], True, True,
                                   kread + [T("qsT", gb, c, half) for c in range(4)], [bt])
                                mi = 1 if kb == 1 else (3 if (first_own and b == 0) else 2)
                                pi_ = (b % 2) * 4 + p * 2 + kb
                                act(PT[pi_], bk[:, :], AF.Exp, [bt], [T("PT", pi_)], scale=0.125)
                                tt("pool" if kb == 0 else "dve", PT[pi_].rearrange("p (c t) -> p c t", c=4), PT[pi_].rearrange("p (c t) -> p c t", c=4),
                                   masks_b[:, mi, :].unsqueeze(1).to_broadcast([128, 4, 128]), ALU.mult,
                                   [T("PT", pi_)] + CONST, [T("PT", pi_)])
                                pts[(p, kb)] = pi_
                        pts_all[b] = pts

                    def st2(b):
                        pts = pts_all[b]
                        bn_, bnt = next_bank()
                        bd_, bdt = next_bank()
                        for p in range(2):
                            for kb in range(2):
                                vblk = b + kb
                                vread = [T("vs", 0)] if vblk == 0 else [T("vs", 1 + (vblk - 1) // 4)]
                                mm(bn_[p * 64:(p + 1) * 64, :], vs[:, vblk, g * 64:(g + 1) * 64], PT[pts[(p, kb)]], kb == 0, kb == 1,
                                   vread + [T("PT", pts[(p, kb)])], [bnt])
                        for p in range(2):
                            for kb in range(2):
                                mm(bd_[p * 64:(p + 1) * 64, :], ones_b[:, :], PT[pts[(p, kb)]], kb == 0, kb == 1,
                                   [T("PT", pts[(p, kb)])] + CONST, [bdt])
                        sw = b % 2
                        tt("dve", swtmp[sw].rearrange("p (c t) -> p c t", c=4), bd_[:, :].rearrange("p (c t) -> p c t", c=4),
                           sinkexp[:, g * 4:(g + 1) * 4].unsqueeze(2).to_broadcast([128, 4, 128]), ALU.add,
                           [bdt] + CONST, [T("swtmp", sw)])
                        act(swtmp[sw], swtmp[sw], AF.Ln, [T("swtmp", sw)], [T("swtmp", sw)])
                        act(swtmp[sw], swtmp[sw], AF.Exp, [T("swtmp", sw)], [T("swtmp", sw)], scale=-1.0)
                        tt("dve", mixT[:, 8 + 4 * g: 12 + 4 * g, b * 128:(b + 1) * 128],
                           bn_[:, :].rearrange("p (c t) -> p c t", c=4), swtmp[sw].rearrange("p (c t) -> p c t", c=4), ALU.mult,
                           [bnt, T("swtmp", sw)], [T("mixT", 8 + 4 * g + c) for c in range(4)])

                    st1(0)
                    yield
                    for b in range(8):
                        if b + 1 < 8:
                            st1(b + 1)
                            yield
                        st2(b)
                        yield

                def load_xres(t, alias_xT=False):
                    sl = ZSLOT[t]
                    row0 = tok0 + t * 128
                    extra = [T("xT", sl // 2, tt_) for tt_ in range(8)] if alias_xT else []
                    dma("sp", f"z{sl}", zt[sl], x_own[row0:row0 + 128, :], [], [T("zt", sl, q) for q in range(4)] + extra)

                def run_interleaved(a, b):
                    alive = [g_ for g_ in (a, b) if g_ is not None]
                    while alive:
                        for g_ in list(alive):
                            try:
                                next(g_)
                            except StopIteration:
                                alive.remove(g_)

                prev_chunks = None
                for h in range(4):
                    run_interleaved(prev_chunks, gla_proj(h))
                    prev_chunks = gla_chunks(h)
                if full:
                    run_interleaved(prev_chunks, swa_proj(0, True))
                    P.label = f"{int(full)}{pi}:swa"
                    P.barrier()
                    run_interleaved(swa_blocks(0), swa_proj(1, False))
                    for t in range(4):
                        load_xres(t, alias_xT=True)
                    run_interleaved(swa_blocks(1), None)
                else:
                    run_interleaved(prev_chunks, None)
                if full or pi == 1:
                    for g in range(2):
                        cp("dve", ksT[:, g, 0:128], ksT[:, g, 1024:1152], [T("ksT", g, 1)], [T("ksT", g, "carry")])
                    cp("dve", vs[:, 0, :], vs[:, 8, :], [T("vs", 2)], [T("vs", 0)])
                if not full:
                    return

                P.label = f"{int(full)}{pi}:outproj"
                def hdn_alias(hb, fc):
                    return [T("zt", hb * 2 + fc // 4, q) for q in range(4)]

                def mlp_up_gen(s, halves):
                    hb = s % 2
                    for u in range(4):
                        c0 = s * 1024 + u * 256
                        wv, wt = load_w(w_up_v[:, :, c0:c0 + 256], 16, 256)
                        for fcl in range(2):
                            fc = u * 2 + fcl
                            for half in halves:
                                bk, bt = next_bank()
                                for kc in range(16):
                                    mm(bk[:, :], wv[:, kc, fcl * 128:(fcl + 1) * 128], mixT[:, kc, half * 512:(half + 1) * 512],
                                       kc == 0, kc == 15, wt + [T("x1T", kc, half)], [bt])
                                tf = half
                                act(tmpr[tf], bk[:, :], AF.Relu, [bt], [T("tmpr", tf)])
                                tt("dve", hdn[hb][:, fc, half * 512:(half + 1) * 512], tmpr[tf], tmpr[tf], ALU.mult,
                                   [T("tmpr", tf)], [T("hdn", hb, fc, half)] + hdn_alias(hb, fc))
                        yield

                def mlp_up(s):
                    run_interleaved(mlp_up_gen(s, (0, 1)), None)

                P.barrier()

                def op_mm_stage(half, preloaded):
                    tiles = [half * 4 + i for i in range(4)]
                    for q in range(4):
                        banks = [next_bank() for _ in tiles]
                        bidx = [bt_.name[1] for _, bt_ in banks]
                        held_banks.update(bidx)
                        for c0 in (0, 256):
                            wv, wt = load_w(w_out_v[:, :, q * 512 + c0:q * 512 + c0 + 256], 16, 256)
                            for ti, t in enumerate(tiles):
                                if q == 0 and c0 == 0 and t not in preloaded:
                                    load_xres(t)
                                bk, bt = banks[ti]
                                for kc in range(16):
                                    mm(bk[:, c0:c0 + 256], mixT[:, kc, t * 128:(t + 1) * 128], wv[:, kc, :], kc == 0, kc == 15,
                                       wt + [T("mixT", kc)], [bt])
                                if c0 == 256:
                                    sl = ZSLOT[t]
                                    zq = zt[sl][:, q * 512:(q + 1) * 512]
                                    stt(zq, zq, ALPHA, bk[:, :], ALU.mult, ALU.add, [T("zt", sl, q), bt], [T("zt", sl, q)])
                                    P.add("dve", lambda hh, o=bnst[:, sl, q, :], i=zq: hh.bn_stats(o, i), [T("zt", sl, q)], [T("bnst", sl)])
                                    held_banks.discard(bidx[ti])
                                yield

                def ln_A(t):
                    sl = ZSLOT[t]
                    mv, mvt = next_small(2)
                    P.add("dve", lambda hh, o=mv, i=bnst[:, sl].rearrange("p a b -> p (a b)"): hh.bn_aggr(o, i), [T("bnst", sl)], [mvt])
                    ln_, lnt = next_small()
                    act(ln_, mv[:, 1:2], AF.Ln, [mvt], [lnt], bias=LN_EPS)
                    rs, rst = next_small()
                    act(rs, ln_, AF.Exp, [lnt], [rst], scale=-0.5)
                    nm, nmt = next_small()
                    stt(nm, mv[:, 0:1], -1.0, rs, ALU.mult, ALU.mult, [mvt, rst], [nmt])
                    ztl = [T("zt", sl, q) for q in range(4)]
                    act(zt[sl], zt[sl], AF.Identity, ztl + [rst, nmt], ztl, bias=nm, scale=rs)

                def ln_B(t):
                    sl = ZSLOT[t]
                    half = t // 4
                    for q in range(4):
                        bk, bt = next_bank()
                        for j in range(4):
                            dc = q * 4 + j
                            tr(bk[:, j * 128:(j + 1) * 128], zt[sl][:, dc * 128:(dc + 1) * 128], ident_f[:], [T("zt", sl, q)] + CONST, [bt])
                        for j in range(4):
                            dc = q * 4 + j
                            if q % 2 == 0:
                                act(acc[:, dc, t * 128:(t + 1) * 128], bk[:, j * 128:(j + 1) * 128], AF.Identity, [bt] + CONST,
                                    [T("acc", dc, half)], bias=lnpa[:, 16 + dc:17 + dc], scale=lnpa[:, dc:dc + 1])
                            else:
                                ts("dve", acc[:, dc, t * 128:(t + 1) * 128], bk[:, j * 128:(j + 1) * 128], lnpa[:, dc:dc + 1],
                                   lnpa[:, 16 + dc:17 + dc], ALU.mult, ALU.add, [bt] + CONST, [T("acc", dc, half)])

                def ln_stage(half):
                    for t in [half * 4 + i for i in range(4)]:
                        ln_A(t)
                        ln_B(t)
                        yield

                def ln_B_stage(half):
                    for t in [half * 4 + i for i in range(4)]:
                        ln_B(t)
                        yield

                def x1t_conv(half):
                    for dc in range(16):
                        if dc % 2:
                            act(mixT[:, dc, half * 512:(half + 1) * 512], acc[:, dc, half * 512:(half + 1) * 512], AF.Copy,
                                [T("acc", dc, half)], [T("x1T", dc, half), T("mixT", dc)], scale=1.0 / ALPHA)
                        else:
                            ts("dve", mixT[:, dc, half * 512:(half + 1) * 512], acc[:, dc, half * 512:(half + 1) * 512], 1.0 / ALPHA, None,
                               ALU.mult, ALU.bypass, [T("acc", dc, half)], [T("x1T", dc, half), T("mixT", dc)])
                        if dc % 4 == 3:
                            yield

                run_interleaved(op_mm_stage(0, [0, 1, 2, 3]), None)
                load_xres(4)
                load_xres(5)
                for t in range(4):
                    ln_A(t)
                run_interleaved(ln_B_stage(0), op_mm_stage(1, [4, 5]))
                def chain(*gens):
                    for g_ in gens:
                        yield from g_

                P.label = f"{int(full)}{pi}:mlp"
                ln_A(4)
                run_interleaved(x1t_conv(0), None)
                for t in (5, 6, 7):
                    ln_A(t)
                run_interleaved(ln_B_stage(1), mlp_up_gen(0, (0,)))
                run_interleaved(x1t_conv(1), None)
                run_interleaved(mlp_up_gen(0, (1,)), None)

                P.label = f"{int(full)}{pi}:mlp"
                def mlp_down(s):
                    hb = s % 2
                    for q in range(4):
                        wv, wt = load_w(w_down_v[:, s * 8:(s + 1) * 8, q * 512:(q + 1) * 512], 8, 512)
                        for dcl in range(4):
                            dc = q * 4 + dcl
                            for half in range(2):
                                bk, bt = next_bank()
                                for fc in range(8):
                                    mm(bk[:, :], wv[:, fc, dcl * 128:(dcl + 1) * 128], hdn[hb][:, fc, half * 512:(half + 1) * 512],
                                       fc == 0, fc == 7, wt + [T("hdn", hb, fc, half)], [bt])
                                tt("dve", acc[:, dc, half * 512:(half + 1) * 512], acc[:, dc, half * 512:(half + 1) * 512], bk[:, :], ALU.add,
                                   [bt, T("acc", dc, half)], [T("acc", dc, half)])

                for s in range(8):
                    if s + 1 < 8:
                        mlp_up(s + 1)
                    elif nxt is not None:
                        issue_x_loads(*nxt)
                    mlp_down(s)

                P.label = f"{int(full)}{pi}:epi"
                P.barrier()
                dma("sp", "bc2", g2bc, ln2row[0:1, :].partition_broadcast(128), [], [T("g2bc")])
                dma("sp", "bc2", b2bc, ln2row[1:2, :].partition_broadcast(128), [], [T("g2bc")])
                for t in range(8):
                    half = t // 4
                    zb = t % 2
                    z2 = ztmp2[zb]
                    for q in range(4):
                        bk, bt = next_bank()
                        for j in range(4):
                            dc = q * 4 + j
                            tr(bk[:, j * 128:(j + 1) * 128], acc[:, dc, t * 128:(t + 1) * 128], ident_f[:], [T("acc", dc, half)] + CONST, [bt])
                        cp("act", z2[:, q * 512:(q + 1) * 512], bk[:, :], [bt], [T("z2", zb, q)])
                        P.add("dve", lambda hh, o=bnst[:, zb, q, :], i=z2[:, q * 512:(q + 1) * 512]: hh.bn_stats(o, i),
                              [T("z2", zb, q)], [T("bnst", zb)])
                    mv, mvt = next_small(2)
                    P.add("dve", lambda hh, o=mv, i=bnst[:, zb].rearrange("p a b -> p (a b)"): hh.bn_aggr(o, i), [T("bnst", zb)], [mvt])
                    ln_, lnt = next_small()
                    act(ln_, mv[:, 1:2], AF.Ln, [mvt], [lnt], bias=LN_EPS)
                    rs, rst = next_small()
                    act(rs, ln_, AF.Exp, [lnt], [rst], scale=-0.5)
                    nm, nmt = next_small()
                    stt(nm, mv[:, 0:1], -1.0, rs, ALU.mult, ALU.mult, [mvt, rst], [nmt])
                    z2l = [T("z2", zb, q) for q in range(4)]
                    act(z2, z2, AF.Identity, z2l + [rst, nmt], z2l, bias=nm, scale=rs)
                    os_ = t % 2
                    tt("dve", z2, z2, g2bc, ALU.mult, z2l + [T("g2bc")], z2l)
                    tt("pool", ostage[os_], z2, b2bc, ALU.add, z2l + [T("g2bc")], [T("ost", os_)])
                    row0 = tok0 + t * 128
                    dma("sp", f"ost{os_}", y[row0:row0 + 128, :], ostage[os_], [T("ost", os_)], [T("ystore", os_)])

            issue_x_loads(x_prev, 0)
            do_pass(x_prev, 0, False, False, (x_prev, 1), False)
            do_pass(x_prev, 1, False, False, (x_own, 0), False)
            do_pass(x_own, 0, True, True, (x_own, 1), False)
            do_pass(x_own, 1, True, False, None, True)
            P.add("sp", lambda h: h.nop(), [T("ystore", 0), T("ystore", 1)], [])
            return wlog, xlog

        plan, xpl = record(Prog(), None, None)
        P = Prog()
        record(P, plan, xpl)
        nc._pe_labels = [op.label for op in P.ops["pe"] if op.fn is not None]
        P.emit(nc, block, sems_eng, sems_stream)
    return nc


def host_layout(x, w_in, w_gk2, b_gk, gla_norm_w, swa_sinks, w_out, ln1_g, ln1_b, w_up, w_down, ln2_g, ln2_b):
    f = np.float32
    w = np.asarray(w_in[0], f)
    qg, kg, vg, gg = w[:, 0:512], w[:, 512:1024], w[:, 1024:2048], w[:, 2048:3072]
    gk = w[:, 3072:3088]
    qs, ks, vsw = w[:, 3088:4112], w[:, 4112:4240], w[:, 4240:4368]
    cols = [ks[:, 0:64], ks[:, 0:64], ks[:, 64:128], ks[:, 64:128], vsw, gk]
    for h in range(4):
        cols += [kg[:, h * 128:(h + 1) * 128], vg[:, h * 256:(h + 1) * 256], qg[:, h * 128:(h + 1) * 128], gg[:, h * 256:(h + 1) * 256]]
    cols.append(qs)
    w_in_r = np.ascontiguousarray(np.concatenate(cols, axis=1))
    assert w_in_r.shape == (D, WIN_COLS)
    sinks = np.zeros((128, 8), f)
    sk = np.asarray(swa_sinks[0], f)
    for g in range(2):
        for c in range(4):
            for p in range(2):
                sinks[p * 64:(p + 1) * 64, g * 4 + c] = sk[8 * g + 2 * c + p]
    lnp = np.concatenate([np.asarray(a[0], f).reshape(16, 128).T for a in (ln1_g, ln1_b, ln2_g, ln2_b)], axis=1)
    ln2row = np.stack([np.asarray(ln2_g[0], f), np.asarray(ln2_b[0], f)])
    j = np.arange(128)[:, None]
    i = np.arange(128)[None, :]
    causal = (j <= i).astype(f)
    cur = (j <= i).astype(f)
    prev = (j > i).astype(f)
    scanpat = np.ones((128, 512), f)
    scanpat[:, ::128] = 0.0
    common = {
        "w_in": w_in_r,
        "w_gk2": np.ascontiguousarray(np.asarray(w_gk2[0], f)),
        "b_gk": np.ascontiguousarray(np.asarray(b_gk[0], f).reshape(4, 128).T),
        "normw": np.ascontiguousarray(np.asarray(gla_norm_w[0], f)[None, :]),
        "sinks": sinks,
        "w_out": np.ascontiguousarray(np.asarray(w_out[0], f)),
        "lnp": np.ascontiguousarray(lnp),
        "ln2row": np.ascontiguousarray(ln2row),
        "w_up": np.ascontiguousarray(np.asarray(w_up[0], f)),
        "w_down": np.ascontiguousarray(np.asarray(w_down[0], f)),
        "scanpat": scanpat,
        "ident": np.eye(128, dtype=f),
    }
    xs = np.asarray(x, f)
    in_maps = []
    for c in range(8):
        b, half = c // 2, c % 2
        m = dict(common)
        m["x_own"] = np.ascontiguousarray(xs[b, half * NTOK:(half + 1) * NTOK])
        m["x_prev"] = np.ascontiguousarray(xs[b, 0:NTOK]) if half == 1 else np.zeros((NTOK, D), f)
        prev0 = prev if half == 1 else np.zeros((128, 128), f)
        m["masks"] = np.ascontiguousarray(np.concatenate([causal, cur, prev, prev0], axis=1))
        in_maps.append(m)
    return in_maps


_NC_CACHE = {}


def kernel(x, w_in, w_gk2, b_gk, gla_norm_w, swa_sinks, w_out, ln1_g, ln1_b, w_up, w_down, ln2_g, ln2_b):
    in_maps = host_layout(x, w_in, w_gk2, b_gk, gla_norm_w, swa_sinks, w_out, ln1_g, ln1_b, w_up, w_down, ln2_g, ln2_b)
    if "nc" not in _NC_CACHE:
        _NC_CACHE["nc"] = build_program()
    res = run_bass_kernel_spmd(_NC_CACHE["nc"], in_maps, core_ids=list(range(8)))
    out = np.empty((4, 4096, D), np.float32)
    for c in range(8):
        b, half = c // 2, c % 2
        out[b, half * NTOK:(half + 1) * NTOK] = res.results[c]["y"]
    return out
```
